# Optimizing a Trainium2 kernel written in Bass

```python
import math
import jax
import jax.numpy as jnp
from jax import lax
import numpy as np

D_MODEL = 1024
BATCH = 8
SEQ = 2048
DEPTH = 2
DEC_BATCH = 128
DEC_SEQ = 4
PAST_LEN = 16384
PAGE_SIZE = 128

EPS = 1e-6
A_HEADS = D_MODEL // 256
A_DK = 128
A_DV = 128
A_QK_W = A_HEADS * A_DK
A_V_W = A_HEADS * A_DV
A_CONV = 4
A_CONV_CH = 2 * A_QK_W + A_V_W
A_CHUNK = 64
A_IN = A_CONV_CH + A_V_W + 2 * A_HEADS
B_N = 64
B_HEADS = D_MODEL // 128
B_W = B_HEADS * B_N
B_DECAY_LORA = 64
B_AAA_LORA = 64
B_GATE_LORA = 128
B_LN_EPS = 64e-5
B_IN = 3 * B_W + B_DECAY_LORA + B_AAA_LORA + B_GATE_LORA
C_WIDTH = D_MODEL // 2
C_BLOCKS = 8
C_BLOCK = C_WIDTH // C_BLOCKS
C_CONV = 4
C_POW = 8.0
C_IN = 2 * C_WIDTH
N_BRANCH = 3
BRANCH_W = A_V_W
G_IN = N_BRANCH * D_MODEL
N_IN = A_IN + B_IN + C_IN + G_IN
D_FF = 4 * D_MODEL

kernel_name = 'hybrid_gdn_rwkv7_rglru_decode_step'


def split_cols(z, widths):
    out, off = [], 0
    for w in widths:
        out.append(z[..., off:off + w])
        off += w
    return out


def rmsnorm(x, g):
    xf = x.astype(jnp.float32)
    y = xf * lax.rsqrt(jnp.mean(xf * xf, axis=-1, keepdims=True) + EPS)
    return (y * g.astype(jnp.float32)).astype(x.dtype)


def l2norm(x):
    return x * lax.rsqrt(jnp.sum(x * x, axis=-1, keepdims=True) + EPS)


def causal_dwconv(x, buf, w):
    T = x.shape[1]
    width = w.shape[0]
    xp = jnp.concatenate([buf.astype(x.dtype), x], axis=1)
    y = xp[:, 0:T] * w[0]
    for j in range(1, width):
        y = y + xp[:, j:j + T] * w[j]
    return y, xp[:, T:]


def gated_delta_chunked(q, k, v, g, beta, S0):
    Bsz, T, H, DK = q.shape
    DV = v.shape[-1]
    C = A_CHUNK
    n = -(-T // C)
    pad = n * C - T

    def padt(a):
        return jnp.pad(a, [(0, 0), (0, pad)] + [(0, 0)] * (a.ndim - 2))

    def to_chunks(a):
        a = a.reshape((Bsz, n, C) + a.shape[2:])
        return a.transpose((1, 0, 3, 2) + tuple(range(4, a.ndim)))

    q, k, v, g, beta = [to_chunks(padt(t)) for t in (q, k, v, g, beta)]
    gc = jnp.cumsum(g, axis=-1)
    idx = jnp.arange(C)
    causal = idx[:, None] >= idx[None, :]
    strict = idx[:, None] > idx[None, :]
    decay = jnp.exp(jnp.where(causal, gc[..., :, None] - gc[..., None, :], -jnp.inf))
    kb = k * beta[..., None]
    L = jnp.where(strict, jnp.einsum('nbhid,nbhjd->nbhij', kb, k) * decay, 0.0)
    A = jnp.eye(C, dtype=jnp.float32) + L
    rhs = jnp.concatenate([v * beta[..., None], kb * jnp.exp(gc)[..., None]], axis=-1)
    sol = lax.linalg.triangular_solve(A, rhs, left_side=True, lower=True, unit_diagonal=True)
    u, w = sol[..., :DV], sol[..., DV:]
    attn = jnp.einsum('nbhid,nbhjd->nbhij', q, k) * decay
    qg = q * jnp.exp(gc)[..., None]
    kdec = k * jnp.exp(gc[..., -1:] - gc)[..., None]
    glast = jnp.exp(gc[..., -1])

    def step(S, xs):
        u_c, w_c, attn_c, qg_c, kdec_c, gl_c = xs
        vnew = u_c - jnp.einsum('bhcd,bhde->bhce', w_c, S)
        o = jnp.einsum('bhcd,bhde->bhce', qg_c, S) + jnp.einsum('bhij,bhje->bhie', attn_c, vnew)
        S = S * gl_c[..., None, None] + jnp.einsum('bhcd,bhce->bhde', kdec_c, vnew)
        return S, o

    S, o = lax.scan(step, S0, (u, w, attn, qg, kdec, glast))
    o = o.transpose(1, 0, 3, 2, 4).reshape(Bsz, n * C, H, DV)[:, :T]
    return o, S


def gdn_branch(za, conv_buf, S0, p):
    Bsz, T, _ = za.shape
    f32 = jnp.float32
    qkv, z, b_raw, a_raw = split_cols(za, (A_CONV_CH, A_V_W, A_HEADS, A_HEADS))
    qkv, new_conv = causal_dwconv(qkv, conv_buf, p['a_conv_w'])
    qkv = jax.nn.silu(qkv.astype(f32))
    q, k, v = split_cols(qkv, (A_QK_W, A_QK_W, A_V_W))
    q = l2norm(q.reshape(Bsz, T, A_HEADS, A_DK)) * (A_DK ** -0.5)
    k = l2norm(k.reshape(Bsz, T, A_HEADS, A_DK))
    v = v.reshape(Bsz, T, A_HEADS, A_DV)
    beta = jax.nn.sigmoid(b_raw.astype(f32))
    g = -jnp.exp(p['a_A_log'].astype(f32)) * jax.nn.softplus(a_raw.astype(f32) + p['a_dt_bias'])
    o, S = gated_delta_chunked(q, k, v, g, beta, S0.astype(f32))
    o = o * lax.rsqrt(jnp.mean(o * o, axis=-1, keepdims=True) + EPS)
    o = o * p['a_norm_g'] * jax.nn.silu(z.astype(f32).reshape(Bsz, T, A_HEADS, A_DV))
    return o.reshape(Bsz, T, A_V_W), S, new_conv


def rwkv7_scan(r, w, k, v, kk, a, S0):
    def step(S, xs):
        r_t, w_t, k_t, v_t, kk_t, a_t = xs
        sa = jnp.einsum('bhvk,bhk->bhv', S, -kk_t)
        S = (S * w_t[:, :, None, :] + sa[..., None] * (kk_t * a_t)[:, :, None, :]
             + v_t[..., None] * k_t[:, :, None, :])
        o = jnp.einsum('bhvk,bhk->bhv', S, r_t)
        return S, o

    xs = tuple(jnp.moveaxis(t, 1, 0) for t in (r, w, k, v, kk, a))
    S, o = lax.scan(step, S0, xs)
    return jnp.moveaxis(o, 0, 1), S


def rwkv7_branch(zb, shift_buf, S0, p):
    Bsz, T, _ = zb.shape
    f32 = jnp.float32
    zb = zb.astype(f32)
    prev = jnp.concatenate([shift_buf.astype(f32), zb[:, :-1]], axis=1)
    zs = zb + (prev - zb) * p['b_mu']
    r, k, v, xw, xa, xg = split_cols(zs, (B_W, B_W, B_W, B_DECAY_LORA, B_AAA_LORA, B_GATE_LORA))
    w_log = -jax.nn.softplus(-(p['b_w0'] + jnp.tanh(xw) @ p['b_w_up'])) - 0.5
    decay = jnp.exp(-jnp.exp(w_log))
    a = jax.nn.sigmoid(p['b_a0'] + xa @ p['b_a_up'])
    g = jax.nn.sigmoid(xg) @ p['b_g_up']
    kk = k * p['b_k_k']
    k = k * (1.0 + (a - 1.0) * p['b_k_a'])
    r, k, v, kk, a, decay = [t.reshape(Bsz, T, B_HEADS, B_N) for t in (r, k, v, kk, a, decay)]
    kk = l2norm(kk)
    o, S = rwkv7_scan(r, decay, k, v, kk, a, S0.astype(f32))
    mu = jnp.mean(o, axis=-1, keepdims=True)
    var = jnp.mean(jnp.square(o - mu), axis=-1, keepdims=True)
    o = ((o - mu) * lax.rsqrt(var + B_LN_EPS)).reshape(Bsz, T, B_W) * p['b_ln_w'] + p['b_ln_b']
    bonus = jnp.sum(r * k * p['b_r_k'], axis=-1, keepdims=True) * v
    o = o + bonus.reshape(Bsz, T, B_W)
    return o * g, S, zb[:, -1:]


def diag_linear_scan(a, b, h0):
    b = b.at[:, 0].add(a[:, 0] * h0)

    def combine(lft, rgt):
        al, bl = lft
        ar, br = rgt
        return al * ar, ar * bl + br

    _, h = lax.associative_scan(combine, (a, b), axis=1)
    return h


def rglru_branch(zc, conv_buf, h0, p):
    Bsz, T, _ = zc.shape
    f32 = jnp.float32
    xb, gb = split_cols(zc, (C_WIDTH, C_WIDTH))
    xc, new_conv = causal_dwconv(xb, conv_buf, p['c_conv_w'])
    xc = (xc + p['c_conv_b']).astype(f32)
    xblk = xc.reshape(Bsz, T, C_BLOCKS, C_BLOCK)
    r = jax.nn.sigmoid(jnp.einsum('btgi,gij->btgj', xblk, p['c_wa']).reshape(Bsz, T, C_WIDTH) + p['c_ba'])
    i = jax.nn.sigmoid(jnp.einsum('btgi,gij->btgj', xblk, p['c_wx']).reshape(Bsz, T, C_WIDTH) + p['c_bx'])
    log_a = -C_POW * r * jax.nn.softplus(-p['c_L'].astype(f32))
    a = jnp.exp(log_a)
    b = jnp.sqrt(-jnp.expm1(2.0 * log_a)) * (i * xc)
    h = diag_linear_scan(a, b, h0.astype(f32))
    y = h * jax.nn.gelu(gb.astype(f32))
    return y, h[:, -1], new_conv


def hybrid_layer(x, a_S, a_conv, b_S, b_shift, c_h, c_conv, p):
    Bsz, T, _ = x.shape
    h = rmsnorm(x, p['norm1_g'])
    z = h @ p['w_in']
    za, zb, zc, zg = split_cols(z, (A_IN, B_IN, C_IN, G_IN))
    ya, a_S, a_conv = gdn_branch(za, a_conv, a_S, p)
    yb, b_S, b_shift = rwkv7_branch(zb, b_shift, b_S, p)
    yc, c_h, c_conv = rglru_branch(zc, c_conv, c_h, p)
    br = jnp.stack([ya, yb, yc], axis=2).astype(x.dtype)
    proj = jnp.einsum('btnc,ncd->btnd', br, p['w_branch'])
    gates = jax.nn.sigmoid(zg.reshape(Bsz, T, N_BRANCH, D_MODEL))
    x = x + (jnp.sum(gates * proj, axis=2) @ p['w_out']).astype(x.dtype)
    h2 = rmsnorm(x, p['norm2_g'])
    x = x + (jnp.square(jax.nn.relu(h2 @ p['w_up'])) @ p['w_down']).astype(x.dtype)
    return x, (a_S, a_conv, b_S, b_shift, c_h, c_conv)


def setup_inputs(seed: int = 0) -> dict:
    key = jax.random.key(seed)
    ks = iter(jax.random.split(key, 48))
    nrm = lambda shape, s: jax.random.normal(next(ks), shape, jnp.float32) * s
    uni = lambda shape, lo, hi: jax.random.uniform(next(ks), shape, jnp.float32, lo, hi)
    x_prompt = nrm((BATCH, SEQ, D_MODEL), 1.0)
    x_sample = nrm((DEC_BATCH, DEC_SEQ, D_MODEL), 1.0)
    state_a_S = nrm((DEPTH, DEC_BATCH, A_HEADS, A_DK, A_DV), 0.3)
    state_a_conv = nrm((DEPTH, DEC_BATCH, A_CONV - 1, A_CONV_CH), 1.0)
    state_b_S = nrm((DEPTH, DEC_BATCH, B_HEADS, B_N, B_N), 0.3)
    state_b_shift = nrm((DEPTH, DEC_BATCH, 1, B_IN), 1.0)
    state_c_h = nrm((DEPTH, DEC_BATCH, C_WIDTH), 0.5)
    state_c_conv = nrm((DEPTH, DEC_BATCH, C_CONV - 1, C_WIDTH), 1.0)
    norm1_g = 1.0 + nrm((DEPTH, D_MODEL), 0.01)
    w_in = nrm((DEPTH, D_MODEL, N_IN), D_MODEL ** -0.5)
    a_conv_w = nrm((DEPTH, A_CONV, A_CONV_CH), A_CONV ** -0.5)
    a_A_log = jnp.log(uni((DEPTH, A_HEADS), 1.0, 16.0))
    dt = jnp.exp(uni((DEPTH, A_HEADS), math.log(1e-3), math.log(1e-1)))
    a_dt_bias = dt + jnp.log(-jnp.expm1(-dt))
    a_norm_g = 1.0 + nrm((DEPTH, A_DV), 0.01)
    b_mu = uni((DEPTH, B_IN), 0.0, 1.0)
    b_w0 = uni((DEPTH, B_W), -6.0, -1.0)
    b_w_up = nrm((DEPTH, B_DECAY_LORA, B_W), 0.1 * B_DECAY_LORA ** -0.5)
    b_a0 = nrm((DEPTH, B_W), 0.1)
    b_a_up = nrm((DEPTH, B_AAA_LORA, B_W), 0.1 * B_AAA_LORA ** -0.5)
    b_g_up = nrm((DEPTH, B_GATE_LORA, B_W), B_GATE_LORA ** -0.5)
    b_k_k = 0.85 + nrm((DEPTH, B_W), 0.01)
    b_k_a = 1.0 + nrm((DEPTH, B_W), 0.01)
    b_r_k = nrm((DEPTH, B_HEADS, B_N), 0.1)
    b_ln_w = 1.0 + nrm((DEPTH, B_W), 0.01)
    b_ln_b = nrm((DEPTH, B_W), 0.01)
    c_conv_w = nrm((DEPTH, C_CONV, C_WIDTH), C_CONV ** -0.5)
    c_conv_b = nrm((DEPTH, C_WIDTH), 0.01)
    c_wa = nrm((DEPTH, C_BLOCKS, C_BLOCK, C_BLOCK), C_BLOCK ** -0.5)
    c_ba = nrm((DEPTH, C_WIDTH), 0.01)
    c_wx = nrm((DEPTH, C_BLOCKS, C_BLOCK, C_BLOCK), C_BLOCK ** -0.5)
    c_bx = nrm((DEPTH, C_WIDTH), 0.01)
    base = uni((DEPTH, C_WIDTH), 0.9, 0.999) ** (1.0 / C_POW)
    c_L = jnp.log(base) - jnp.log1p(-base)
    w_branch = nrm((DEPTH, N_BRANCH, BRANCH_W, D_MODEL), BRANCH_W ** -0.5)
    w_out = nrm((DEPTH, D_MODEL, D_MODEL), D_MODEL ** -0.5)
    norm2_g = 1.0 + nrm((DEPTH, D_MODEL), 0.01)
    w_up = nrm((DEPTH, D_MODEL, D_FF), D_MODEL ** -0.5)
    w_down = nrm((DEPTH, D_FF, D_MODEL), 0.5 * D_FF ** -0.5)
    final_norm_g = 1.0 + nrm((D_MODEL,), 0.01)
    return {'x_prompt': x_prompt, 'x_sample': x_sample,
            'state_a_S': state_a_S, 'state_a_conv': state_a_conv,
            'state_b_S': state_b_S, 'state_b_shift': state_b_shift,
            'state_c_h': state_c_h, 'state_c_conv': state_c_conv,
            'norm1_g': norm1_g, 'w_in': w_in,
            'a_conv_w': a_conv_w, 'a_A_log': a_A_log, 'a_dt_bias': a_dt_bias, 'a_norm_g': a_norm_g,
            'b_mu': b_mu, 'b_w0': b_w0, 'b_w_up': b_w_up, 'b_a0': b_a0, 'b_a_up': b_a_up,
            'b_g_up': b_g_up, 'b_k_k': b_k_k, 'b_k_a': b_k_a, 'b_r_k': b_r_k,
            'b_ln_w': b_ln_w, 'b_ln_b': b_ln_b,
            'c_conv_w': c_conv_w, 'c_conv_b': c_conv_b, 'c_wa': c_wa, 'c_ba': c_ba,
            'c_wx': c_wx, 'c_bx': c_bx, 'c_L': c_L,
            'w_branch': w_branch, 'w_out': w_out, 'norm2_g': norm2_g,
            'w_up': w_up, 'w_down': w_down, 'final_norm_g': final_norm_g}


def reference(x_prompt, x_sample, state_a_S, state_a_conv, state_b_S, state_b_shift, state_c_h, state_c_conv,
              norm1_g, w_in, a_conv_w, a_A_log, a_dt_bias, a_norm_g,
              b_mu, b_w0, b_w_up, b_a0, b_a_up, b_g_up, b_k_k, b_k_a, b_r_k, b_ln_w, b_ln_b,
              c_conv_w, c_conv_b, c_wa, c_ba, c_wx, c_bx, c_L,
              w_branch, w_out, norm2_g, w_up, w_down, final_norm_g):
    f32 = jnp.float32
    xp, xs = x_prompt, x_sample
    Bp = x_prompt.shape[0]
    p_new = [[] for _ in range(6)]
    s_new = [[] for _ in range(6)]
    for l in range(DEPTH):
        p = {'norm1_g': norm1_g[l], 'w_in': w_in[l],
             'a_conv_w': a_conv_w[l], 'a_A_log': a_A_log[l], 'a_dt_bias': a_dt_bias[l], 'a_norm_g': a_norm_g[l],
             'b_mu': b_mu[l], 'b_w0': b_w0[l], 'b_w_up': b_w_up[l], 'b_a0': b_a0[l], 'b_a_up': b_a_up[l],
             'b_g_up': b_g_up[l], 'b_k_k': b_k_k[l], 'b_k_a': b_k_a[l], 'b_r_k': b_r_k[l],
             'b_ln_w': b_ln_w[l], 'b_ln_b': b_ln_b[l],
             'c_conv_w': c_conv_w[l], 'c_conv_b': c_conv_b[l], 'c_wa': c_wa[l], 'c_ba': c_ba[l],
             'c_wx': c_wx[l], 'c_bx': c_bx[l], 'c_L': c_L[l],
             'w_branch': w_branch[l], 'w_out': w_out[l], 'norm2_g': norm2_g[l],
             'w_up': w_up[l], 'w_down': w_down[l]}
        xp, pst = hybrid_layer(
            xp,
            jnp.zeros((Bp, A_HEADS, A_DK, A_DV), f32),
            jnp.zeros((Bp, A_CONV - 1, A_CONV_CH), xp.dtype),
            jnp.zeros((Bp, B_HEADS, B_N, B_N), f32),
            jnp.zeros((Bp, 1, B_IN), xp.dtype),
            jnp.zeros((Bp, C_WIDTH), f32),
            jnp.zeros((Bp, C_CONV - 1, C_WIDTH), xp.dtype),
            p)
        xs, sst = hybrid_layer(xs, state_a_S[l], state_a_conv[l], state_b_S[l], state_b_shift[l],
                               state_c_h[l], state_c_conv[l], p)
        for j in range(6):
            p_new[j].append(pst[j])
            s_new[j].append(sst[j])
    y_prompt = rmsnorm(xp, final_norm_g)
    y_sample = rmsnorm(xs, final_norm_g)
    return (y_prompt, y_sample,
            jnp.stack(p_new[0]), jnp.stack(p_new[1]), jnp.stack(p_new[2]),
            jnp.stack(p_new[3]), jnp.stack(p_new[4]), jnp.stack(p_new[5]),
            jnp.stack(s_new[0]), jnp.stack(s_new[1]), jnp.stack(s_new[2]),
            jnp.stack(s_new[3]), jnp.stack(s_new[4]), jnp.stack(s_new[5]))
```

```python
import os
import numpy as np
from contextlib import ExitStack
import concourse.bass as bass
import concourse.mybir as mybir
from concourse.bass_utils import run_bass_kernel_spmd

F32 = mybir.dt.float32
BF16 = mybir.dt.bfloat16
AF = mybir.ActivationFunctionType
ALU = mybir.AluOpType
AX = mybir.AxisListType

ENGS = ['pe', 'act', 'dve', 'pool', 'sp']

EN_A = os.environ.get('K_EN_A', '1') == '1'
EN_B = os.environ.get('K_EN_B', '1') == '1'
EN_C = os.environ.get('K_EN_C', '1') == '1'
K_LAYERS = int(os.environ.get('K_LAYERS', '2'))
K_MERGE = os.environ.get('K_MERGE', '1') == '1'
K_ASTOP = int(os.environ.get('K_ASTOP', '0'))
SKIP_SELF_WAW = os.environ.get('K_SELF_WAW', '0') == '0'
K_WARM = int(os.environ.get('K_WARM', '0'))


class _Stop(Exception):
    pass


def interleave(*gens):
    gens = [g for g in gens if g is not None]
    while gens:
        for g in list(gens):
            try:
                next(g)
            except StopIteration:
                gens.remove(g)


def drain(g):
    for _ in g:
        pass


class Pipe:
    def __init__(self):
        self.active = []

    def add(self, g):
        self.active.append(g)

    def step_all(self):
        for g in list(self.active):
            try:
                next(g)
            except StopIteration:
                self.active.remove(g)

    def finish(self, g):
        while g in self.active:
            self.step_all()

    def run_with(self, main):
        self.active.insert(0, main)
        self.finish(main)

    def drain_all(self):
        while self.active:
            self.step_all()


class V:
    __slots__ = ('ap', 'keys')

    def __init__(self, ap, *keys):
        self.ap = ap
        self.keys = [k if isinstance(k, tuple) else (k,) for k in keys]


class Prog:
    def __init__(self, nc, es):
        self.nc = nc
        self.es = es
        self.q = {e: [] for e in ENGS}
        self.cnt = {e: 0 for e in ENGS}
        self.waited = {e: {} for e in ENGS}
        self.semh = {}
        self.rec = {}
        self.children = {}
        self.dma_cnt = {}
        self.psum_tiles = []
        self.psum_i = 0
        self.warm_fn = None
        self._in_warm = False
        self.psum_extra = []
        for e in ['pe', 'act', 'dve', 'pool']:
            self.sem(e)

    def sem(self, name):
        if name not in self.semh:
            self.semh[name] = self.es.enter_context(
                self.nc.semaphore('s_' + name.replace(':', '_').replace('/', '_')))
        return self.semh[name]

    def sb(self, name, shape, dtype=F32):
        return self.es.enter_context(self.nc.sbuf_tensor(name, list(shape), dtype))

    def init_psum(self, n=8, reserve=0):
        for i in range(n):
            t = self.es.enter_context(self.nc.psum_tensor('ps%d' % i, [128, 512], F32))
            if i < n - reserve:
                self.psum_tiles.append(('ps%d' % i, t))
            else:
                self.psum_extra.append(('ps%d' % i, t))

    def ps(self):
        name, t = self.psum_tiles[self.psum_i % len(self.psum_tiles)]
        self.psum_i += 1
        return name, t

    @staticmethod
    def _k(k):
        return k if isinstance(k, tuple) else (k,)

    def _conflicts(self, key):
        out = []
        for i in range(1, len(key) + 1):
            r = self.rec.get(key[:i])
            if r is not None:
                out.append(r)
        stack = list(self.children.get(key, ()))
        while stack:
            c = stack.pop()
            r = self.rec.get(c)
            if r is not None:
                out.append(r)
            stack.extend(self.children.get(c, ()))
        return out

    def _get(self, key):
        r = self.rec.get(key)
        if r is None:
            r = [None, []]
            self.rec[key] = r
            for i in range(1, len(key)):
                self.children.setdefault(key[:i], set()).add(key[:i + 1])
        return r

    def emit(self, eng, fn, r=(), w=(), dma_tile=None):
        r = [self._k(k) for k in r]
        w = [self._k(k) for k in w]
        psr = [(k[0],) for k in r if k[0].startswith('ps')]
        r = [k for k in r if not k[0].startswith('ps')]
        w = [((k[0],) if k[0].startswith('ps') else k) for k in w] + psr
        is_dma = dma_tile is not None
        semname = None
        if is_dma:
            semname = 'dma:' + '/'.join(map(str, self._k(dma_tile)))
        deps = []
        for k in r:
            for rc in self._conflicts(k):
                if rc[0] is not None:
                    deps.append((rc[0], 'raw', False))
        for k in w:
            isps = k[0].startswith('ps')
            for rc in self._conflicts(k):
                if rc[0] is not None:
                    deps.append((rc[0], 'waw', isps))
                for ev in rc[1]:
                    deps.append((ev, 'war', isps))
        need = {}
        for (ev, kind, isps) in deps:
            sem, val, src_eng, src_dma = ev
            if src_eng == eng and not src_dma and not is_dma:
                if eng == 'pe' or isps:
                    continue
                if kind == 'war' or (kind == 'waw' and SKIP_SELF_WAW):
                    continue
            if is_dma and src_dma and sem == semname and kind == 'waw':
                continue
            if src_dma:
                val = 16 * self.dma_cnt[sem]
            if need.get(sem, 0) < val:
                need[sem] = val
        waits = []
        wd = self.waited[eng]
        for sem, val in need.items():
            if wd.get(sem, 0) >= val:
                continue
            wd[sem] = val
            waits.append((sem, val))
        if eng == 'pe' and self.warm_fn is not None and waits and not self._in_warm:
            self._in_warm = True
            self.warm_fn()
            self._in_warm = False
        if is_dma:
            self.sem(semname)
            self.dma_cnt[semname] = self.dma_cnt.get(semname, 0) + 1
            ev = (semname, 16 * self.dma_cnt[semname], eng, True)
            inc = (semname, 16)
        else:
            self.cnt[eng] += 1
            ev = (eng, self.cnt[eng], eng, False)
            inc = (eng, 1)
        self.q[eng].append((waits, fn, inc))
        for k in w:
            rc = self._get(k)
            rc[0] = ev
            rc[1] = []
            stack = list(self.children.get(k, ()))
            while stack:
                c = stack.pop()
                if c in self.rec:
                    self.rec[c] = [None, []]
                stack.extend(self.children.get(c, ()))
        for k in r:
            rc = self._get(k)
            rc[1].append(ev)
        return ev

    def final_wait_all(self, eng='sp'):
        waits = []
        for sem, n in self.dma_cnt.items():
            waits.append((sem, 16 * n))
        for e in ['pe', 'act', 'dve', 'pool']:
            if self.cnt[e]:
                waits.append((e, self.cnt[e]))
        self.q[eng].append((waits, None, None))

    def build(self):
        nc = self.nc
        with nc.Block() as block:
            def replay(ename):
                def f(engine):
                    for (waits, fn, inc) in self.q[ename]:
                        for (sem, val) in waits:
                            engine.wait_ge(self.semh[sem], val)
                        if fn is None:
                            continue
                        ins = fn(engine)
                        ins.then_inc(self.semh[inc[0]], inc[1])
                return f
            block.tensor(replay('pe'))
            block.scalar(replay('act'))
            block.vector(replay('dve'))
            block.gpsimd(replay('pool'))
            block.sync(replay('sp'))

    @staticmethod
    def _rk(*ops):
        ks = []
        for o in ops:
            if isinstance(o, V):
                ks += o.keys
        return ks

    @staticmethod
    def _a(o):
        return o.ap if isinstance(o, V) else o

    def mm(self, out, lhsT, rhs, start=True, stop=True):
        self.emit('pe', lambda e: e.matmul(out.ap, lhsT=lhsT.ap, rhs=rhs.ap, start=start, stop=stop),
                  r=self._rk(lhsT, rhs), w=out.keys)

    def tr(self, out, in_, ident):
        self.emit('pe', lambda e: e.transpose(out=out.ap, in_=in_.ap, identity=ident.ap),
                  r=self._rk(in_, ident), w=out.keys)

    def act(self, out, in_, func, bias=None, scale=None):
        kw = {}
        if bias is not None:
            kw['bias'] = self._a(bias)
        if scale is not None:
            kw['scale'] = self._a(scale)
        self.emit('act', lambda e: e.activation(out=out.ap, in_=in_.ap, func=func, **kw),
                  r=self._rk(in_, bias, scale), w=out.keys)

    def tt(self, out, in0, in1, op, eng='dve'):
        self.emit(eng, lambda e: e.tensor_tensor(out=out.ap, in0=in0.ap, in1=in1.ap, op=op),
                  r=self._rk(in0, in1), w=out.keys)

    def ts(self, out, in0, s1, s2, op0, op1=None, eng='dve'):
        if op1 is None:
            fn = lambda e: e.tensor_scalar(out=out.ap, in0=in0.ap, scalar1=self._a(s1), scalar2=None, op0=op0)
        else:
            fn = lambda e: e.tensor_scalar(out=out.ap, in0=in0.ap, scalar1=self._a(s1), scalar2=self._a(s2),
                                           op0=op0, op1=op1)
        self.emit(eng, fn, r=self._rk(in0, s1, s2), w=out.keys)

    def stt(self, out, in0, scalar, in1, op0, op1):
        self.emit('dve', lambda e: e.scalar_tensor_tensor(out=out.ap, in0=in0.ap, scalar=self._a(scalar),
                                                          in1=in1.ap, op0=op0, op1=op1),
                  r=self._rk(in0, scalar, in1), w=out.keys)

    def scan(self, out, d0, d1, initial, op0, op1):
        self.emit('dve', lambda e: e.tensor_tensor_scan(out=out.ap, data0=d0.ap, data1=d1.ap,
                                                        initial=self._a(initial), op0=op0, op1=op1),
                  r=self._rk(d0, d1, initial), w=out.keys)

    def recip(self, out, in_):
        self.emit('dve', lambda e: e.reciprocal(out=out.ap, in_=in_.ap), r=in_.keys, w=out.keys)

    def red(self, out, in_, op=ALU.add):
        self.emit('dve', lambda e: e.tensor_reduce(out=out.ap, in_=in_.ap, axis=AX.X, op=op),
                  r=in_.keys, w=out.keys)

    def copy(self, out, in_, eng='dve'):
        if eng == 'act':
            self.emit('act', lambda e: e.activation(out=out.ap, in_=in_.ap, func=AF.Copy), r=in_.keys, w=out.keys)
        else:
            self.emit(eng, lambda e: e.tensor_copy(out=out.ap, in_=in_.ap), r=in_.keys, w=out.keys)

    def memset(self, out, val, eng='pool'):
        self.emit(eng, lambda e: e.memset(out.ap, val), w=out.keys)

    def asel(self, out, in_, pattern, cmp, fill, base, cm):
        self.emit('pool', lambda e: e.affine_select(out=out.ap, in_=in_.ap, pattern=pattern, compare_op=cmp,
                                                    fill=fill, base=base, channel_multiplier=cm),
                  r=in_.keys, w=out.keys)

    def dma(self, out, in_, tile_key, eng='sp', out_is_sb=True, nc_ok=False):
        kw = {}
        if nc_ok:
            kw['allow_slow_non_contiguous'] = True
        oa = self._a(out)
        ia = self._a(in_)
        self.emit(eng, lambda e: e.dma_start(out=oa, in_=ia, **kw),
                  r=self._rk(in_), w=self._rk(out), dma_tile=tile_key)


NCORES = 8
D = 1024
KC = 8
SEQ = 2048
DEPTH = 2
NSEQ_CORE = 16
TP = 1024
NS = 8
TS = 32
T = TP + TS
NBLK = 2
TT = [(0, 352), (352, 704), (704, 1056)]
N_IN = 7944
A_OFF, B_OFF, C_OFF, G_OFF = 0, 2056, 3848, 4872
EPS = 1e-6
B_LN_EPS = 64e-5
RW = 1088
BW = 1152
NCHUNK = 9
NEU_LEVELS_P = 7
NEU_LEVELS_S = 2


def chunk_info(c):
    if c < 8:
        return c * 128, 128, 1
    return TP, TS, NS


PVECS = [
    ('norm1_g', 16), ('norm2_g', 16), ('final_norm_g', 8), ('a_conv_w', 96), ('a_norm_g', 2),
    ('b_mu', 28), ('b_w0', 8), ('b_a0', 8), ('b_k_k', 8), ('b_k_a', 8), ('b_ln_w', 8), ('b_ln_b', 8),
    ('b_r_k', 8), ('c_conv_w', 32), ('c_conv_b', 8), ('c_ba', 8), ('c_bx', 8), ('c_L', 8),
]
PREARR = {
    'norm1_g': "l (r p) -> (l r) p", 'norm2_g': "l (r p) -> (l r) p", 'final_norm_g': "(r p) -> r p",
    'a_conv_w': "l j (r p) -> (l j r) p", 'a_norm_g': "l p -> l p", 'b_mu': "l (r p) -> (l r) p",
    'b_w0': "l (r p) -> (l r) p", 'b_a0': "l (r p) -> (l r) p", 'b_k_k': "l (r p) -> (l r) p",
    'b_k_a': "l (r p) -> (l r) p", 'b_ln_w': "l (r p) -> (l r) p", 'b_ln_b': "l (r p) -> (l r) p",
    'b_r_k': "l (r h2) n -> (l r) (h2 n)", 'c_conv_w': "l j (r p) -> (l j r) p",
    'c_conv_b': "l (r p) -> (l r) p", 'c_ba': "l (r p) -> (l r) p", 'c_bx': "l (r p) -> (l r) p",
    'c_L': "l (r p) -> (l r) p",
}


def param_rows():
    off = {}
    r = 0
    for name, n in PVECS:
        if (r % 128) + n > 128:
            r = (r // 128 + 1) * 128
        off[name] = r
        r += n
    nst = (r + 127) // 128
    return off, nst


POFF, PNST = param_rows()

INPUT_SHAPES = {
    'xp': [SEQ, D], 'xs': [NSEQ_CORE * 4, D],
    'sa_S': [DEPTH, NSEQ_CORE, 4, 128, 128], 'sa_conv': [DEPTH, NSEQ_CORE, 3, 1536],
    'sb_S': [DEPTH, NSEQ_CORE, 8, 64, 64], 'sb_shift': [DEPTH, NSEQ_CORE, 1, 1792],
    'sc_h': [DEPTH, NSEQ_CORE, 512], 'sc_conv': [DEPTH, NSEQ_CORE, 3, 512],
    'norm1_g': [DEPTH, D], 'w_in': [DEPTH, D, N_IN], 'a_conv_w': [DEPTH, 4, 1536], 'a_A_log': [DEPTH, 4],
    'a_dt_bias': [DEPTH, 4], 'a_norm_g': [DEPTH, 128], 'b_mu': [DEPTH, 1792], 'b_w0': [DEPTH, 512],
    'b_w_up': [DEPTH, 64, 512], 'b_a0': [DEPTH, 512], 'b_a_up': [DEPTH, 64, 512], 'b_g_up': [DEPTH, 128, 512],
    'b_k_k': [DEPTH, 512], 'b_k_a': [DEPTH, 512], 'b_r_k': [DEPTH, 8, 64], 'b_ln_w': [DEPTH, 512],
    'b_ln_b': [DEPTH, 512], 'c_conv_w': [DEPTH, 4, 512], 'c_conv_b': [DEPTH, 512], 'c_wa': [DEPTH, 8, 64, 64],
    'c_ba': [DEPTH, 512], 'c_wx': [DEPTH, 8, 64, 64], 'c_bx': [DEPTH, 512], 'c_L': [DEPTH, 512],
    'w_branch': [DEPTH, 3, 512, D], 'w_out': [DEPTH, D, D], 'norm2_g': [DEPTH, D], 'w_up': [DEPTH, D, 4 * D],
    'w_down': [DEPTH, 4 * D, D], 'final_norm_g': [D],
}
OUTPUT_SHAPES = {
    'y_p': [SEQ, D], 'y_s': [NSEQ_CORE * 4, D],
    'p_a_S': [DEPTH, 4, 128, 128], 'p_a_conv': [DEPTH, 3, 1536], 'p_b_S': [DEPTH, 8, 64, 64],
    'p_b_shift': [DEPTH, 1, 1792], 'p_c_h': [DEPTH, 512], 'p_c_conv': [DEPTH, 3, 512],
    's_a_S': [DEPTH, NSEQ_CORE, 4, 128, 128], 's_a_conv': [DEPTH, NSEQ_CORE, 3, 1536],
    's_b_S': [DEPTH, NSEQ_CORE, 8, 64, 64], 's_b_shift': [DEPTH, NSEQ_CORE, 1, 1792],
    's_c_h': [DEPTH, NSEQ_CORE, 512], 's_c_conv': [DEPTH, NSEQ_CORE, 3, 512],
}


class Builder:
    def __init__(self):
        self.nc = bass.Bass("TRN2", target_bir_lowering=False)
        nc = self.nc
        self.I = {k: nc.dram_tensor(k, s, F32, kind="ExternalInput").ap() for k, s in INPUT_SHAPES.items()}
        self.O = {k: nc.dram_tensor(k, s, F32, kind="ExternalOutput").ap() for k, s in OUTPUT_SHAPES.items()}

    def alloc(self):
        P = self.P
        self.xT = P.sb('xT', [128, KC, T], F32)
        self.hT = P.sb('hT', [128, KC, T], BF16)
        self.yb = P.sb('yb', [128, 12, T], BF16)
        self.ring = [P.sb('wr%d' % i, [128, 4096], BF16) for i in range(3)]
        self.R = [P.sb('R%d' % i, [128, RW], F32) for i in range(8)]
        self.Bt = [P.sb('B%d' % i, [128, BW], BF16) for i in range(10)]
        self.PT = P.sb('PT', [128, PNST * 128], F32)
        self.identf = P.sb('identf', [128, 128], F32)
        self.identb = P.sb('identb', [128, 128], BF16)
        self.onesf = P.sb('onesf', [128, 128], F32)
        self.onesb = P.sb('onesb', [128, 128], BF16)
        self.ones64b = P.sb('ones64b', [128, 128], BF16)
        self.ones64f = P.sb('ones64f', [128, 128], F32)
        self.m_uincl = P.sb('m_uincl', [128, 2, 128], F32)
        self.m_lstr = P.sb('m_lstr', [128, 2, 128], F32)
        self.m_bigL = P.sb('m_bigL', [128, 2, 128], F32)
        self.m_negU = P.sb('m_negU', [128, 2, 128], F32)
        self.m_ustr = P.sb('m_ustr', [128, 2, 128], F32)
        self.seqm = P.sb('seqm', [32, 8], F32)
        self.seqmT = P.sb('seqmT', [8, 32], F32)
        self.rmask = P.sb('rmask', [128, T], BF16)
        self.small = P.sb('small', [128, 4, 352], F32)
        self.sqt = P.sb('sqt', [128, 352], BF16)
        self.sqts = [self.sqt, P.sb('sqt1', [128, 352], BF16), P.sb('sqt2', [128, 352], BF16)]
        self.sqtk = ['sqt', 'sqt1', 'sqt2']
        self.csm = P.sb('csm', [128, 4, 96], F32)
        self.c128f = [P.sb('cf0', [128, 512], F32)] + [P.sb('cf%d' % i, [128, 128], F32) for i in range(1, 6)]
        self.c128b = [P.sb('cb%d' % i, [128, 512], BF16) for i in range(8)]
        self.SA = P.sb('SA', [128, DEPTH, 4, 128], F32)
        self.SAb = P.sb('SAb', [128, 4, 128], BF16)
        self.HB = P.sb('HB', [128, DEPTH, 4, 64], F32)
        self.HBb = P.sb('HBb', [128, 4, 64], BF16)
        self.hC = P.sb('hC', [128, DEPTH, 4], F32)
        self.tailA = P.sb('tailA', [128, DEPTH, 12, 3], F32)
        self.tailB = P.sb('tailB', [128, DEPTH, 14], F32)
        self.tailC = P.sb('tailC', [128, DEPTH, 4, 3], F32)
        self.Ss = P.sb('Ss', [128, NS * 128], F32)
        self.a4 = P.sb('a4', [4, DEPTH, 2], F32)
        self.colsA = P.sb('colsA', [128, NCHUNK, 24], F32)
        self.glc = P.sb('glc', [128, 16], F32)
        self.lorW = P.sb('lorW', [128, 512], BF16)
        self.gupW = P.sb('gupW', [128, 512], BF16)
        self.cgate = P.sb('cgate', [128, 4, 2, 128], BF16)
        self.rkbd = P.sb('rkbd', [128, 4, 128], BF16)
        self.stage = [P.sb('stg0', [32, 512], F32)]
        self.stage.append(self.stage[0])

    def mch(self, j, t0, t1):
        r = 4 + j // 2
        v = self.R[r][:, 0:2 * 528].bitcast(BF16)
        o = (j % 2) * T
        return V(v[:, o + t0:o + t1], ('R%d' % r, j % 2))

    def astop(self, k):
        if hasattr(self, 'marks'):
            P = self.P
            self.marks.append(('  a-stage%d' % k, P.cnt['pe'], P.cnt['act'], P.cnt['dve']))
        if K_ASTOP == k:
            raise _Stop()

    def ti_pipe(self, gen_fn):
        interleave(*[gen_fn(ti, t0, t1, t1 - t0) for ti, (t0, t1) in enumerate(TT)])

    def pcol(self, name, idx):
        r = POFF[name] + idx
        return V(self.PT[:, r:r + 1], 'PT')

    def setup_consts(self):
        P = self.P
        I = self.I
        onesf = V(self.onesf[:], 'onesf')
        P.memset(onesf, 1.0)
        P.memset(V(self.onesb[:], 'onesb'), 1.0)
        P.asel(V(self.identf[:], 'identf'), onesf, [[-1, 128]], ALU.is_equal, 0.0, 0, 1)
        P.copy(V(self.identb[:], 'identb'), V(self.identf[:], 'identf'), eng='pool')
        o64 = V(self.ones64f[:], 'ones64f')
        P.memset(o64, 0.0)
        P.memset(V(self.ones64f[0:64, 0:64], 'ones64f'), 1.0)
        P.memset(V(self.ones64f[64:128, 64:128], 'ones64f'), 1.0)
        P.copy(V(self.ones64b[:], 'ones64b'), o64, eng='pool')
        P.asel(V(self.seqm[:], 'seqm'), V(self.onesf[0:32, 0:8], 'onesf'), [[-4, 8]], ALU.is_ge, 0.0, 0, 1)
        P.asel(V(self.seqm[:], 'seqm'), V(self.seqm[:], 'seqm'), [[4, 8]], ALU.is_ge, 0.0, 3, -1)
        P.asel(V(self.seqmT[:], 'seqmT'), V(self.onesf[0:8, 0:32], 'onesf'), [[1, 32]], ALU.is_ge, 0.0, 0, -4)
        P.asel(V(self.seqmT[:], 'seqmT'), V(self.seqmT[:], 'seqmT'), [[-1, 32]], ALU.is_ge, 0.0, 3, 4)
        ui = V(self.m_uincl[:, 0, :], 'm_uincl')
        P.asel(ui, onesf, [[1, 128]], ALU.is_ge, 0.0, 0, -1)
        ls = V(self.m_lstr[:, 0, :], 'm_lstr')
        P.asel(ls, onesf, [[-1, 128]], ALU.is_gt, 0.0, 0, 1)
        pn, ps = P.ps()
        same = V(ps[0:32, 0:32], pn)
        P.mm(same, V(self.seqmT[:], 'seqmT'), V(self.seqmT[:], 'seqmT'))
        P.memset(V(self.m_uincl[:, 1, :], 'm_uincl'), 0.0)
        P.memset(V(self.m_lstr[:, 1, :], 'm_lstr'), 0.0)
        P.tt(V(self.m_uincl[0:32, 1, 0:32], 'm_uincl'), same, V(self.m_uincl[0:32, 0, 0:32], 'm_uincl'), ALU.mult)
        P.tt(V(self.m_lstr[0:32, 1, 0:32], 'm_lstr'), same, V(self.m_lstr[0:32, 0, 0:32], 'm_lstr'), ALU.mult)
        P.ts(V(self.m_bigL[:], 'm_bigL'), V(self.m_lstr[:], 'm_lstr'), -1e4, 1e4, ALU.mult, ALU.add)
        P.ts(V(self.m_negU[:], 'm_negU'), V(self.m_uincl[:], 'm_uincl'), 1e4, -1e4, ALU.mult, ALU.add)
        P.tt(V(self.m_ustr[:, 0, :], 'm_ustr'), V(self.m_uincl[:, 0, :], 'm_uincl'), V(self.identf[:], 'identf'), ALU.subtract)
        P.memset(V(self.m_ustr[:, 1, :], 'm_ustr'), 0.0)
        P.tt(V(self.m_ustr[0:32, 1, 0:32], 'm_ustr'), V(self.m_uincl[0:32, 1, 0:32], 'm_uincl'),
             V(self.identf[0:32, 0:32], 'identf'), ALU.subtract)
        rm = V(self.rmask[:], 'rmask')
        P.memset(rm, 1.0)
        P.memset(V(self.rmask[:, 0:TP].rearrange("p (c n) -> p c n", n=128)[:, :, 0], 'rmask'), 0.0)
        P.memset(V(self.rmask[:, TP:T].rearrange("p (s j) -> p s j", j=4)[:, :, 0], 'rmask'), 0.0)
        stg = self.R[0]
        for st in range(PNST):
            for name, n in PVECS:
                r0 = POFF[name]
                if r0 // 128 != st:
                    continue
                if name == 'a_norm_g':
                    src = I[name]
                elif name == 'b_r_k':
                    src = I[name].rearrange("l (r h2) n -> (l r) (h2 n)", h2=2)
                else:
                    src = I[name].rearrange(PREARR[name], p=128)
                P.dma(V(stg[r0 % 128:r0 % 128 + n, 0:128], 'R0'), src, 'R0')
            nrows = max((POFF[nm] + n - st * 128) for nm, n in PVECS if POFF[nm] // 128 == st)
            pn, ps = P.ps()
            P.tr(V(ps[:, 0:nrows], pn), V(stg[0:nrows, 0:128], 'R0'), V(self.identf[0:nrows, 0:nrows], 'identf'))
            P.copy(V(self.PT[:, st * 128:st * 128 + nrows], 'PT'), V(ps[:, 0:nrows], pn), eng='act')
        P.dma(V(self.a4[:, :, 0], 'a4'), I['a_A_log'].rearrange("l h -> h l"), 'a4', nc_ok=True)
        P.dma(V(self.a4[:, :, 1], 'a4'), I['a_dt_bias'].rearrange("l h -> h l"), 'a4', nc_ok=True)
        P.act(V(self.a4[:, :, 0], 'a4'), V(self.a4[:, :, 0], 'a4'), AF.Exp)
        P.ts(V(self.a4[:, :, 0], 'a4'), V(self.a4[:, :, 0], 'a4'), -1.0, None, ALU.mult)
        for nm in ['SA', 'HB', 'hC', 'tailA', 'tailB', 'tailC']:
            P.memset(V(getattr(self, nm)[:], nm), 0.0)
        P.memset(V(self.cgate[:], 'cgate'), 0.0)

    def weight_schedule(self):
        I = self.I
        sched = []
        for blk in range(NBLK):
            for l in range(K_LAYERS):
                win = I['w_in'][l].rearrange("(kc p) n -> p kc n", p=128)

                def w512(c0, win=win):
                    return [(lambda t: t[:, :].rearrange("p (kc n) -> p kc n", kc=8), win[:, :, c0:c0 + 512])]

                if EN_A:
                    sched.append((('A_ba', blk, l),
                                  [(lambda t: t[:, 0:64].rearrange("p (kc n) -> p kc n", kc=8),
                                    win[:, :, A_OFF + 2048:A_OFF + 2056])]))
                    for hd in range(4):
                        pcs = []
                        for cc in range(4):
                            pcs.append((lambda t, cc=cc: t[:, :].rearrange("p (kc c n) -> p kc c n", kc=8, c=4)[:, :, cc, :],
                                        win[:, :, A_OFF + cc * 512 + hd * 128:A_OFF + cc * 512 + (hd + 1) * 128]))
                        sched.append((('A_head', blk, l, hd), pcs))
                if EN_B:
                    sched.append((('B_lora', blk, l),
                                  [(lambda t: t[:, 0:2048].rearrange("p (kc n) -> p kc n", kc=8),
                                    win[:, :, B_OFF + 1536:B_OFF + 1792])]))
                    for pr in range(4):
                        pcs = []
                        for cc in range(3):
                            pcs.append((lambda t, cc=cc: t[:, 0:3072].rearrange("p (kc c n) -> p kc c n", kc=8, c=3)[:, :, cc, :],
                                        win[:, :, B_OFF + cc * 512 + pr * 128:B_OFF + cc * 512 + (pr + 1) * 128]))
                        sched.append((('B_pair', blk, l, pr), pcs))
                if EN_C:
                    sched.append((('C_x', blk, l), w512(C_OFF)))
                    sched.append((('C_g', blk, l), w512(C_OFF + 512)))
                if not K_MERGE:
                    continue
                wbr = I['w_branch'][l]
                for jg in range(2):
                    for b in range(3):
                        sched.append((('gate', blk, l, jg, b), w512(G_OFF + b * 1024 + jg * 512)))
                        sched.append((('wbr', blk, l, jg, b),
                                      [(lambda t: t[:, 0:2048].rearrange("p (kc n) -> p kc n", kc=4),
                                        wbr[b].rearrange("(kc p) n -> p kc n", p=128)[:, :, jg * 512:(jg + 1) * 512])]))
                wo = I['w_out'][l].rearrange("(kc p) n -> p kc n", p=128)
                for jh in range(2):
                    sched.append((('wout', blk, l, jh),
                                  [(lambda t: t[:, :].rearrange("p (kc n) -> p kc n", kc=8), wo[:, :, jh * 512:(jh + 1) * 512])]))
                wu = I['w_up'][l].rearrange("(kc p) n -> p kc n", p=128)
                wd = I['w_down'][l].rearrange("(kc p) n -> p kc n", p=128)
                for q in range(4):
                    for g in range(2):
                        c0 = (2 * q + g) * 512
                        sched.append((('wup', blk, l, q, g),
                                      [(lambda t: t[:, :].rearrange("p (kc n) -> p kc n", kc=8), wu[:, :, c0:c0 + 512])]))
                    for jh in range(2):
                        sched.append((('wdn', blk, l, q, jh),
                                      [(lambda t: t[:, :].rearrange("p (kc n) -> p kc n", kc=8),
                                        wd[:, q * 8:(q + 1) * 8, jh * 512:(jh + 1) * 512])]))
        return sched

    def wnext(self, tag):
        P = self.P
        i = self.w_i
        if K_ASTOP:
            while self.sched[self.w_i][0] != tag:
                self.w_i += 1
            i = self.w_i
            self.w_issued = max(self.w_issued, i)
        assert self.sched[i][0] == tag, (self.sched[i][0], tag)
        while self.w_issued < min(len(self.sched), i + 2):
            j = self.w_issued
            slot = j % 3
            t = self.ring[slot]
            key = 'wr%d' % slot
            for (dst_fn, src) in self.sched[j][1]:
                P.dma(V(dst_fn(t), key), src, key, eng='pool')
            self.w_issued += 1
        self.w_i += 1
        return self.ring[i % 3], 'wr%d' % (i % 3)

    def load_x_block(self, blk):
        P = self.P
        identf = V(self.identf[:], 'identf')
        for i in range(8):
            stg = self.R[i % 2]
            sk = 'R%d' % (i % 2)
            P.dma(V(stg[:, 0:1024], sk), self.I['xp'][blk * TP + i * 128:blk * TP + (i + 1) * 128, :], sk)
            for half in range(2):
                pn, ps = P.ps()
                for c in range(4):
                    cc = half * 4 + c
                    P.tr(V(ps[:, c * 128:(c + 1) * 128], pn), V(stg[:, cc * 128:(cc + 1) * 128], sk), identf)
                P.copy(V(self.xT[:, half * 4:(half + 1) * 4, i * 128:(i + 1) * 128], 'xT'),
                       V(ps[:, :].rearrange("p (c n) -> p c n", c=4), pn), eng='act' if half else 'dve')
        stg = self.R[0]
        P.dma(V(stg[0:TS, 0:1024], 'R0'), self.I['xs'][blk * TS:(blk + 1) * TS, :], 'R0')
        pn, ps = P.ps()
        for c in range(8):
            P.tr(V(ps[:, c * 32:(c + 1) * 32], pn), V(stg[0:32, c * 128:(c + 1) * 128], 'R0'),
                 V(self.identf[0:32, 0:32], 'identf'))
        P.copy(V(self.xT[:, :, TP:T], 'xT'), V(ps[:, 0:256].rearrange("p (c n) -> p c n", c=8), pn))

    def xkeys(self, tt):
        return ('xT', tt)

    def rmsnorm_to_h(self, gname, l):
        P = self.P
        onesb = V(self.onesb[:], 'onesb')
        def g(ti, t0, t1, n):
            pn, ps = P.ps()
            sq = self.Bt[ti]
            sqk = 'B%d' % ti
            for g3, (c0, c1) in enumerate([(0, 3), (3, 6), (6, 8)]):
                P.act(V(sq[:, 0:(c1 - c0) * n].rearrange("p (c n) -> p c n", c=c1 - c0), sqk),
                      V(self.xT[:, c0:c1, t0:t1], ('xT', ti)), AF.Square)
                for c in range(c0, c1):
                    P.mm(V(ps[:, 0:n], pn), onesb, V(sq[:, (c - c0) * n:(c - c0 + 1) * n], sqk),
                         start=(c == 0), stop=(c == 7))
                yield
            rs = V(self.small[:, ti, 0:n], ('small', ti))
            P.act(rs, V(ps[:, 0:n], pn), AF.Ln, bias=self.epsc, scale=1.0 / D)
            P.act(rs, rs, AF.Exp, scale=-0.5)
            yield
            for c in range(KC):
                gcol = self.pcol(gname, (l * 8 + c) if l is not None else c)
                P.stt(V(self.hT[:, c, t0:t1], ('hT', ti)), V(self.xT[:, c, t0:t1], ('xT', ti)), gcol, rs,
                      ALU.mult, ALU.mult)
                if c % 4 == 3:
                    yield
        self.ti_pipe(g)

    def proj(self, wt, wkey, colsel, M, evac):
        P = self.P
        for ti, (t0, t1) in enumerate(TT):
            n = t1 - t0
            pn, ps = P.ps()
            for kc in range(KC):
                P.mm(V(ps[0:M, 0:n], pn), V(colsel(wt, kc), wkey), V(self.hT[:, kc, t0:t1], ('hT', ti)),
                     start=(kc == 0), stop=(kc == KC - 1))
            evac(ti, t0, t1, V(ps[0:M, 0:n], pn))

    def evac_to_X(self, Xt, Xk, hist):
        P = self.P
        w = hist + 4
        base = hist + TP

        def f(ti, t0, t1, psv):
            if ti < 2:
                P.copy(V(Xt[:, hist + t0:hist + t1], Xk), psv, eng='act')
            else:
                npr = TP - t0
                P.copy(V(Xt[:, hist + t0:hist + TP], Xk), V(psv.ap[:, 0:npr], *psv.keys), eng='act')
                P.copy(V(Xt[:, base:base + NS * w].rearrange("p (s j) -> p s j", j=w)[:, :, hist:w], Xk),
                       V(psv.ap[:, npr:npr + TS].rearrange("p (s j) -> p s j", j=4), *psv.keys), eng='dve')
        return f

    def conv4(self, out, ok, Xt, Xk, wname, wrow_fn):
        P = self.P
        hist = 3
        base = hist + TP
        for j in range(3, -1, -1):
            wc = self.pcol(wname, wrow_fn(j))
            src = V(Xt[:, j:j + TP], Xk)
            dst = V(out[:, 0:TP], ok)
            if j == 3:
                P.act(dst, src, AF.Identity, scale=wc)
            else:
                P.stt(dst, src, wc, dst, ALU.mult, ALU.add)
            srcs = V(Xt[:, base:base + NS * 7].rearrange("p (s j) -> p s j", j=7)[:, :, j:j + 4], Xk)
            dsts = V(out[:, TP:T].rearrange("p (s j) -> p s j", j=4), ok)
            if j == 3:
                P.ts(dsts, srcs, wc, None, ALU.mult)
            else:
                P.stt(dsts, srcs, wc, dsts, ALU.mult, ALU.add)

    def load_hist_T(self, src_rows, nrows, ncol_chunks, dst_fn):
        P = self.P
        stg = self.stage[0]
        P.dma(V(stg[0:nrows, 0:ncol_chunks * 128], 'stg0'), src_rows, 'stg0')
        for c in range(ncol_chunks):
            pn, ps = P.ps()
            P.tr(V(ps[:, 0:nrows], pn), V(stg[0:nrows, c * 128:(c + 1) * 128], 'stg0'),
                 V(self.identf[0:nrows, 0:nrows], 'identf'))
            dst_fn(c, V(ps[:, 0:nrows], pn))

    def store_rows(self, dram_ap, psv, nrows, ncols):
        P = self.P
        stg = self.stage[0]
        P.copy(V(stg[0:nrows, 0:ncols], 'stg0'), psv, eng='act')
        P.dma(dram_ap, V(stg[0:nrows, 0:ncols], 'stg0'), 'stg0')

    def rows_mm(self, tok_ap_fn, M, wt, wkey, colsel_n, ncols):
        P = self.P
        pn, ps = P.ps()
        for kc in range(KC):
            P.mm(V(ps[0:M, 0:ncols], pn), V(tok_ap_fn(kc), 'hT'), V(colsel_n(wt, kc), wkey),
                 start=(kc == 0), stop=(kc == KC - 1))
        return V(ps[0:M, 0:ncols], pn)

    def neumann(self, Nb, NTb, n, G, levels, X32, Xb, keys, fp32=False):
        P = self.P
        kN, kNT, kX32, kXb = keys
        idb = V(self.identf[0:n, 0:n].unsqueeze(1).to_broadcast([n, G, n]), 'identf')
        P.tt(V(X32[0:n, 0:G * n].rearrange("p (g n) -> p g n", g=G), kX32),
             V(NTb[0:n, 0:G * n].rearrange("p (g n) -> p g n", g=G), kNT), idb, ALU.add)
        if fp32:
            Xop, kXop = X32, kX32
        else:
            Xop, kXop = Xb, kXb
            P.copy(V(Xb[0:n, 0:G * n], kXb), V(X32[0:n, 0:G * n], kX32), eng='act')
        for m in range(1, levels):
            last = (m == levels - 1)
            pn1, ps1 = P.ps()
            for g in range(G):
                sl = slice(g * n, (g + 1) * n)
                P.mm(V(ps1[0:n, sl], pn1), V(NTb[0:n, sl], kNT), V(Nb[0:n, sl], kN))
            if not last:
                pn2, ps2 = P.ps()
                for g in range(G):
                    sl = slice(g * n, (g + 1) * n)
                    P.mm(V(ps2[0:n, sl], pn2), V(Nb[0:n, sl], kN), V(NTb[0:n, sl], kNT))
            yield
            P.copy(V(Nb[0:n, 0:G * n], kN), V(ps1[0:n, 0:G * n], pn1), eng='act')
            if not last:
                P.copy(V(NTb[0:n, 0:G * n], kNT), V(ps2[0:n, 0:G * n], pn2), eng='dve')
            yield
            pn3, ps3 = P.ps()
            for g in range(G):
                sl = slice(g * n, (g + 1) * n)
                P.mm(V(ps3[0:n, sl], pn3), V(Nb[0:n, sl], kN), V(Xop[0:n, sl], kXop))
            if not fp32:
                P.tt(V(Xb[0:n, 0:G * n], kXb), V(ps3[0:n, 0:G * n], pn3), V(X32[0:n, 0:G * n], kX32), ALU.add)
                yield
                if not last:
                    P.tt(V(X32[0:n, 0:G * n], kX32), V(ps3[0:n, 0:G * n], pn3), V(X32[0:n, 0:G * n], kX32), ALU.add)
            else:
                P.tt(V(X32[0:n, 0:G * n], kX32), V(ps3[0:n, 0:G * n], pn3), V(X32[0:n, 0:G * n], kX32), ALU.add)
                yield
                if last:
                    P.copy(V(Xb[0:n, 0:G * n], kXb), V(X32[0:n, 0:G * n], kX32), eng='act')
            yield

    def branch_C(self, blk, l):
        P = self.P
        I, O = self.I, self.O
        R, Bt = self.R, self.Bt
        s0 = blk * NS
        for g in range(2):
            for ax, nm in enumerate(['c_wa', 'c_wx']):
                src = I[nm][l].rearrange("(c g) i j -> g i c j", g=2)[g]
                P.dma(V(self.cgate[g * 64:(g + 1) * 64, :, ax, g * 64:(g + 1) * 64], 'cgate'), src, 'cgate',
                      eng='pool')
        wx_t, wx_k = self.wnext(('C_x', blk, l))
        w512 = lambda t: t[:, :].rearrange("p (kc n) -> p kc n", kc=8)
        csm = self.csm
        self.load_hist_T(I['sc_h'][l, s0:s0 + NS, :], NS, 4,
                         lambda c, psv: P.copy(V(csm[:, 0, c * 8:(c + 1) * 8], ('csm', 0)), psv))
        self.load_hist_T(I['sc_conv'][l, s0:s0 + NS].rearrange("s j c -> (s j) c"), NS * 3, 4,
                         lambda c, psv: P.copy(V(csm[:, 1, c * 24:(c + 1) * 24], ('csm', 1)), psv))
        for j in range(3):
            psv = self.rows_mm(lambda kc, j=j: self.hT[:, kc, TP:T].rearrange("p (s j) -> p s j", j=4)[:, :, 1 + j],
                               NS, wx_t, wx_k, lambda t, kc: w512(t)[:, kc, :], 512)
            self.store_rows(O['s_c_conv'][l, s0:s0 + NS, j, :], psv, NS, 512)
        if blk == NBLK - 1:
            psv = self.rows_mm(lambda kc: self.hT[:, kc, TP - 3:TP], 3, wx_t, wx_k, lambda t, kc: w512(t)[:, kc, :], 512)
            self.store_rows(O['p_c_conv'][l], psv, 3, 512)
        wg_t, wg_k = self.wnext(('C_g', blk, l))
        for c in range(4):
            X, Xk = (R[0], 'R0') if c % 2 == 0 else (R[6], 'R6')
            P.copy(V(X[:, 0:3], Xk), V(self.tailC[:, l, c, :], 'tailC'))
            P.copy(V(X[:, 3 + TP:3 + TP + NS * 7].rearrange("p (s j) -> p s j", j=7)[:, :, 0:3], Xk),
                   V(csm[:, 1, c * 24:(c + 1) * 24].rearrange("p (s j) -> p s j", j=3), ('csm', 1)))
            self.proj(wx_t, wx_k, lambda t, kc, c=c: w512(t)[:, kc, c * 128:(c + 1) * 128], 128,
                      self.evac_to_X(X, Xk, 3))
            P.copy(V(self.tailC[:, l, c, :], 'tailC'), V(X[:, TP:TP + 3], Xk))
            xc, xck = R[1], 'R1'
            self.conv4(xc, xck, X, Xk, 'c_conv_w', lambda j, c=c: (l * 4 + j) * 4 + c)
            P.ts(V(xc[:, 0:T], xck), V(xc[:, 0:T], xck), self.pcol('c_conv_b', l * 4 + c), None, ALU.add)
            xcb, xcbk = Bt[2], 'B2'
            P.copy(V(xcb[:, 0:T], xcbk), V(xc[:, 0:T], xck), eng='act')
            cl = V(csm[:, 2, 0:1], ('csm', 2))
            cl2 = V(csm[:, 2, 1:2], ('csm', 2))
            P.act(cl, self.pcol('c_L', l * 4 + c), AF.Exp, scale=-1.0)
            P.act(cl, cl, AF.Ln, bias=self.onec)
            P.ts(cl2, cl, -16.0, None, ALU.mult)
            P.ts(cl, cl, -8.0, None, ALU.mult)
            ra, rak = R[2], 'R2'
            ri, rik = R[3], 'R3'
            for ti, (t0, t1) in enumerate(TT):
                n = t1 - t0
                pn, ps = P.ps()
                P.mm(V(ps[:, 0:n], pn), V(self.cgate[:, c, 0, :], 'cgate'), V(xcb[:, t0:t1], xcbk))
                P.act(V(ra[:, t0:t1], rak), V(ps[:, 0:n], pn), AF.Sigmoid, bias=self.pcol('c_ba', l * 4 + c))
                pn, ps = P.ps()
                P.mm(V(ps[:, 0:n], pn), V(self.cgate[:, c, 1, :], 'cgate'), V(xcb[:, t0:t1], xcbk))
                P.act(V(ri[:, t0:t1], rik), V(ps[:, 0:n], pn), AF.Sigmoid, bias=self.pcol('c_bx', l * 4 + c))
            s_, sk = R[4], 'R4'
            P.act(V(s_[:, 0:T], sk), V(ra[:, 0:T], rak), AF.Exp, scale=cl2)
            P.act(V(s_[:, 0:T], sk), V(s_[:, 0:T], sk), AF.Sqrt, scale=-1.0, bias=self.onec)
            P.act(V(ra[:, 0:T], rak), V(ra[:, 0:T], rak), AF.Exp, scale=cl)
            P.tt(V(ri[:, 0:T], rik), V(ri[:, 0:T], rik), V(xc[:, 0:T], xck), ALU.mult)
            P.tt(V(s_[:, 0:T], sk), V(s_[:, 0:T], sk), V(ri[:, 0:T], rik), ALU.mult)
            a_, ak = ra, rak
            a_s = V(a_[:, TP:T].rearrange("p (s j) -> p s j", j=4)[:, :, 0], ak)
            b_s = V(s_[:, TP:T].rearrange("p (s j) -> p s j", j=4)[:, :, 0], sk)
            h0s = V(csm[:, 0, c * 8:(c + 1) * 8], ('csm', 0))
            tmp8 = V(csm[:, 2, 8:16], ('csm', 2))
            P.tt(tmp8, a_s, h0s, ALU.mult)
            P.tt(b_s, b_s, tmp8, ALU.add)
            P.memset(a_s, 0.0, eng='dve')
            hh, hk = R[5], 'R5'
            P.scan(V(hh[:, 0:T], hk), V(a_[:, 0:T], ak), V(s_[:, 0:T], sk), V(self.hC[:, l, c:c + 1], 'hC'),
                   ALU.mult, ALU.add)
            P.copy(V(self.hC[:, l, c:c + 1], 'hC'), V(hh[:, TP - 1:TP], hk))
            P.copy(V(csm[:, 3, c * 8:(c + 1) * 8], ('csm', 3)),
                   V(hh[:, TP:T].rearrange("p (s j) -> p s j", j=4)[:, :, 3], hk))

            pss = []
            for ti, (t0, t1) in enumerate(TT):
                n = t1 - t0
                pn, ps = P.ps()
                for kc in range(KC):
                    P.mm(V(ps[:, 0:n], pn), V(w512(wg_t)[:, kc, c * 128:(c + 1) * 128], wg_k),
                         V(self.hT[:, kc, t0:t1], ('hT', ti)), start=(kc == 0), stop=(kc == KC - 1))
                pss.append(V(ps[:, 0:n], pn))

            def gg(ti, t0, t1, n, c=c, pss=pss):
                psv = pss[ti]
                tA = V(self.small[:, ti, 0:n], ('small', ti))
                P.act(tA, psv, AF.Square)
                yield
                P.ts(tA, tA, 0.044715, 1.0, ALU.mult, ALU.add)
                P.tt(tA, psv, tA, ALU.mult)
                yield
                P.act(tA, tA, AF.Sigmoid, scale=1.5957691216)
                yield
                P.tt(tA, V(hh[:, t0:t1], hk), tA, ALU.mult)
                P.tt(V(self.yb[:, 8 + c, t0:t1], ('yb', 8 + c, ti)), psv, tA, ALU.mult)
                yield
            self.ti_pipe(gg)
        pn, ps = P.ps()
        for c in range(4):
            P.tr(V(ps[0:NS, c * 128:(c + 1) * 128], pn), V(csm[:, 3, c * 8:(c + 1) * 8], ('csm', 3)),
                 V(self.identf[:], 'identf'))
        self.store_rows(O['s_c_h'][l, s0:s0 + NS, :], V(ps[0:NS, 0:512], pn), NS, 512)
        if blk == NBLK - 1:
            pn, ps = P.ps()
            P.tr(V(ps[0:4, 0:128], pn), V(self.hC[:, l, :], 'hC'), V(self.identf[:], 'identf'))
            self.store_rows(O['p_c_h'][l].rearrange("(c p) -> c p", p=128), V(ps[0:4, 0:128], pn), 4, 128)

    def branch_A(self, blk, l):
        P = self.P
        I, O = self.I, self.O
        R, Bt = self.R, self.Bt
        s0 = blk * NS
        identf = V(self.identf[:], 'identf')
        wb_t, wb_k = self.wnext(('A_ba', blk, l))
        wba = lambda t: t[:, 0:64].rearrange("p (kc n) -> p kc n", kc=8)
        rows, rowsk = R[1], 'R1'
        negA = V(self.a4[:, l, 0:1], 'a4')
        dtb = V(self.a4[:, l, 1:2], 'a4')
        one4 = V(self.onec.ap[0:4, :], 'consts')
        colsA = self.colsA

        self.proj(wb_t, wb_k, lambda t, kc: wba(t)[:, kc, 0:4], 4,
                  lambda ti, t0, t1, psv: P.act(V(rows[0:4, t0:t1], rowsk), psv, AF.Sigmoid))
        for c in range(NCHUNK):
            t0, n, nseq = chunk_info(c)
            pn, ps = P.ps()
            P.tr(V(ps[0:n, 0:4], pn), V(rows[0:4, t0:t0 + n], rowsk), V(self.identf[0:4, 0:4], 'identf'))
            P.copy(V(colsA[0:n, c, 0:4], ('colsA', c)), V(ps[0:n, 0:4], pn))

        def ev_g(ti, t0, t1, psv):
            gv = V(rows[0:4, t0:t1], rowsk)
            P.act(gv, psv, AF.Exp, bias=dtb)
            P.act(gv, gv, AF.Ln, bias=one4)
            P.ts(gv, gv, negA, None, ALU.mult)
        self.proj(wb_t, wb_k, lambda t, kc: wba(t)[:, kc, 4:8], 4, ev_g)
        for c in range(NCHUNK):
            t0, n, nseq = chunk_info(c)
            mi = 0 if c < 8 else 1
            ck = ('colsA', c)
            pn, ps = P.ps()
            P.tr(V(ps[0:n, 0:4], pn), V(rows[0:4, t0:t0 + n], rowsk), V(self.identf[0:4, 0:4], 'identf'))
            P.copy(V(colsA[0:n, c, 4:8], ck), V(ps[0:n, 0:4], pn))
            pn, ps = P.ps()
            P.mm(V(ps[0:n, 0:4], pn), V(self.m_uincl[0:n, mi, 0:n], 'm_uincl'), V(colsA[0:n, c, 4:8], ck))
            P.mm(V(ps[0:n, 4:8], pn), V(self.m_lstr[0:n, mi, 0:n], 'm_lstr'), V(colsA[0:n, c, 4:8], ck))
            P.copy(V(colsA[0:n, c, 8:16], ck), V(ps[0:n, 0:8], pn))
            P.act(V(colsA[0:n, c, 16:24], ck), V(colsA[0:n, c, 8:16], ck), AF.Exp)
            P.tt(V(colsA[0:n, c, 16:20], ck), V(colsA[0:n, c, 16:20], ck), V(colsA[0:n, c, 0:4], ck), ALU.mult)
            P.ts(V(colsA[0:n, c, 4:8], ck), V(colsA[0:n, c, 0:4], ck), -1.0, None, ALU.mult)

        self.astop(1)
        for hd in range(4):
            wt, wk = self.wnext(('A_head', blk, l, hd))
            wv = lambda t: t[:, :].rearrange("p (kc c n) -> p kc c n", kc=8, c=4)
            stg = self.stage[0]
            for cc in range(3):
                P.dma(V(stg[0:24, cc * 128:(cc + 1) * 128], 'stg0'),
                      I['sa_conv'][l, s0:s0 + NS].rearrange("s j c -> (s j) c")[:, cc * 512 + hd * 128:cc * 512 + (hd + 1) * 128],
                      'stg0')
            pnh, psh = P.ps()
            for cc in range(3):
                P.tr(V(psh[:, cc * 24:(cc + 1) * 24], pnh), V(stg[0:24, cc * 128:(cc + 1) * 128], 'stg0'),
                     V(self.identf[0:24, 0:24], 'identf'))
            hist = V(self.csm[:, 0, 0:72], ('csm', 0))
            P.copy(hist, V(psh[:, 0:72], pnh))
            Cs = [(R[3], 'R3'), (R[4], 'R4'), (R[5], 'R5')]
            for cc in range(3):
                X, Xk = (R[0], 'R0') if cc % 2 == 0 else (R[2], 'R2')
                P.copy(V(X[:, 3 + TP:3 + TP + NS * 7].rearrange("p (s j) -> p s j", j=7)[:, :, 0:3], Xk),
                       V(self.csm[:, 0, cc * 24:(cc + 1) * 24].rearrange("p (s j) -> p s j", j=3), ('csm', 0)))
                P.copy(V(X[:, 0:3], Xk), V(self.tailA[:, l, cc * 4 + hd, :], 'tailA'))
                self.proj(wt, wk, lambda t, kc, cc=cc: wv(t)[:, kc, cc, :], 128, self.evac_to_X(X, Xk, 3))
                P.copy(V(self.tailA[:, l, cc * 4 + hd, :], 'tailA'), V(X[:, TP:TP + 3], Xk))
                Cc, Ck = Cs[cc]
                self.conv4(Cc, Ck, X, Xk, 'a_conv_w', lambda j, cc=cc: (l * 4 + j) * 12 + cc * 4 + hd)
                P.act(V(Cc[:, 0:T], Ck), V(Cc[:, 0:T], Ck), AF.Silu)
            self.astop(2)
            for j in range(3):
                psv = self.rows_mm(lambda kc, j=j: self.hT[:, kc, TP:T].rearrange("p (s j) -> p s j", j=4)[:, :, 1 + j],
                                   NS, wt, wk, lambda t, kc: t[:, kc * 512:kc * 512 + 384], 384)
                P.copy(V(stg[0:NS, 0:384], 'stg0'), psv, eng='act')
                for cc in range(3):
                    P.dma(O['s_a_conv'][l, s0:s0 + NS, j, cc * 512 + hd * 128:cc * 512 + (hd + 1) * 128],
                          V(stg[0:NS, cc * 128:(cc + 1) * 128], 'stg0'), 'stg0')
            if blk == NBLK - 1:
                psv = self.rows_mm(lambda kc: self.hT[:, kc, TP - 3:TP], 3, wt, wk,
                                   lambda t, kc: t[:, kc * 512:kc * 512 + 384], 384)
                P.copy(V(stg[0:3, 0:384], 'stg0'), psv, eng='act')
                for cc in range(3):
                    P.dma(O['p_a_conv'][l, :, cc * 512 + hd * 128:cc * 512 + (hd + 1) * 128],
                          V(stg[0:3, cc * 128:(cc + 1) * 128], 'stg0'), 'stg0')
            self.astop(3)
            zg, zgk = R[6], 'R6'
            self.proj(wt, wk, lambda t, kc: wv(t)[:, kc, 3, :], 128,
                      lambda ti, t0, t1, psv: P.act(V(zg[:, t0:t1], zgk), psv, AF.Silu))
            qn, qnk = Bt[0], 'B0'
            kn, knk = Bt[1], 'B1'
            knf, knfk = Cs[1]
            vf, vfk = Cs[2]
            for which, (Cc, Ck) in enumerate(Cs[0:2]):
                def g(ti, t0, t1, n, which=which, Cc=Cc, Ck=Ck):
                    sq = V(self.sqts[ti][:, 0:n], self.sqtk[ti])
                    P.act(sq, V(Cc[:, t0:t1], (Ck, ti)), AF.Square)
                    yield
                    pn, ps = P.ps()
                    P.mm(V(ps[:, 0:n], pn), V(self.onesb[:], 'onesb'), sq)
                    yield
                    rs = V(self.small[:, ti, 0:n], ('small', ti))
                    P.act(rs, V(ps[:, 0:n], pn), AF.Ln, bias=self.epsc)
                    P.act(rs, rs, AF.Exp, scale=-0.5)
                    yield
                    if which == 0:
                        P.stt(V(qn[:, t0:t1], (qnk, ti)), V(Cc[:, t0:t1], (Ck, ti)), 128.0 ** -0.5, rs, ALU.mult, ALU.mult)
                    else:
                        P.tt(V(Cc[:, t0:t1], (Ck, ti)), V(Cc[:, t0:t1], (Ck, ti)), rs, ALU.mult)
                        P.copy(V(kn[:, t0:t1], (knk, ti)), V(Cc[:, t0:t1], (Ck, ti)), eng='act')
                    yield
                self.ti_pipe(g)
            lc, lck = R[7], 'R7'
            for ti, (t0, t1) in enumerate(TT):
                n = t1 - t0
                gmv = V(self.small[0:4, 2, 0:n], ('small', 2))
                P.ts(gmv, V(rows[0:4, t0:t1], rowsk), V(self.identf[0:4, hd:hd + 1], 'identf'), None, ALU.mult)
                pn, ps = P.ps()
                P.mm(V(ps[:, 0:n], pn), V(self.onesf[0:4, :], 'onesf'), gmv)
                P.copy(V(lc[:, t0:t1], lck), V(ps[:, 0:n], pn), eng='act')
            P.scan(V(lc[:, 0:T], lck), V(self.rmask[:, 0:T], 'rmask'), V(lc[:, 0:T], lck), 0.0, ALU.mult, ALU.add)
            qg, qgk = Bt[2], 'B2'
            for ti, (t0, t1) in enumerate(TT):
                n = t1 - t0
                ev = V(self.small[:, 3, 0:n], ('small', 3))
                P.act(ev, V(lc[:, t0:t1], lck), AF.Exp)
                P.tt(V(qg[:, t0:t1], qgk), V(qn[:, t0:t1], qnk), ev, ALU.mult)
            P.act(V(self.glc[:, 0:8], 'glc'), V(lc[:, 0:TP].rearrange("p (c n) -> p c n", n=128)[:, :, 127], lck), AF.Exp)
            P.act(V(self.glc[:, 8:16], 'glc'), V(lc[:, TP:T].rearrange("p (s j) -> p s j", j=4)[:, :, 3], lck), AF.Exp)
            self.astop(4)
            rw, rwk = Bt[4], 'B4'
            kd, kdk = Bt[5], 'B5'
            rv, rvk = Bt[6], 'B6'
            aT, aTk = Bt[7], 'B7'
            nw, nwk = Bt[8], 'B8'
            Xb, Xbk = Bt[9], 'B9'
            for c in range(NCHUNK):
                t0, n, nseq = chunk_info(c)
                co = c * 128
                ck = ('colsA', c)
                pn, ps = P.ps()
                P.tr(V(ps[0:n, 0:128], pn), V(knf[:, t0:t0 + n], knfk), identf)
                P.tr(V(ps[0:n, 128:256], pn), V(vf[:, t0:t0 + n], vfk), identf)
                P.ts(V(rw[0:n, co:co + 128], rwk), V(ps[0:n, 0:128], pn), V(colsA[0:n, c, 16 + hd:17 + hd], ck), None, ALU.mult)
                P.act(V(kd[0:n, co:co + 128], kdk), V(ps[0:n, 0:128], pn), AF.Identity, scale=V(colsA[0:n, c, 20 + hd:21 + hd], ck))
                P.act(V(rv[0:n, co:co + 128], rvk), V(ps[0:n, 128:256], pn), AF.Identity, scale=V(colsA[0:n, c, hd:hd + 1], ck))
            self.astop(5)
            nsets = [(R[0][:, 0:512], 'R0', R[0][:, 512:1024], 'R0', self.c128f[0], 'cf0', self.c128b[2], 'cb2',
                      self.c128f[1], 'cf1', self.c128f[2], 'cf2'),
                     (R[2][:, 0:512], 'R2', R[2][:, 512:1024], 'R2', R[3][:, 512:1024], ('R3', 1), self.c128b[0], 'cb0',
                      self.c128f[3], 'cf3', self.c128f[4], 'cf4')]
            def gen_GN(cs, G, ns):
                Nb, Nbk, NTb, NTbk, X32, X32k, XbT, XbTk, d1t, d1k, d2t, d2k = ns
                n = 128 if G == 4 else TS
                mi = 0 if G == 4 else 1
                for gi, c in enumerate(cs):
                    t0 = chunk_info(c)[0]
                    ck = ('colsA', c)
                    sl = slice(gi * n, (gi + 1) * n)
                    pn, ps = P.ps()
                    P.mm(V(ps[0:n, 0:n], (pn, 0)), V(kn[:, t0:t0 + n], knk), V(kn[:, t0:t0 + n], knk))
                    P.mm(V(ps[0:n, 128:128 + n], (pn, 1)), V(kn[:, t0:t0 + n], knk), V(qn[:, t0:t0 + n], qnk))
                    d1 = V(d1t[0:n, 0:n], d1k)
                    P.stt(d1, V(lc[0:n, t0:t0 + n], lck), V(colsA[0:n, c, 8 + hd:9 + hd], ck),
                          V(self.m_bigL[0:n, mi, 0:n], 'm_bigL'), ALU.subtract, ALU.max)
                    P.act(d1, d1, AF.Exp, scale=-1.0)
                    P.stt(V(Nb[0:n, sl], Nbk), V(ps[0:n, 0:n], (pn, 0)), V(colsA[0:n, c, 4 + hd:5 + hd], ck), d1,
                          ALU.mult, ALU.mult)
                    d2 = V(d2t[0:n, 0:n], d2k)
                    P.stt(d2, V(lc[0:n, t0:t0 + n], lck), V(colsA[0:n, c, 8 + hd:9 + hd], ck),
                          V(self.m_negU[0:n, mi, 0:n], 'm_negU'), ALU.subtract, ALU.min)
                    P.act(d2, d2, AF.Exp)
                    P.tt(V(aT[0:n, c * 128:c * 128 + n], (aTk, c)), V(ps[0:n, 128:128 + n], (pn, 1)), d2, ALU.mult)
                    yield
                pn, ps = P.ps()
                for gi in range(G):
                    sl = slice(gi * n, (gi + 1) * n)
                    P.tr(V(ps[0:n, sl], pn), V(Nb[0:n, sl], Nbk), V(self.identf[0:n, 0:n], 'identf'))
                P.copy(V(NTb[0:n, 0:G * n], NTbk), V(ps[0:n, 0:G * n], pn))
                yield
                yield from self.neumann(Nb, NTb, n, G, NEU_LEVELS_P if G == 4 else NEU_LEVELS_S, X32, XbT,
                                        (Nbk, NTbk, X32k, XbTk), fp32=True)
                for gi, c in enumerate(cs):
                    sl = slice(gi * n, (gi + 1) * n)
                    P.copy(V(Xb[0:n, c * 128:c * 128 + n], (Xbk, c)), V(XbT[0:n, sl], XbTk), eng='dve')
                pn, ps = P.ps()
                for gi, c in enumerate(cs):
                    P.mm(V(ps[:, gi * n:(gi + 1) * n], pn), V(rw[0:n, c * 128:(c + 1) * 128], rwk),
                         V(XbT[0:n, gi * n:(gi + 1) * n], XbTk))
                t0 = chunk_info(cs[0])[0]
                P.act(V(nw[:, t0:t0 + G * n], *[(nwk, c_) for c_ in cs]), V(ps[:, 0:G * n], pn), AF.Identity, scale=-1.0)
                yield
            self.astop(6)
            oT, oTk = R[3], 'R3'
            S32 = V(self.SA[:, l, hd, :], ('SA', l, hd))
            Sb = V(self.SAb[:, hd, :], ('SAb', hd))
            P.copy(Sb, S32, eng='act')
            vn, vnk = self.c128b[3], 'cb3'
            def gen_chain(c_list):
              for c in c_list:
                t0 = c * 128
                co = c * 128
                pn1, ps1 = P.ps()
                pv = V(ps1[:, 0:128], pn1)
                P.mm(pv, V(Xb[:, co:co + 128], (Xbk, c)), V(rv[:, co:co + 128], rvk), start=True, stop=False)
                P.mm(pv, V(nw[:, t0:t0 + 128], (nwk, c)), Sb, start=False, stop=True)
                vnv = V(vn[:, (c % 2) * 128:(c % 2 + 1) * 128], (vnk, c % 2))
                P.copy(vnv, pv, eng='act')
                yield
                pn2, ps2 = P.ps()
                po = V(ps2[:, 0:128], pn2)
                P.mm(po, Sb, V(qg[:, t0:t0 + 128], qgk), start=True, stop=False)
                P.mm(po, vnv, V(aT[:, co:co + 128], (aTk, c)), start=False, stop=True)
                P.copy(V(oT[:, t0:t0 + 128], (oTk, c // 4)), po, eng='dve')
                yield
                pn3, ps3 = P.ps()
                pS = V(ps3[:, 0:128], pn3)
                P.mm(pS, V(kd[:, co:co + 128], kdk), vnv)
                P.stt(Sb, S32, V(self.glc[:, c:c + 1], 'glc'), pS, ALU.mult, ALU.add)
                P.stt(S32, S32, V(self.glc[:, c:c + 1], 'glc'), pS, ALU.mult, ALU.add)
                yield
            self.astop(61)
            pipe = Pipe()
            g0 = gen_GN((0, 1, 2, 3), 4, nsets[0])
            g1 = gen_GN((4, 5, 6, 7), 4, nsets[1])
            pipe.add(g0)
            pipe.add(g1)
            pipe.finish(g0)
            self.astop(62)
            pipe.run_with(gen_chain([0, 1, 2, 3]))
            pipe.finish(g1)
            self.astop(63)
            pipe.add(gen_GN((8,), 1, nsets[0]))
            pipe.run_with(gen_chain([4, 5, 6, 7]))
            pipe.drain_all()
            if blk == NBLK - 1:
                P.dma(O['p_a_S'][l, hd], S32, ('SA', l, hd))
            self.astop(7)
            c = 8
            t0 = TP
            co = c * 128
            Ss = V(self.Ss[:, :], 'Ss')
            ssb_t = R[2][:, 0:512].bitcast(BF16)
            Ssb = V(ssb_t, 'R2')
            P.dma(V(self.Ss[:, :].rearrange("p (s e) -> p s e", s=NS), 'Ss'),
                  I['sa_S'][l, s0:s0 + NS, hd].rearrange("s d e -> d s e"), 'Ss')
            P.copy(Ssb, Ss, eng='act')
            ex, exk = R[0], 'R0'
            red1 = V(self.c128f[3][0:TS, 0:128], 'cf3')
            red2 = V(self.c128f[4][0:TS, 0:128], 'cf4')

            def state_apply(lhsT_v, out_red):
                pn1, ps1 = P.ps()
                pn2, ps2 = P.ps()
                P.mm(V(ps1[0:TS, 0:512], pn1), lhsT_v, V(ssb_t[:, 0:512], 'R2'))
                P.mm(V(ps2[0:TS, 0:512], pn2), lhsT_v, V(ssb_t[:, 512:1024], 'R2'))
                P.tt(V(ex[0:TS, 0:512].rearrange("p (s e) -> p s e", s=4), exk),
                     V(ps1[0:TS, 0:512].rearrange("p (s e) -> p s e", s=4), pn1),
                     V(self.seqm[:, 0:4].unsqueeze(2).to_broadcast([TS, 4, 128]), 'seqm'), ALU.mult)
                P.tt(V(ex[0:TS, 512:1024].rearrange("p (s e) -> p s e", s=4), exk),
                     V(ps2[0:TS, 0:512].rearrange("p (s e) -> p s e", s=4), pn2),
                     V(self.seqm[:, 4:8].unsqueeze(2).to_broadcast([TS, 4, 128]), 'seqm'), ALU.mult)
                P.red(out_red, V(ex[0:TS, 0:1024].rearrange("p (s e) -> p e s", s=NS), exk))
            state_apply(V(nw[:, t0:t0 + TS], (nwk, 8)), red1)
            pn, ps = P.ps()
            P.mm(V(ps[0:TS, 0:128], pn), V(Xb[0:TS, co:co + TS], (Xbk, 8)), V(rv[0:TS, co:co + 128], rvk))
            vns32 = V(self.c128f[5][0:TS, 0:128], 'cf5')
            P.tt(vns32, V(ps[0:TS, 0:128], pn), red1, ALU.add)
            vns = V(vn[0:TS, 256:384], (vnk, 2))
            P.copy(vns, vns32, eng='act')
            state_apply(V(qg[:, t0:t0 + TS], qgk), red2)
            pn, ps = P.ps()
            P.mm(V(ps[0:TS, 0:128], pn), V(aT[0:TS, co:co + TS], (aTk, 8)), vns)
            P.tt(red2, V(ps[0:TS, 0:128], pn), red2, ALU.add)
            pn, ps = P.ps()
            P.tr(V(ps[:, 0:TS], pn), red2, V(self.identf[0:TS, 0:TS], 'identf'))
            P.copy(V(oT[:, t0:t0 + TS], oTk), V(ps[:, 0:TS], pn), eng='act')
            vex_t = R[0][0:TS, 0:512].bitcast(BF16)
            P.tt(V(vex_t.rearrange("p (s e) -> p s e", s=NS), exk),
                 V(vns32.ap.unsqueeze(1).to_broadcast([TS, NS, 128]), 'cf5'),
                 V(self.seqm[:, :].unsqueeze(2).to_broadcast([TS, NS, 128]), 'seqm'), ALU.mult)
            for hf in range(2):
                pn, ps = P.ps()
                P.mm(V(ps[:, 0:512], pn), V(kd[0:TS, co:co + 128], kdk), V(vex_t[:, hf * 512:(hf + 1) * 512], exk))
                ssl = V(self.Ss[:, hf * 512:(hf + 1) * 512].rearrange("p (s e) -> p s e", s=4), 'Ss')
                glb = V(self.glc[:, 8 + hf * 4:8 + (hf + 1) * 4].unsqueeze(2).to_broadcast([128, 4, 128]), 'glc')
                P.tt(ssl, ssl, glb, ALU.mult)
                P.tt(ssl, ssl, V(ps[:, 0:512].rearrange("p (s e) -> p s e", s=4), pn), ALU.add)
            P.dma(O['s_a_S'][l, s0:s0 + NS, hd].rearrange("s d e -> d s e"),
                  V(self.Ss[:, :].rearrange("p (s e) -> p s e", s=NS), 'Ss'), 'Ss')
            self.astop(8)
            def g8(ti, t0, t1, n):
                sq = V(self.sqts[ti][:, 0:n], self.sqtk[ti])
                P.act(sq, V(oT[:, t0:t1], oTk), AF.Square)
                yield
                pn, ps = P.ps()
                P.mm(V(ps[:, 0:n], pn), V(self.onesb[:], 'onesb'), sq)
                yield
                rs = V(self.small[:, ti, 0:n], ('small', ti))
                P.act(rs, V(ps[:, 0:n], pn), AF.Ln, bias=self.epsc, scale=1.0 / 128)
                P.act(rs, rs, AF.Exp, scale=-0.5)
                yield
                tmp = V(self.small[:, 3, 0:n], ('small', 3))
                P.stt(tmp, V(oT[:, t0:t1], oTk), self.pcol('a_norm_g', l), rs, ALU.mult, ALU.mult)
                P.tt(V(self.yb[:, hd, t0:t1], ('yb', hd, ti)), tmp, V(zg[:, t0:t1], zgk), ALU.mult)
                yield
            self.ti_pipe(g8)

    def branch_B(self, blk, l):
        P = self.P
        I, O = self.I, self.O
        R, Bt = self.R, self.Bt
        s0 = blk * NS
        identf = V(self.identf[:], 'identf')
        csm = self.csm
        P.dma(V(self.lorW[0:64, :], 'lorW'), I['b_w_up'][l], 'lorW', eng='pool')
        P.dma(V(self.lorW[64:128, :], 'lorW'), I['b_a_up'][l], 'lorW', eng='pool')
        P.dma(V(self.gupW[:, :], 'gupW'), I['b_g_up'][l], 'gupW', eng='pool')
        for p in range(4):
            P.ts(V(self.rkbd[:, p, :], 'rkbd'), V(self.ones64f[:], 'ones64f'), self.pcol('b_r_k', l * 4 + p), None, ALU.mult)
        stg = self.stage[0]
        for pc in range(4):
            c0 = pc * 512
            w = min(512, 1792 - c0)
            P.dma(V(stg[0:NS, 0:w], 'stg0'), I['sb_shift'][l, s0:s0 + NS, 0, c0:c0 + w], 'stg0')
            for cc in range(w // 128):
                ch = pc * 4 + cc
                pn, ps = P.ps()
                P.tr(V(ps[:, 0:NS], pn), V(stg[0:NS, cc * 128:(cc + 1) * 128], 'stg0'), V(self.identf[0:NS, 0:NS], 'identf'))
                if ch < 12:
                    P.copy(V(csm[:, 0, ch * 8:(ch + 1) * 8], ('csm', 0)), V(ps[:, 0:NS], pn))
                else:
                    P.copy(V(csm[:, 1, (ch - 12) * 8:(ch - 11) * 8], ('csm', 1)), V(ps[:, 0:NS], pn))

        def hist_of(ch):
            if ch < 12:
                return V(csm[:, 0, ch * 8:(ch + 1) * 8], ('csm', 0))
            return V(csm[:, 1, (ch - 12) * 8:(ch - 11) * 8], ('csm', 1))

        ZB = 1 + TP
        zsel = [0]

        def shift_proj(wt, wk, colsel, ch, out_t, out_k):
            Z, Zk = (R[0], 'R0') if zsel[0] % 2 == 0 else (R[4], 'R4')
            zsel[0] += 1
            P.copy(V(Z[:, 0:1], Zk), V(self.tailB[:, l, ch:ch + 1], 'tailB'))
            P.copy(V(Z[:, ZB:ZB + NS * 5].rearrange("p (s j) -> p s j", j=5)[:, :, 0], Zk), hist_of(ch))
            self.proj(wt, wk, colsel, 128, self.evac_to_X(Z, Zk, 1))
            P.copy(V(self.tailB[:, l, ch:ch + 1], 'tailB'), V(Z[:, TP:TP + 1], Zk))
            mu = self.pcol('b_mu', l * 14 + ch)
            omu = V(csm[:, 3, (zsel[0] % 2):(zsel[0] % 2) + 1], ('csm', 3))
            P.ts(omu, mu, -1.0, 1.0, ALU.mult, ALU.add)
            P.act(V(out_t[:, 0:TP], out_k), V(Z[:, 1:1 + TP], Zk), AF.Identity, scale=omu)
            P.stt(V(out_t[:, 0:TP], out_k), V(Z[:, 0:TP], Zk), mu, V(out_t[:, 0:TP], out_k), ALU.mult, ALU.add)
            zs3 = Z[:, ZB:ZB + NS * 5].rearrange("p (s j) -> p s j", j=5)
            o3 = V(out_t[:, TP:T].rearrange("p (s j) -> p s j", j=4), out_k)
            P.tt(o3, V(zs3[:, :, 0:4], Zk), V(zs3[:, :, 1:5], Zk), ALU.subtract)
            P.stt(o3, o3, mu, V(zs3[:, :, 1:5], Zk), ALU.mult, ALU.add)

        wl_t, wl_k = self.wnext(('B_lora', blk, l))
        wl = lambda t: t[:, 0:2048].rearrange("p (kc n) -> p kc n", kc=8)
        lx, lxk = Bt[0], 'B0'
        sgx, sgxk = Bt[1], 'B1'
        tmpz, tmpzk = R[1], 'R1'
        shift_proj(wl_t, wl_k, lambda t, kc: wl(t)[:, kc, 0:128], 12, tmpz, tmpzk)
        P.act(V(lx[0:64, 0:T], lxk), V(tmpz[0:64, 0:T], tmpzk), AF.Tanh)
        P.copy(V(lx[64:128, 0:T], lxk), V(tmpz[64:128, 0:T], tmpzk), eng='act')
        shift_proj(wl_t, wl_k, lambda t, kc: wl(t)[:, kc, 128:256], 13, tmpz, tmpzk)
        P.act(V(sgx[:, 0:T], sgxk), V(tmpz[:, 0:T], tmpzk), AF.Sigmoid)
        psv = self.rows_mm(lambda kc: self.hT[:, kc, TP:T].rearrange("p (s j) -> p s j", j=4)[:, :, 3], NS, wl_t, wl_k,
                           lambda t, kc: wl(t)[:, kc, :], 256)
        self.store_rows(O['s_b_shift'][l, s0:s0 + NS, 0, 1536:1792], psv, NS, 256)
        if blk == NBLK - 1:
            psv = self.rows_mm(lambda kc: self.hT[:, kc, TP - 1:TP], 1, wl_t, wl_k, lambda t, kc: wl(t)[:, kc, :], 256)
            self.store_rows(O['p_b_shift'][l, :, 1536:1792], psv, 1, 256)

        cb = self.c128b
        cf = self.c128f
        for p in range(4):
            wt, wk = self.wnext(('B_pair', blk, l, p))
            wv = lambda t: t[:, 0:3072].rearrange("p (kc c n) -> p kc c n", kc=8, c=3)
            stS = R[5]
            for h_ in range(2):
                P.dma(V(stS[0:64, 0:1024].rearrange("p (s h k) -> p s h k", s=NS, h=2)[:, :, h_, :], 'R5'),
                      I['sb_S'][l, s0:s0 + NS, 2 * p + h_].rearrange("s v k -> v s k"), 'R5')
            pn, ps = P.ps()
            for sq_ in range(NS):
                P.tr(V(ps[:, sq_ * 64:(sq_ + 1) * 64], pn), V(stS[0:64, sq_ * 128:(sq_ + 1) * 128], 'R5'),
                     V(self.identf[0:64, 0:64], 'identf'))
            Hs = V(self.Ss[:, 0:512], 'Ss')
            P.copy(Hs, V(ps[:, 0:512], pn))
            Hsb_t = cb[7]
            Hsb = V(Hsb_t[:, 0:512], 'cb7')
            P.copy(Hsb, Hs, eng='act')
            if hasattr(self, 'marks'):
                self.marks.append(('  b-proj', P.cnt['pe'], P.cnt['act'], P.cnt['dve']))
            r32, r32k = R[1], 'R1'
            k32, k32k = R[2], 'R2'
            v32, v32k = R[3], 'R3'
            shift_proj(wt, wk, lambda t, kc: wv(t)[:, kc, 0, :], p, r32, r32k)
            shift_proj(wt, wk, lambda t, kc: wv(t)[:, kc, 1, :], 4 + p, k32, k32k)
            shift_proj(wt, wk, lambda t, kc: wv(t)[:, kc, 2, :], 8 + p, v32, v32k)
            psv = self.rows_mm(lambda kc: self.hT[:, kc, TP:T].rearrange("p (s j) -> p s j", j=4)[:, :, 3], NS, wt, wk,
                               lambda t, kc: t[:, kc * 384:(kc + 1) * 384], 384)
            P.copy(V(stg[0:NS, 0:384], 'stg0'), psv, eng='act')
            for cc in range(3):
                P.dma(O['s_b_shift'][l, s0:s0 + NS, 0, cc * 512 + p * 128:cc * 512 + (p + 1) * 128],
                      V(stg[0:NS, cc * 128:(cc + 1) * 128], 'stg0'), 'stg0')
            if blk == NBLK - 1:
                psv = self.rows_mm(lambda kc: self.hT[:, kc, TP - 1:TP], 1, wt, wk,
                                   lambda t, kc: t[:, kc * 384:(kc + 1) * 384], 384)
                P.copy(V(stg[0:1, 0:384], 'stg0'), psv, eng='act')
                for cc in range(3):
                    P.dma(O['p_b_shift'][l, :, cc * 512 + p * 128:cc * 512 + (p + 1) * 128],
                          V(stg[0:1, cc * 128:(cc + 1) * 128], 'stg0'), 'stg0')
            if hasattr(self, 'marks'):
                self.marks.append(('  b-lwag', P.cnt['pe'], P.cnt['act'], P.cnt['dve']))
            lw, lwk = R[4], 'R4'
            a32, a32k = R[5], 'R5'
            gb, gbk = Bt[9], 'B9'
            for ti, (t0, t1) in enumerate(TT):
                n = t1 - t0
                pn, ps = P.ps()
                P.mm(V(ps[:, 0:n], pn), V(self.lorW[0:64, p * 128:(p + 1) * 128], 'lorW'), V(lx[0:64, t0:t1], lxk))
                P.act(V(lw[:, t0:t1], lwk), V(ps[:, 0:n], pn), AF.Sigmoid, bias=self.pcol('b_w0', l * 4 + p))
                pn, ps = P.ps()
                P.mm(V(ps[:, 0:n], pn), V(self.lorW[64:128, p * 128:(p + 1) * 128], 'lorW'), V(lx[64:128, t0:t1], lxk))
                P.act(V(a32[:, t0:t1], a32k), V(ps[:, 0:n], pn), AF.Sigmoid, bias=self.pcol('b_a0', l * 4 + p))
                pn, ps = P.ps()
                P.mm(V(ps[:, 0:n], pn), V(self.gupW[:, p * 128:(p + 1) * 128], 'gupW'), V(sgx[:, t0:t1], sgxk))
                P.copy(V(gb[:, t0:t1], gbk), V(ps[:, 0:n], pn), eng='act')
            P.ts(V(lw[:, 0:T], lwk), V(lw[:, 0:T], lwk), -0.6065306597126334, None, ALU.mult)
            kkn, kknk = R[7], 'R7'
            kkc = self.pcol('b_k_k', l * 4 + p)
            def gk(ti, t0, t1, n):
                sq = V(self.sqts[ti][:, 0:n], self.sqtk[ti])
                P.act(sq, V(k32[:, t0:t1], k32k), AF.Square, scale=kkc)
                yield
                pn, ps = P.ps()
                P.mm(V(ps[:, 0:n], pn), V(self.ones64b[:], 'ones64b'), sq)
                yield
                rs = V(self.small[:, ti, 0:n], ('small', ti))
                P.act(rs, V(ps[:, 0:n], pn), AF.Ln, bias=self.epsc)
                P.act(rs, rs, AF.Exp, scale=-0.5)
                yield
                P.stt(V(kkn[:, t0:t1], (kknk, ti)), V(k32[:, t0:t1], k32k), kkc, rs, ALU.mult, ALU.mult)
                yield
            self.ti_pipe(gk)
            if hasattr(self, 'marks'):
                self.marks.append(('  b-elem', P.cnt['pe'], P.cnt['act'], P.cnt['dve']))
            kac = self.pcol('b_k_a', l * 4 + p)
            omk = V(csm[:, 2, 0:1], ('csm', 2))
            P.ts(omk, kac, -1.0, 1.0, ALU.mult, ALU.add)
            tmp, tmpk = R[6], 'R6'
            P.ts(V(tmp[:, 0:T], tmpk), V(a32[:, 0:T], a32k), kac, omk, ALU.mult, ALU.add)
            P.tt(V(k32[:, 0:T], k32k), V(k32[:, 0:T], k32k), V(tmp[:, 0:T], tmpk), ALU.mult)
            bon, bonk = R[6], 'R6'
            for ti, (t0, t1) in enumerate(TT):
                n = t1 - t0
                sq = V(self.sqt[:, 0:n], 'sqt')
                P.tt(sq, V(r32[:, t0:t1], r32k), V(k32[:, t0:t1], k32k), ALU.mult)
                pn, ps = P.ps()
                P.mm(V(ps[:, 0:n], pn), V(self.rkbd[:, p, :], 'rkbd'), sq)
                P.tt(V(bon[:, t0:t1], bonk), V(ps[:, 0:n], pn), V(v32[:, t0:t1], v32k), ALU.mult)
            lc, lck = R[0], 'R0'
            P.scan(V(lc[:, 0:T], lck), V(self.rmask[:, 0:T], 'rmask'), V(lw[:, 0:T], lwk), 0.0, ALU.mult, ALU.add)
            P.tt(V(lw[:, 0:T], lwk), V(lc[:, 0:T], lck), V(lw[:, 0:T], lwk), ALU.subtract)
            qt, qtk = Bt[2], 'B2'
            at, atk = Bt[3], 'B3'
            bt_, btk = Bt[4], 'B4'
            kt, ktk = Bt[5], 'B5'
            bh, bhk = R[1], 'R1'
            for ti, (t0, t1) in enumerate(TT):
                n = t1 - t0
                ev = V(self.small[:, 2, 0:n], ('small', 2))
                P.act(ev, V(lc[:, t0:t1], lck), AF.Exp)
                P.tt(V(qt[:, t0:t1], qtk), V(r32[:, t0:t1], r32k), ev, ALU.mult)
                ev2 = V(self.small[:, 3, 0:n], ('small', 3))
                P.act(ev2, V(lw[:, t0:t1], lwk), AF.Exp)
                P.stt(V(at[:, t0:t1], atk), V(kkn[:, t0:t1], kknk), -1.0, ev2, ALU.mult, ALU.mult)
            P.tt(V(bh[:, 0:T], bhk), V(kkn[:, 0:T], kknk), V(a32[:, 0:T], a32k), ALU.mult)
            for ti, (t0, t1) in enumerate(TT):
                n = t1 - t0
                ev = V(self.small[:, 2, 0:n], ('small', 2))
                P.act(ev, V(lc[:, t0:t1], lck), AF.Exp, scale=-1.0)
                P.tt(V(bt_[:, t0:t1], btk), V(bh[:, t0:t1], bhk), ev, ALU.mult)
                P.tt(V(kt[:, t0:t1], ktk), V(k32[:, t0:t1], k32k), ev, ALU.mult)
            P.act(V(self.glc[:, 0:8], 'glc'), V(lc[:, 0:TP].rearrange("p (c n) -> p c n", n=128)[:, :, 127], lck), AF.Exp)
            P.act(V(self.glc[:, 8:16], 'glc'), V(lc[:, TP:T].rearrange("p (s j) -> p s j", j=4)[:, :, 3], lck), AF.Exp)
            if hasattr(self, 'marks'):
                self.marks.append(('  b-tm', P.cnt['pe'], P.cnt['act'], P.cnt['dve']))
            bd, bdk = Bt[6], 'B6'
            kdt, kdtk = Bt[7], 'B7'
            vt, vtk = Bt[8], 'B8'
            for c in range(NCHUNK):
                t0, n, nseq = chunk_info(c)
                co = c * 128
                if c % 2 == 0:
                    edt, edk, f1t, f1k, f2t, f2k = cf[1], 'cf1', cf[2], 'cf2', cf[3], 'cf3'
                else:
                    edt, edk, f1t, f1k, f2t, f2k = cf[4], 'cf4', cf[5], 'cf5', cf[0], 'cf0'
                ed = V(edt[:, 0:n], edk)
                if c < 8:
                    P.act(ed, V(lc[:, t0:t0 + n], lck), AF.Exp, scale=-1.0, bias=V(lc[:, t0 + n - 1:t0 + n], lck))
                else:
                    lc3 = lc[:, TP:T].rearrange("p (s j) -> p s j", j=4)
                    P.tt(V(edt[:, 0:n].rearrange("p (s j) -> p s j", j=4), edk),
                         V(lc3[:, :, 3:4].to_broadcast([128, NS, 4]), lck), V(lc3, lck), ALU.subtract)
                    P.act(ed, ed, AF.Exp)
                f1 = V(f1t[:, 0:n], f1k)
                f2 = V(f2t[:, 0:n], f2k)
                P.tt(f1, V(bh[:, t0:t0 + n], bhk), ed, ALU.mult)
                P.tt(f2, V(k32[:, t0:t0 + n], k32k), ed, ALU.mult)
                pn, ps = P.ps()
                P.tr(V(ps[0:n, 0:128], pn), f1, identf)
                P.tr(V(ps[0:n, 128:256], pn), f2, identf)
                P.tr(V(ps[0:n, 256:384], pn), V(v32[:, t0:t0 + n], v32k), identf)
                P.copy(V(bd[0:n, co:co + 128], bdk), V(ps[0:n, 0:128], pn), eng='act')
                P.copy(V(kdt[0:n, co:co + 128], kdtk), V(ps[0:n, 128:256], pn), eng='act')
                P.copy(V(vt[0:n, co:co + 128], vtk), V(ps[0:n, 256:384], pn), eng='act')
            if hasattr(self, 'marks'):
                self.marks.append(('  b-groups', P.cnt['pe'], P.cnt['act'], P.cnt['dve']))
            oT, oTk = R[4], 'R4'
            H32 = V(self.HB[:, l, p, :], ('HB', l, p))
            Hb = V(self.HBb[:, p, :], ('HBb', p))
            P.copy(Hb, H32, eng='act')
            xc_t, xck_ = cb[6], 'cb6'
            def gen_GN(cs, bs, nsb):
                XbT, XbTk, LkT, LkTk, AqbT, AqbTk, AqkT, AqkTk = bs
                Nb, Nbk, NTb, NTbk, X32, X32k = nsb
                n = 128 if len(cs) == 2 else TS
                mi = 0 if len(cs) == 2 else 1
                G = 2 * len(cs)
                for (gi_type, (Lt, Ltk, Rt, Rtk, mask, mk, dst, dstk)) in enumerate([
                        (at, atk, bt_, btk, self.m_lstr, 'm_lstr', Nb, Nbk),
                        (kt, ktk, at, atk, self.m_ustr, 'm_ustr', LkT, LkTk),
                        (bt_, btk, qt, qtk, self.m_uincl, 'm_uincl', AqbT, AqbTk),
                        (kt, ktk, qt, qtk, self.m_uincl, 'm_uincl', AqkT, AqkTk)]):
                    for hp in range(2):
                        pn, ps = P.ps()
                        hs = slice(hp * 64, (hp + 1) * 64)
                        for ci, c in enumerate(cs):
                            t0 = chunk_info(c)[0]
                            P.mm(V(ps[0:n, ci * n:(ci + 1) * n], pn), V(Lt[hs, t0:t0 + n], Ltk), V(Rt[hs, t0:t0 + n], Rtk))
                        nci = len(cs)
                        dv = dst[0:n, 0:G * n].rearrange("p (ci hp n) -> p ci hp n", ci=nci, hp=2)[:, :, hp, :]
                        P.tt(V(dv, dstk), V(ps[0:n, 0:nci * n].rearrange("p (ci n) -> p ci n", ci=nci), pn),
                             V(mask[0:n, mi, 0:n].unsqueeze(1).to_broadcast([n, nci, n]), mk), ALU.mult)
                        yield
                pn, ps = P.ps()
                psb = ps[:, :].bitcast(BF16)
                for g in range(G):
                    sl = slice(g * n, (g + 1) * n)
                    P.tr(V(psb[0:n, sl], pn), V(Nb[0:n, sl], Nbk), V(self.identb[0:n, 0:n], 'identb'))
                P.copy(V(NTb[0:n, 0:G * n], NTbk), V(psb[0:n, 0:G * n], pn))
                yield
                yield from self.neumann(Nb, NTb, n, G, NEU_LEVELS_P if mi == 0 else NEU_LEVELS_S, X32, XbT, (Nbk, NTbk, X32k, XbTk))
            def gen_chain(cs, bs):
                XbT, XbTk, LkT, LkTk, AqbT, AqbTk, AqkT, AqkTk = bs
                for ci, c in enumerate(cs):
                    t0 = chunk_info(c)[0]
                    co = c * 128
                    if c < 8:
                        xcv = V(xc_t[:, 0:128], xck_)
                        for hp in range(2):
                            g = ci * 2 + hp
                            hs = slice(hp * 64, (hp + 1) * 64)
                            pn, ps = P.ps()
                            P.mm(V(ps[:, 0:64], pn), V(at[hs, t0:t0 + 128], atk), V(self.HBb[hs, p, :], ('HBb', p)),
                                 start=True, stop=False)
                            P.mm(V(ps[:, 0:64], pn), V(LkT[:, g * 128:(g + 1) * 128], LkTk), V(vt[:, co + hp * 64:co + (hp + 1) * 64], vtk),
                                 start=False, stop=True)
                            P.copy(V(xc_t[:, hp * 64:(hp + 1) * 64], xck_), V(ps[:, 0:64], pn), eng='act' if hp else 'dve')
                        yield
                        pn, ps = P.ps()
                        for hp in range(2):
                            g = ci * 2 + hp
                            P.mm(V(ps[:, hp * 64:(hp + 1) * 64], pn), V(XbT[:, g * 128:(g + 1) * 128], XbTk),
                                 V(xc_t[:, hp * 64:(hp + 1) * 64], xck_))
                        uv = V(xc_t[:, 128:256], xck_)
                        P.copy(uv, V(ps[:, 0:128], pn), eng='act')
                        yield
                        for hp in range(2):
                            g = ci * 2 + hp
                            hs = slice(hp * 64, (hp + 1) * 64)
                            pn, ps = P.ps()
                            po = V(ps[hs, 0:128], pn)
                            P.mm(po, V(self.HBb[hs, p, :], ('HBb', p)), V(qt[hs, t0:t0 + 128], qtk), start=True, stop=False)
                            P.mm(po, V(xc_t[:, 128 + hp * 64:128 + (hp + 1) * 64], xck_), V(AqbT[:, g * 128:(g + 1) * 128], AqbTk),
                                 start=False, stop=False)
                            P.mm(po, V(vt[:, co + hp * 64:co + (hp + 1) * 64], vtk), V(AqkT[:, g * 128:(g + 1) * 128], AqkTk),
                                 start=False, stop=True)
                            P.copy(V(oT[hs, t0:t0 + 128], oTk), po, eng='dve' if hp else 'act')
                        yield
                        pn, ps = P.ps()
                        for hp in range(2):
                            hs = slice(hp * 64, (hp + 1) * 64)
                            ph = V(ps[hs, 0:64], pn)
                            P.mm(ph, V(bd[:, co + hp * 64:co + (hp + 1) * 64], bdk), V(xc_t[:, 128 + hp * 64:128 + (hp + 1) * 64], xck_),
                                 start=True, stop=False)
                            P.mm(ph, V(kdt[:, co + hp * 64:co + (hp + 1) * 64], kdtk), V(vt[:, co + hp * 64:co + (hp + 1) * 64], vtk),
                                 start=False, stop=True)
                        P.stt(Hb, H32, V(self.glc[:, c:c + 1], 'glc'), V(ps[:, 0:64], pn), ALU.mult, ALU.add)
                        P.stt(H32, H32, V(self.glc[:, c:c + 1], 'glc'), V(ps[:, 0:64], pn), ALU.mult, ALU.add)
                        yield
                    else:
                        self.b_sample_chunk(p, l, blk, at, atk, qt, qtk, LkT, LkTk, AqbT, AqbTk, AqkT, AqkTk, XbT, XbTk,
                                            bd, bdk, kdt, kdtk, vt, vtk, oT, oTk, Hsb_t)
            r5b = R[5][:, 0:1024].bitcast(BF16)
            r7b = R[7][:, 0:1024].bitcast(BF16)
            r0b = R[0][:, 0:512].bitcast(BF16)
            bsets = [(cb[2], 'cb2', cb[3], 'cb3', cb[4], 'cb4', cb[5], 'cb5'),
                     (r5b[:, 0:512], ('R5', 0), r5b[:, 512:1024], ('R5', 1), r5b[:, 1024:1536], ('R5', 2),
                      r5b[:, 1536:2048], ('R5', 3)),
                     (r7b[:, 0:512], ('R7', 0), r7b[:, 512:1024], ('R7', 1), r7b[:, 1024:1536], ('R7', 2),
                      r7b[:, 1536:2048], ('R7', 3))]
            nsets = [(cb[0], 'cb0', cb[1], 'cb1', cf[0], 'cf0'),
                     (r0b[:, 0:512], ('R0', 0), r0b[:, 512:1024], ('R0', 1), R[0][:, 512:1024], ('R0', 2))]
            groups = [(0, 1), (2, 3), (4, 5), (6, 7), (8,)]
            pipe = Pipe()
            gens = {}

            def start(k):
                gens[k] = gen_GN(groups[k], bsets[k % 3], nsets[k % 2])
                pipe.add(gens[k])
            start(0)
            start(1)
            for gi_ in range(len(groups)):
                pipe.finish(gens[gi_])
                if gi_ + 2 < len(groups):
                    start(gi_ + 2)
                pipe.run_with(gen_chain(groups[gi_], bsets[gi_ % 3]))
            pipe.drain_all()
            if blk == NBLK - 1:
                pn, ps = P.ps()
                P.tr(V(ps[0:64, 0:128], pn), H32, identf)
                P.copy(V(cf[4][0:64, 0:128], 'cf4'), V(ps[0:64, 0:128], pn))
                P.dma(O['p_b_S'][l, 2 * p:2 * p + 2].rearrange("h v k -> v h k"),
                      V(cf[4][0:64, 0:128].rearrange("p (h k) -> p h k", h=2), 'cf4'), 'cf4')
            if hasattr(self, 'marks'):
                self.marks.append(('  b-post', P.cnt['pe'], P.cnt['act'], P.cnt['dve']))
            stS = R[5]
            for hf in range(2):
                pn, ps = P.ps()
                for sq_ in range(4):
                    s_ = hf * 4 + sq_
                    P.tr(V(ps[0:64, sq_ * 128:(sq_ + 1) * 128], pn), V(self.Ss[:, s_ * 64:(s_ + 1) * 64], 'Ss'), identf)
                P.copy(V(stS[0:64, hf * 512:(hf + 1) * 512], 'R5'), V(ps[0:64, 0:512], pn), eng='act' if hf else 'dve')
            for h_ in range(2):
                P.dma(O['s_b_S'][l, s0:s0 + NS, 2 * p + h_].rearrange("s v k -> v s k"),
                      V(stS[0:64, 0:1024].rearrange("p (s h k) -> p s h k", s=NS, h=2)[:, :, h_, :], 'R5'), 'R5')
            for ti, (t0, t1) in enumerate(TT):
                n = t1 - t0
                tA = V(self.small[:, 2, 0:n], ('small', 2))
                tB = V(self.small[:, 3, 0:n], ('small', 3))
                P.act(tA, V(oT[:, t0:t1], oTk), AF.Square)
                pn1, ps1 = P.ps()
                P.mm(V(ps1[:, 0:n], pn1), V(self.ones64f[:], 'ones64f'), V(oT[:, t0:t1], oTk))
                pn2, ps2 = P.ps()
                P.mm(V(ps2[:, 0:n], pn2), V(self.ones64f[:], 'ones64f'), tA)
                P.act(tA, V(ps1[:, 0:n], pn1), AF.Identity, scale=1.0 / 64)
                P.tt(tB, tA, tA, ALU.mult)
                P.stt(tB, V(ps2[:, 0:n], pn2), 1.0 / 64, tB, ALU.mult, ALU.subtract)
                P.act(tB, tB, AF.Ln, bias=self.lnepsc)
                P.act(tB, tB, AF.Exp, scale=-0.5)
                ov = V(oT[:, t0:t1], oTk)
                P.tt(ov, ov, tA, ALU.subtract)
                P.tt(ov, ov, tB, ALU.mult)
                P.ts(ov, ov, self.pcol('b_ln_w', l * 4 + p), self.pcol('b_ln_b', l * 4 + p), ALU.mult, ALU.add)
                P.tt(ov, ov, V(bon[:, t0:t1], bonk), ALU.add)
                P.tt(V(self.yb[:, 4 + p, t0:t1], ('yb', 4 + p)), ov, V(gb[:, t0:t1], gbk), ALU.mult)

    def b_sample_chunk(self, p, l, blk, at, atk, qt, qtk, LkT, LkTk, AqbT, AqbTk, AqkT, AqkTk, XbT, XbTk,
                       bd, bdk, kdt, kdtk, vt, vtk, oT, oTk, Hsb_t):
        P = self.P
        cf = self.c128f
        cb = self.c128b
        t0 = TP
        co = 8 * 128
        n = TS
        ex, exk = self.R[7], 'R7'
        seq2 = V(self.seqm[:, :].unsqueeze(2).to_broadcast([TS, NS, 64]), 'seqm')

        def state_apply(src_t, src_k, out_red):
            for hp in range(2):
                hs = slice(hp * 64, (hp + 1) * 64)
                pn, ps = P.ps()
                P.mm(V(ps[0:n, 0:512], pn), V(src_t[hs, t0:t0 + n], src_k), V(Hsb_t[hs, 0:512], 'cb7'))
                P.tt(V(ex[0:n, hp * 512:(hp + 1) * 512].rearrange("p (s e) -> p s e", s=NS), exk),
                     V(ps[0:n, 0:512].rearrange("p (s e) -> p s e", s=NS), pn), seq2, ALU.mult)
            P.red(out_red, V(ex[0:n, 0:1024].rearrange("p (h s e) -> p h e s", h=2, s=NS), exk))
        xc32 = V(cf[1][0:n, 0:128].rearrange("p (h e) -> p h e", h=2), 'cf1')
        state_apply(at, atk, xc32)
        pn, ps = P.ps()
        for hp in range(2):
            P.mm(V(ps[0:n, hp * 64:(hp + 1) * 64], pn), V(LkT[0:n, hp * n:(hp + 1) * n], LkTk),
                 V(vt[0:n, co + hp * 64:co + (hp + 1) * 64], vtk))
        xcb = V(cb[6][0:n, 0:128], 'cb6')
        P.tt(xcb, V(ps[0:n, 0:128], pn), V(cf[1][0:n, 0:128], 'cf1'), ALU.add)
        pn, ps = P.ps()
        for hp in range(2):
            P.mm(V(ps[0:n, hp * 64:(hp + 1) * 64], pn), V(XbT[0:n, hp * n:(hp + 1) * n], XbTk),
                 V(cb[6][0:n, hp * 64:(hp + 1) * 64], 'cb6'))
        u32 = V(cf[2][0:n, 0:128], 'cf2')
        P.copy(u32, V(ps[0:n, 0:128], pn), eng='act')
        ub = V(cb[6][0:n, 128:256], 'cb6')
        P.copy(ub, u32, eng='dve')
        o32 = V(cf[3][0:n, 0:128], 'cf3')
        state_apply(qt, qtk, V(cf[3][0:n, 0:128].rearrange("p (h e) -> p h e", h=2), 'cf3'))
        pn, ps = P.ps()
        for hp in range(2):
            po = V(ps[0:n, hp * 64:(hp + 1) * 64], pn)
            P.mm(po, V(AqbT[0:n, hp * n:(hp + 1) * n], AqbTk), V(cb[6][0:n, 128 + hp * 64:128 + (hp + 1) * 64], 'cb6'),
                 start=True, stop=False)
            P.mm(po, V(AqkT[0:n, hp * n:(hp + 1) * n], AqkTk), V(vt[0:n, co + hp * 64:co + (hp + 1) * 64], vtk),
                 start=False, stop=True)
        P.tt(o32, V(ps[0:n, 0:128], pn), o32, ALU.add)
        pn, ps = P.ps()
        P.tr(V(ps[:, 0:n], pn), o32, V(self.identf[0:n, 0:n], 'identf'))
        P.copy(V(oT[:, t0:t0 + n], oTk), V(ps[:, 0:n], pn), eng='act')
        uex = self.R[7][0:n, 0:512].bitcast(BF16)
        vex = self.R[7][0:n, 512:1024].bitcast(BF16)
        for hp in range(2):
            P.tt(V(uex[:, hp * 512:(hp + 1) * 512].rearrange("p (s e) -> p s e", s=NS), exk),
                 V(cf[2][0:n, hp * 64:(hp + 1) * 64].unsqueeze(1).to_broadcast([n, NS, 64]), 'cf2'), seq2, ALU.mult)
            P.tt(V(vex[:, hp * 512:(hp + 1) * 512].rearrange("p (s e) -> p s e", s=NS), exk),
                 V(vt[0:n, co + hp * 64:co + (hp + 1) * 64].unsqueeze(1).to_broadcast([n, NS, 64]), vtk), seq2, ALU.mult)
        pn, ps = P.ps()
        for hp in range(2):
            hs = slice(hp * 64, (hp + 1) * 64)
            ph = V(ps[hs, 0:512], pn)
            P.mm(ph, V(bd[0:n, co + hp * 64:co + (hp + 1) * 64], bdk), V(uex[:, hp * 512:(hp + 1) * 512], exk),
                 start=True, stop=False)
            P.mm(ph, V(kdt[0:n, co + hp * 64:co + (hp + 1) * 64], kdtk), V(vex[:, hp * 512:(hp + 1) * 512], exk),
                 start=False, stop=True)
        Hs3 = V(self.Ss[:, 0:512].rearrange("p (s e) -> p s e", s=NS), 'Ss')
        P.tt(Hs3, Hs3, V(self.glc[:, 8:16].unsqueeze(2).to_broadcast([128, NS, 64]), 'glc'), ALU.mult)
        P.tt(Hs3, Hs3, V(ps[:, 0:512].rearrange("p (s e) -> p s e", s=NS), pn), ALU.add)

    def merge_and_ffn(self, blk, l):
        P = self.P
        R = self.R
        w512 = lambda t: t[:, :].rearrange("p (kc n) -> p kc n", kc=8)
        w4 = lambda t: t[:, 0:2048].rearrange("p (kc n) -> p kc n", kc=4)
        for jg in range(2):
            for b in range(3):
                gt, gk = self.wnext(('gate', blk, l, jg, b))
                bt, bk = self.wnext(('wbr', blk, l, jg, b))
                for jj in range(4):
                    j = jg * 4 + jj
                    acc, acck = R[jj], 'R%d' % jj
                    for ti, (t0, t1) in enumerate(TT):
                        n = t1 - t0
                        png, psg = P.ps()
                        for kc in range(KC):
                            P.mm(V(psg[:, 0:n], png), V(w512(gt)[:, kc, jj * 128:(jj + 1) * 128], gk),
                                 V(self.hT[:, kc, t0:t1], ('hT', ti)), start=(kc == 0), stop=(kc == KC - 1))
                        sg = V(self.small[:, 2 + (ti % 2), 0:n], ('small', 2 + (ti % 2)))
                        P.act(sg, V(psg[:, 0:n], png), AF.Sigmoid)
                        pnp, psp = P.ps()
                        for kc in range(4):
                            P.mm(V(psp[:, 0:n], pnp), V(w4(bt)[:, kc, jj * 128:(jj + 1) * 128], bk),
                                 V(self.yb[:, b * 4 + kc, t0:t1], ('yb', b * 4 + kc)), start=(kc == 0), stop=(kc == 3))
                        av = V(acc[:, t0:t1], (acck, ti))
                        if b == 0:
                            P.tt(av, V(psp[:, 0:n], pnp), sg, ALU.mult)
                        else:
                            P.tt(sg, V(psp[:, 0:n], pnp), sg, ALU.mult)
                            if b == 1:
                                P.tt(av, av, sg, ALU.add)
                            else:
                                P.tt(self.mch(j, t0, t1), av, sg, ALU.add)
        for jh in range(2):
            wt, wk = self.wnext(('wout', blk, l, jh))
            for jj in range(4):
                j = jh * 4 + jj
                for ti, (t0, t1) in enumerate(TT):
                    n = t1 - t0
                    pn, ps = P.ps()
                    for kc in range(KC):
                        P.mm(V(ps[:, 0:n], pn), V(w512(wt)[:, kc, jj * 128:(jj + 1) * 128], wk),
                             self.mch(kc, t0, t1), start=(kc == 0), stop=(kc == KC - 1))
                    xv = V(self.xT[:, j, t0:t1], ('xT', ti))
                    P.tt(xv, V(ps[:, 0:n], pn), xv, ALU.add)
        if hasattr(self, 'marks'):
            self.marks.append(('ffn b%d l%d' % (blk, l), P.cnt['pe'], P.cnt['act'], P.cnt['dve']))
        self.rmsnorm_to_h('norm2_g', l)
        for q in range(4):
            for g in range(2):
                wt, wk = self.wnext(('wup', blk, l, q, g))
                for jj in range(4):
                    uc = g * 4 + jj
                    for ti, (t0, t1) in enumerate(TT):
                        n = t1 - t0
                        pn, ps = P.ps()
                        for kc in range(KC):
                            P.mm(V(ps[:, 0:n], pn), V(w512(wt)[:, kc, jj * 128:(jj + 1) * 128], wk),
                                 V(self.hT[:, kc, t0:t1], ('hT', ti)), start=(kc == 0), stop=(kc == KC - 1))
                        rl = V(self.small[:, 2 + (ti % 2), 0:n], ('small', 2 + (ti % 2)))
                        P.act(rl, V(ps[:, 0:n], pn), AF.Relu)
                        P.tt(self.mch(uc, t0, t1), rl, rl, ALU.mult)
            for jh in range(2):
                wt, wk = self.wnext(('wdn', blk, l, q, jh))
                for jj in range(4):
                    j = jh * 4 + jj
                    for ti, (t0, t1) in enumerate(TT):
                        n = t1 - t0
                        pn, ps = P.ps()
                        for kc in range(KC):
                            P.mm(V(ps[:, 0:n], pn), V(w512(wt)[:, kc, jj * 128:(jj + 1) * 128], wk),
                                 self.mch(kc, t0, t1), start=(kc == 0), stop=(kc == KC - 1))
                        xv = V(self.xT[:, j, t0:t1], ('xT', ti))
                        P.tt(xv, V(ps[:, 0:n], pn), xv, ALU.add)

    def final_norm_store(self, blk):
        P = self.P
        for ti, (t0, t1) in enumerate(TT):
            n = t1 - t0
            pn, ps = P.ps()
            sq, sqk = self.Bt[ti % 2], 'B%d' % (ti % 2)
            for g3, (c0, c1) in enumerate([(0, 3), (3, 6), (6, 8)]):
                P.act(V(sq[:, 0:(c1 - c0) * n].rearrange("p (c n) -> p c n", c=c1 - c0), sqk),
                      V(self.xT[:, c0:c1, t0:t1], ('xT', ti)), AF.Square)
                for c in range(c0, c1):
                    P.mm(V(ps[:, 0:n], pn), V(self.onesb[:], 'onesb'), V(sq[:, (c - c0) * n:(c - c0 + 1) * n], sqk),
                         start=(c == 0), stop=(c == 7))
            rs = V(self.small[:, ti % 2, 0:n], ('small', ti % 2))
            P.act(rs, V(ps[:, 0:n], pn), AF.Ln, bias=self.epsc, scale=1.0 / D)
            P.act(rs, rs, AF.Exp, scale=-0.5)
            for c in range(KC):
                xv = V(self.xT[:, c, t0:t1], ('xT', ti))
                P.stt(xv, xv, self.pcol('final_norm_g', c), rs, ALU.mult, ALU.mult)
        identf = V(self.identf[:], 'identf')
        for i in range(8):
            stg = self.R[i % 2]
            sk = 'R%d' % (i % 2)
            for half in range(2):
                pn, ps = P.ps()
                for c in range(4):
                    cc = half * 4 + c
                    P.tr(V(ps[:, c * 128:(c + 1) * 128], pn), V(self.xT[:, cc, i * 128:(i + 1) * 128], 'xT'), identf)
                P.copy(V(stg[:, half * 512:(half + 1) * 512], sk), V(ps[:, :], pn), eng='act' if half else 'dve')
            P.dma(self.O['y_p'][blk * TP + i * 128:blk * TP + (i + 1) * 128, :], V(stg[:, 0:1024], sk), sk)
        stg = self.R[0]
        for half in range(2):
            pn, ps = P.ps()
            for c in range(4):
                cc = half * 4 + c
                P.tr(V(ps[0:TS, c * 128:(c + 1) * 128], pn), V(self.xT[:, cc, TP:T], 'xT'), identf)
            P.copy(V(stg[0:TS, half * 512:(half + 1) * 512], 'R0'), V(ps[0:TS, :], pn))
        P.dma(self.O['y_s'][blk * TS:(blk + 1) * TS, :], V(stg[0:TS, 0:1024], 'R0'), 'R0')

    def build(self):
        with ExitStack() as es:
            self.P = Prog(self.nc, es)
            P = self.P
            P.init_psum(reserve=1 if K_WARM else 0)
            self.alloc()
            consts = P.sb('consts', [128, 4], F32)
            P.memset(V(consts[:, 0:1], 'consts'), EPS)
            P.memset(V(consts[:, 1:2], 'consts'), 1.0)
            P.memset(V(consts[:, 2:3], 'consts'), B_LN_EPS)
            self.epsc = V(consts[:, 0:1], 'consts')
            self.onec = V(consts[:, 1:2], 'consts')
            self.lnepsc = V(consts[:, 2:3], 'consts')
            self.setup_consts()
            self.sched = self.weight_schedule()
            self.w_i = 0
            self.w_issued = 0
            def warm_fn():
                pnw, psw = P.psum_extra[0]
                for _ in range(K_WARM):
                    P.mm(V(psw[:, 0:512], pnw), V(self.identb[:], 'identb'), V(self.rmask[:, 0:512], 'rmask'))
            self.marks = []
            mk = lambda lab: self.marks.append((lab, P.cnt['pe'], P.cnt['act'], P.cnt['dve']))
            for blk in range(NBLK):
                mk('load%d' % blk)
                self.load_x_block(blk)
                for l in range(K_LAYERS):
                    mk('norm1 b%d l%d' % (blk, l))
                    self.rmsnorm_to_h('norm1_g', l)
                    if not (EN_A and EN_B and EN_C):
                        P.memset(V(self.yb[:], 'yb'), 0.0, eng='dve')
                    mk('A b%d l%d' % (blk, l))
                    if K_WARM:
                        P.warm_fn = warm_fn
                    if EN_A:
                        try:
                            self.branch_A(blk, l)
                        except _Stop:
                            pass
                    mk('B b%d l%d' % (blk, l))
                    if EN_B:
                        self.branch_B(blk, l)
                    mk('C b%d l%d' % (blk, l))
                    if EN_C:
                        self.branch_C(blk, l)
                    P.warm_fn = None
                    mk('merge b%d l%d' % (blk, l))
                    if K_MERGE:
                        self.merge_and_ffn(blk, l)
                mk('final%d' % blk)
                self.final_norm_store(blk)
            mk('end')
            assert K_ASTOP or self.w_i == len(self.sched)
            P.final_wait_all()
            P.build()
        return self.nc


_CACHE = {}


def kernel(**inputs):
    inp = {k: np.ascontiguousarray(np.asarray(v, dtype=np.float32)) for k, v in inputs.items()}
    if 'nc' not in _CACHE:
        _CACHE['nc'] = Builder().build()
    nc = _CACHE['nc']
    wnames = [k for k in INPUT_SHAPES if k not in ('xp', 'xs', 'sa_S', 'sa_conv', 'sb_S', 'sb_shift', 'sc_h', 'sc_conv')]
    in_maps = []
    for c in range(NCORES):
        s = slice(c * NSEQ_CORE, (c + 1) * NSEQ_CORE)
        m = {
            'xp': inp['x_prompt'][c],
            'xs': inp['x_sample'][s].reshape(NSEQ_CORE * 4, D),
            'sa_S': inp['state_a_S'][:, s], 'sa_conv': inp['state_a_conv'][:, s],
            'sb_S': inp['state_b_S'][:, s], 'sb_shift': inp['state_b_shift'][:, s],
            'sc_h': inp['state_c_h'][:, s], 'sc_conv': inp['state_c_conv'][:, s],
        }
        for k in wnames:
            m[k] = inp[k]
        in_maps.append({k: np.ascontiguousarray(v) for k, v in m.items()})
    res = run_bass_kernel_spmd(nc, in_maps, core_ids=list(range(NCORES)))
    rs = res.results
    y_prompt = np.stack([rs[c]['y_p'] for c in range(NCORES)], axis=0)
    y_sample = np.concatenate([rs[c]['y_s'].reshape(NSEQ_CORE, 4, D) for c in range(NCORES)], axis=0)
    outs = [y_prompt, y_sample]
    for nm in ['p_a_S', 'p_a_conv', 'p_b_S', 'p_b_shift', 'p_c_h', 'p_c_conv']:
        outs.append(np.stack([rs[c][nm] for c in range(NCORES)], axis=1))
    for nm in ['s_a_S', 's_a_conv', 's_b_S', 's_b_shift', 's_c_h', 's_c_conv']:
        outs.append(np.concatenate([rs[c][nm] for c in range(NCORES)], axis=1))
    return tuple(np.ascontiguousarray(o.astype(np.float32)) for o in outs)
```

```python
import os
import numpy as np
from contextlib import ExitStack
import concourse.bass as bass
import concourse.mybir as mybir
from concourse.bass_utils import run_bass_kernel_spmd

F32 = mybir.dt.float32
BF16 = mybir.dt.bfloat16
AF = mybir.ActivationFunctionType
ALU = mybir.AluOpType
AX = mybir.AxisListType

ENGS = ['pe', 'act', 'dve', 'pool', 'sp']

EN_A = os.environ.get('K_EN_A', '1') == '1'
EN_B = os.environ.get('K_EN_B', '1') == '1'
EN_C = os.environ.get('K_EN_C', '1') == '1'
K_LAYERS = int(os.environ.get('K_LAYERS', '2'))
K_MERGE = os.environ.get('K_MERGE', '1') == '1'
K_ASTOP = int(os.environ.get('K_ASTOP', '0'))
SKIP_SELF_WAW = os.environ.get('K_SELF_WAW', '0') == '0'
K_WARM = int(os.environ.get('K_WARM', '0'))


class _Stop(Exception):
    pass


def interleave(*gens):
    gens = [g for g in gens if g is not None]
    while gens:
        for g in list(gens):
            try:
                next(g)
            except StopIteration:
                gens.remove(g)


def drain(g):
    for _ in g:
        pass


class Pipe:
    def __init__(self):
        self.active = []

    def add(self, g):
        self.active.append(g)

    def step_all(self):
        for g in list(self.active):
            try:
                next(g)
            except StopIteration:
                self.active.remove(g)

    def finish(self, g):
        while g in self.active:
            self.step_all()

    def run_with(self, main):
        self.active.insert(0, main)
        self.finish(main)

    def drain_all(self):
        while self.active:
            self.step_all()


class V:
    __slots__ = ('ap', 'keys')

    def __init__(self, ap, *keys):
        self.ap = ap
        self.keys = [k if isinstance(k, tuple) else (k,) for k in keys]


class Prog:
    def __init__(self, nc, es):
        self.nc = nc
        self.es = es
        self.q = {e: [] for e in ENGS}
        self.cnt = {e: 0 for e in ENGS}
        self.waited = {e: {} for e in ENGS}
        self.semh = {}
        self.rec = {}
        self.children = {}
        self.dma_cnt = {}
        self.psum_tiles = []
        self.psum_i = 0
        self.warm_fn = None
        self._in_warm = False
        self.psum_extra = []
        for e in ['pe', 'act', 'dve', 'pool']:
            self.sem(e)

    def sem(self, name):
        if name not in self.semh:
            self.semh[name] = self.es.enter_context(
                self.nc.semaphore('s_' + name.replace(':', '_').replace('/', '_')))
        return self.semh[name]

    def sb(self, name, shape, dtype=F32):
        return self.es.enter_context(self.nc.sbuf_tensor(name, list(shape), dtype))

    def init_psum(self, n=8, reserve=0):
        for i in range(n):
            t = self.es.enter_context(self.nc.psum_tensor('ps%d' % i, [128, 512], F32))
            if i < n - reserve:
                self.psum_tiles.append(('ps%d' % i, t))
            else:
                self.psum_extra.append(('ps%d' % i, t))

    def ps(self):
        name, t = self.psum_tiles[self.psum_i % len(self.psum_tiles)]
        self.psum_i += 1
        return name, t

    @staticmethod
    def _k(k):
        return k if isinstance(k, tuple) else (k,)

    def _conflicts(self, key):
        out = []
        for i in range(1, len(key) + 1):
            r = self.rec.get(key[:i])
            if r is not None:
                out.append(r)
        stack = list(self.children.get(key, ()))
        while stack:
            c = stack.pop()
            r = self.rec.get(c)
            if r is not None:
                out.append(r)
            stack.extend(self.children.get(c, ()))
        return out

    def _get(self, key):
        r = self.rec.get(key)
        if r is None:
            r = [None, []]
            self.rec[key] = r
            for i in range(1, len(key)):
                self.children.setdefault(key[:i], set()).add(key[:i + 1])
        return r

    def emit(self, eng, fn, r=(), w=(), dma_tile=None):
        r = [self._k(k) for k in r]
        w = [self._k(k) for k in w]
        psr = [(k[0],) for k in r if k[0].startswith('ps')]
        r = [k for k in r if not k[0].startswith('ps')]
        w = [((k[0],) if k[0].startswith('ps') else k) for k in w] + psr
        is_dma = dma_tile is not None
        semname = None
        if is_dma:
            semname = 'dma:' + '/'.join(map(str, self._k(dma_tile)))
        deps = []
        for k in r:
            for rc in self._conflicts(k):
                if rc[0] is not None:
                    deps.append((rc[0], 'raw', False))
        for k in w:
            isps = k[0].startswith('ps')
            for rc in self._conflicts(k):
                if rc[0] is not None:
                    deps.append((rc[0], 'waw', isps))
                for ev in rc[1]:
                    deps.append((ev, 'war', isps))
        need = {}
        for (ev, kind, isps) in deps:
            sem, val, src_eng, src_dma = ev
            if src_eng == eng and not src_dma and not is_dma:
                if eng == 'pe' or isps:
                    continue
                if kind == 'war' or (kind == 'waw' and SKIP_SELF_WAW):
                    continue
            if is_dma and src_dma and sem == semname and kind == 'waw':
                continue
            if src_dma:
                val = 16 * self.dma_cnt[sem]
            if need.get(sem, 0) < val:
                need[sem] = val
        waits = []
        wd = self.waited[eng]
        for sem, val in need.items():
            if wd.get(sem, 0) >= val:
                continue
            wd[sem] = val
            waits.append((sem, val))
        if eng == 'pe' and self.warm_fn is not None and waits and not self._in_warm:
            self._in_warm = True
            self.warm_fn()
            self._in_warm = False
        if is_dma:
            self.sem(semname)
            self.dma_cnt[semname] = self.dma_cnt.get(semname, 0) + 1
            ev = (semname, 16 * self.dma_cnt[semname], eng, True)
            inc = (semname, 16)
        else:
            self.cnt[eng] += 1
            ev = (eng, self.cnt[eng], eng, False)
            inc = (eng, 1)
        self.q[eng].append((waits, fn, inc))
        for k in w:
            rc = self._get(k)
            rc[0] = ev
            rc[1] = []
            stack = list(self.children.get(k, ()))
            while stack:
                c = stack.pop()
                if c in self.rec:
                    self.rec[c] = [None, []]
                stack.extend(self.children.get(c, ()))
        for k in r:
            rc = self._get(k)
            rc[1].append(ev)
        return ev

    def final_wait_all(self, eng='sp'):
        waits = []
        for sem, n in self.dma_cnt.items():
            waits.append((sem, 16 * n))
        for e in ['pe', 'act', 'dve', 'pool']:
            if self.cnt[e]:
                waits.append((e, self.cnt[e]))
        self.q[eng].append((waits, None, None))

    def build(self):
        nc = self.nc
        with nc.Block() as block:
            def replay(ename):
                def f(engine):
                    for (waits, fn, inc) in self.q[ename]:
                        for (sem, val) in waits:
                            engine.wait_ge(self.semh[sem], val)
                        if fn is None:
                            continue
                        ins = fn(engine)
                        ins.then_inc(self.semh[inc[0]], inc[1])
                return f
            block.tensor(replay('pe'))
            block.scalar(replay('act'))
            block.vector(replay('dve'))
            block.gpsimd(replay('pool'))
            block.sync(replay('sp'))

    @staticmethod
    def _rk(*ops):
        ks = []
        for o in ops:
            if isinstance(o, V):
                ks += o.keys
        return ks

    @staticmethod
    def _a(o):
        return o.ap if isinstance(o, V) else o

    def mm(self, out, lhsT, rhs, start=True, stop=True):
        self.emit('pe', lambda e: e.matmul(out.ap, lhsT=lhsT.ap, rhs=rhs.ap, start=start, stop=stop),
                  r=self._rk(lhsT, rhs), w=out.keys)

    def tr(self, out, in_, ident):
        self.emit('pe', lambda e: e.transpose(out=out.ap, in_=in_.ap, identity=ident.ap),
                  r=self._rk(in_, ident), w=out.keys)

    def act(self, out, in_, func, bias=None, scale=None):
        kw = {}
        if bias is not None:
            kw['bias'] = self._a(bias)
        if scale is not None:
            kw['scale'] = self._a(scale)
        self.emit('act', lambda e: e.activation(out=out.ap, in_=in_.ap, func=func, **kw),
                  r=self._rk(in_, bias, scale), w=out.keys)

    def tt(self, out, in0, in1, op, eng='dve'):
        self.emit(eng, lambda e: e.tensor_tensor(out=out.ap, in0=in0.ap, in1=in1.ap, op=op),
                  r=self._rk(in0, in1), w=out.keys)

    def ts(self, out, in0, s1, s2, op0, op1=None, eng='dve'):
        if op1 is None:
            fn = lambda e: e.tensor_scalar(out=out.ap, in0=in0.ap, scalar1=self._a(s1), scalar2=None, op0=op0)
        else:
            fn = lambda e: e.tensor_scalar(out=out.ap, in0=in0.ap, scalar1=self._a(s1), scalar2=self._a(s2),
                                           op0=op0, op1=op1)
        self.emit(eng, fn, r=self._rk(in0, s1, s2), w=out.keys)

    def stt(self, out, in0, scalar, in1, op0, op1):
        self.emit('dve', lambda e: e.scalar_tensor_tensor(out=out.ap, in0=in0.ap, scalar=self._a(scalar),
                                                          in1=in1.ap, op0=op0, op1=op1),
                  r=self._rk(in0, scalar, in1), w=out.keys)

    def scan(self, out, d0, d1, initial, op0, op1):
        self.emit('dve', lambda e: e.tensor_tensor_scan(out=out.ap, data0=d0.ap, data1=d1.ap,
                                                        initial=self._a(initial), op0=op0, op1=op1),
                  r=self._rk(d0, d1, initial), w=out.keys)

    def recip(self, out, in_):
        self.emit('dve', lambda e: e.reciprocal(out=out.ap, in_=in_.ap), r=in_.keys, w=out.keys)

    def red(self, out, in_, op=ALU.add):
        self.emit('dve', lambda e: e.tensor_reduce(out=out.ap, in_=in_.ap, axis=AX.X, op=op),
                  r=in_.keys, w=out.keys)

    def copy(self, out, in_, eng='dve'):
        if eng == 'act':
            self.emit('act', lambda e: e.activation(out=out.ap, in_=in_.ap, func=AF.Copy), r=in_.keys, w=out.keys)
        else:
            self.emit(eng, lambda e: e.tensor_copy(out=out.ap, in_=in_.ap), r=in_.keys, w=out.keys)

    def memset(self, out, val, eng='pool'):
        self.emit(eng, lambda e: e.memset(out.ap, val), w=out.keys)

    def asel(self, out, in_, pattern, cmp, fill, base, cm):
        self.emit('pool', lambda e: e.affine_select(out=out.ap, in_=in_.ap, pattern=pattern, compare_op=cmp,
                                                    fill=fill, base=base, channel_multiplier=cm),
                  r=in_.keys, w=out.keys)

    def dma(self, out, in_, tile_key, eng='sp', out_is_sb=True, nc_ok=False):
        kw = {}
        if nc_ok:
            kw['allow_slow_non_contiguous'] = True
        oa = self._a(out)
        ia = self._a(in_)
        self.emit(eng, lambda e: e.dma_start(out=oa, in_=ia, **kw),
                  r=self._rk(in_), w=self._rk(out), dma_tile=tile_key)


NCORES = 8
D = 1024
KC = 8
SEQ = 2048
DEPTH = 2
NSEQ_CORE = 16
TP = 1024
NS = 8
TS = 32
T = TP + TS
NBLK = 2
TT = [(0, 352), (352, 704), (704, 1056)]
N_IN = 7944
A_OFF, B_OFF, C_OFF, G_OFF = 0, 2056, 3848, 4872
EPS = 1e-6
B_LN_EPS = 64e-5
RW = 1088
BW = 1152
NCHUNK = 9
NEU_LEVELS_P = 7
NEU_LEVELS_S = 2


def chunk_info(c):
    if c < 8:
        return c * 128, 128, 1
    return TP, TS, NS


PVECS = [
    ('norm1_g', 16), ('norm2_g', 16), ('final_norm_g', 8), ('a_conv_w', 96), ('a_norm_g', 2),
    ('b_mu', 28), ('b_w0', 8), ('b_a0', 8), ('b_k_k', 8), ('b_k_a', 8), ('b_ln_w', 8), ('b_ln_b', 8),
    ('b_r_k', 8), ('c_conv_w', 32), ('c_conv_b', 8), ('c_ba', 8), ('c_bx', 8), ('c_L', 8),
]
PREARR = {
    'norm1_g': "l (r p) -> (l r) p", 'norm2_g': "l (r p) -> (l r) p", 'final_norm_g': "(r p) -> r p",
    'a_conv_w': "l j (r p) -> (l j r) p", 'a_norm_g': "l p -> l p", 'b_mu': "l (r p) -> (l r) p",
    'b_w0': "l (r p) -> (l r) p", 'b_a0': "l (r p) -> (l r) p", 'b_k_k': "l (r p) -> (l r) p",
    'b_k_a': "l (r p) -> (l r) p", 'b_ln_w': "l (r p) -> (l r) p", 'b_ln_b': "l (r p) -> (l r) p",
    'b_r_k': "l (r h2) n -> (l r) (h2 n)", 'c_conv_w': "l j (r p) -> (l j r) p",
    'c_conv_b': "l (r p) -> (l r) p", 'c_ba': "l (r p) -> (l r) p", 'c_bx': "l (r p) -> (l r) p",
    'c_L': "l (r p) -> (l r) p",
}


def param_rows():
    off = {}
    r = 0
    for name, n in PVECS:
        if (r % 128) + n > 128:
            r = (r // 128 + 1) * 128
        off[name] = r
        r += n
    nst = (r + 127) // 128
    return off, nst


POFF, PNST = param_rows()

INPUT_SHAPES = {
    'xp': [SEQ, D], 'xs': [NSEQ_CORE * 4, D],
    'sa_S': [DEPTH, NSEQ_CORE, 4, 128, 128], 'sa_conv': [DEPTH, NSEQ_CORE, 3, 1536],
    'sb_S': [DEPTH, NSEQ_CORE, 8, 64, 64], 'sb_shift': [DEPTH, NSEQ_CORE, 1, 1792],
    'sc_h': [DEPTH, NSEQ_CORE, 512], 'sc_conv': [DEPTH, NSEQ_CORE, 3, 512],
    'norm1_g': [DEPTH, D], 'w_in': [DEPTH, D, N_IN], 'a_conv_w': [DEPTH, 4, 1536], 'a_A_log': [DEPTH, 4],
    'a_dt_bias': [DEPTH, 4], 'a_norm_g': [DEPTH, 128], 'b_mu': [DEPTH, 1792], 'b_w0': [DEPTH, 512],
    'b_w_up': [DEPTH, 64, 512], 'b_a0': [DEPTH, 512], 'b_a_up': [DEPTH, 64, 512], 'b_g_up': [DEPTH, 128, 512],
    'b_k_k': [DEPTH, 512], 'b_k_a': [DEPTH, 512], 'b_r_k': [DEPTH, 8, 64], 'b_ln_w': [DEPTH, 512],
    'b_ln_b': [DEPTH, 512], 'c_conv_w': [DEPTH, 4, 512], 'c_conv_b': [DEPTH, 512], 'c_wa': [DEPTH, 8, 64, 64],
    'c_ba': [DEPTH, 512], 'c_wx': [DEPTH, 8, 64, 64], 'c_bx': [DEPTH, 512], 'c_L': [DEPTH, 512],
    'w_branch': [DEPTH, 3, 512, D], 'w_out': [DEPTH, D, D], 'norm2_g': [DEPTH, D], 'w_up': [DEPTH, D, 4 * D],
    'w_down': [DEPTH, 4 * D, D], 'final_norm_g': [D],
}
OUTPUT_SHAPES = {
    'y_p': [SEQ, D], 'y_s': [NSEQ_CORE * 4, D],
    'p_a_S': [DEPTH, 4, 128, 128], 'p_a_conv': [DEPTH, 3, 1536], 'p_b_S': [DEPTH, 8, 64, 64],
    'p_b_shift': [DEPTH, 1, 1792], 'p_c_h': [DEPTH, 512], 'p_c_conv': [DEPTH, 3, 512],
    's_a_S': [DEPTH, NSEQ_CORE, 4, 128, 128], 's_a_conv': [DEPTH, NSEQ_CORE, 3, 1536],
    's_b_S': [DEPTH, NSEQ_CORE, 8, 64, 64], 's_b_shift': [DEPTH, NSEQ_CORE, 1, 1792],
    's_c_h': [DEPTH, NSEQ_CORE, 512], 's_c_conv': [DEPTH, NSEQ_CORE, 3, 512],
}


class Builder:
    def __init__(self):
        self.nc = bass.Bass("TRN2", target_bir_lowering=False)
        nc = self.nc
        self.I = {k: nc.dram_tensor(k, s, F32, kind="ExternalInput").ap() for k, s in INPUT_SHAPES.items()}
        self.O = {k: nc.dram_tensor(k, s, F32, kind="ExternalOutput").ap() for k, s in OUTPUT_SHAPES.items()}

    def alloc(self):
        P = self.P
        self.xT = P.sb('xT', [128, KC, T], F32)
        self.hT = P.sb('hT', [128, KC, T], BF16)
        self.yb = P.sb('yb', [128, 12, T], BF16)
        self.ring = [P.sb('wr%d' % i, [128, 4096], BF16) for i in range(3)]
        self.R = [P.sb('R%d' % i, [128, RW], F32) for i in range(8)]
        self.Bt = [P.sb('B%d' % i, [128, BW], BF16) for i in range(10)]
        self.PT = P.sb('PT', [128, PNST * 128], F32)
        self.identf = P.sb('identf', [128, 128], F32)
        self.identb = P.sb('identb', [128, 128], BF16)
        self.onesf = P.sb('onesf', [128, 128], F32)
        self.onesb = P.sb('onesb', [128, 128], BF16)
        self.ones64b = P.sb('ones64b', [128, 128], BF16)
        self.ones64f = P.sb('ones64f', [128, 128], F32)
        self.m_uincl = P.sb('m_uincl', [128, 2, 128], F32)
        self.m_lstr = P.sb('m_lstr', [128, 2, 128], F32)
        self.m_bigL = P.sb('m_bigL', [128, 2, 128], F32)
        self.m_negU = P.sb('m_negU', [128, 2, 128], F32)
        self.m_ustr = P.sb('m_ustr', [128, 2, 128], F32)
        self.seqm = P.sb('seqm', [32, 8], F32)
        self.seqmT = P.sb('seqmT', [8, 32], F32)
        self.rmask = P.sb('rmask', [128, T], BF16)
        self.small = P.sb('small', [128, 4, 352], F32)
        self.sqt = P.sb('sqt', [128, 352], BF16)
        self.sqts = [self.sqt, P.sb('sqt1', [128, 352], BF16), P.sb('sqt2', [128, 352], BF16)]
        self.sqtk = ['sqt', 'sqt1', 'sqt2']
        self.csm = P.sb('csm', [128, 4, 96], F32)
        self.c128f = [P.sb('cf0', [128, 512], F32)] + [P.sb('cf%d' % i, [128, 128], F32) for i in range(1, 6)]
        self.c128b = [P.sb('cb%d' % i, [128, 512], BF16) for i in range(8)]
        self.SA = P.sb('SA', [128, DEPTH, 4, 128], F32)
        self.SAb = P.sb('SAb', [128, 4, 128], BF16)
        self.HB = P.sb('HB', [128, DEPTH, 4, 64], F32)
        self.HBb = P.sb('HBb', [128, 4, 64], BF16)
        self.hC = P.sb('hC', [128, DEPTH, 4], F32)
        self.tailA = P.sb('tailA', [128, DEPTH, 12, 3], F32)
        self.tailB = P.sb('tailB', [128, DEPTH, 14], F32)
        self.tailC = P.sb('tailC', [128, DEPTH, 4, 3], F32)
        self.Ss = P.sb('Ss', [128, NS * 128], F32)
        self.a4 = P.sb('a4', [4, DEPTH, 2], F32)
        self.colsA = P.sb('colsA', [128, NCHUNK, 24], F32)
        self.glc = P.sb('glc', [128, 16], F32)
        self.lorW = P.sb('lorW', [128, 512], BF16)
        self.gupW = P.sb('gupW', [128, 512], BF16)
        self.cgate = P.sb('cgate', [128, 4, 2, 128], BF16)
        self.rkbd = P.sb('rkbd', [128, 4, 128], BF16)
        self.stage = [P.sb('stg0', [32, 512], F32)]
        self.stage.append(self.stage[0])

    def mch(self, j, t0, t1):
        r = 4 + j // 2
        v = self.R[r][:, 0:2 * 528].bitcast(BF16)
        o = (j % 2) * T
        return V(v[:, o + t0:o + t1], ('R%d' % r, j % 2))

    def astop(self, k):
        if hasattr(self, 'marks'):
            P = self.P
            self.marks.append(('  a-stage%d' % k, P.cnt['pe'], P.cnt['act'], P.cnt['dve']))
        if K_ASTOP == k:
            raise _Stop()

    def ti_pipe(self, gen_fn):
        interleave(*[gen_fn(ti, t0, t1, t1 - t0) for ti, (t0, t1) in enumerate(TT)])

    def pcol(self, name, idx):
        r = POFF[name] + idx
        return V(self.PT[:, r:r + 1], 'PT')

    def setup_consts(self):
        P = self.P
        I = self.I
        onesf = V(self.onesf[:], 'onesf')
        P.memset(onesf, 1.0)
        P.memset(V(self.onesb[:], 'onesb'), 1.0)
        P.asel(V(self.identf[:], 'identf'), onesf, [[-1, 128]], ALU.is_equal, 0.0, 0, 1)
        P.copy(V(self.identb[:], 'identb'), V(self.identf[:], 'identf'), eng='pool')
        o64 = V(self.ones64f[:], 'ones64f')
        P.memset(o64, 0.0)
        P.memset(V(self.ones64f[0:64, 0:64], 'ones64f'), 1.0)
        P.memset(V(self.ones64f[64:128, 64:128], 'ones64f'), 1.0)
        P.copy(V(self.ones64b[:], 'ones64b'), o64, eng='pool')
        P.asel(V(self.seqm[:], 'seqm'), V(self.onesf[0:32, 0:8], 'onesf'), [[-4, 8]], ALU.is_ge, 0.0, 0, 1)
        P.asel(V(self.seqm[:], 'seqm'), V(self.seqm[:], 'seqm'), [[4, 8]], ALU.is_ge, 0.0, 3, -1)
        P.asel(V(self.seqmT[:], 'seqmT'), V(self.onesf[0:8, 0:32], 'onesf'), [[1, 32]], ALU.is_ge, 0.0, 0, -4)
        P.asel(V(self.seqmT[:], 'seqmT'), V(self.seqmT[:], 'seqmT'), [[-1, 32]], ALU.is_ge, 0.0, 3, 4)
        ui = V(self.m_uincl[:, 0, :], 'm_uincl')
        P.asel(ui, onesf, [[1, 128]], ALU.is_ge, 0.0, 0, -1)
        ls = V(self.m_lstr[:, 0, :], 'm_lstr')
        P.asel(ls, onesf, [[-1, 128]], ALU.is_gt, 0.0, 0, 1)
        pn, ps = P.ps()
        same = V(ps[0:32, 0:32], pn)
        P.mm(same, V(self.seqmT[:], 'seqmT'), V(self.seqmT[:], 'seqmT'))
        P.memset(V(self.m_uincl[:, 1, :], 'm_uincl'), 0.0)
        P.memset(V(self.m_lstr[:, 1, :], 'm_lstr'), 0.0)
        P.tt(V(self.m_uincl[0:32, 1, 0:32], 'm_uincl'), same, V(self.m_uincl[0:32, 0, 0:32], 'm_uincl'), ALU.mult)
        P.tt(V(self.m_lstr[0:32, 1, 0:32], 'm_lstr'), same, V(self.m_lstr[0:32, 0, 0:32], 'm_lstr'), ALU.mult)
        P.ts(V(self.m_bigL[:], 'm_bigL'), V(self.m_lstr[:], 'm_lstr'), -1e4, 1e4, ALU.mult, ALU.add)
        P.ts(V(self.m_negU[:], 'm_negU'), V(self.m_uincl[:], 'm_uincl'), 1e4, -1e4, ALU.mult, ALU.add)
        P.tt(V(self.m_ustr[:, 0, :], 'm_ustr'), V(self.m_uincl[:, 0, :], 'm_uincl'), V(self.identf[:], 'identf'), ALU.subtract)
        P.memset(V(self.m_ustr[:, 1, :], 'm_ustr'), 0.0)
        P.tt(V(self.m_ustr[0:32, 1, 0:32], 'm_ustr'), V(self.m_uincl[0:32, 1, 0:32], 'm_uincl'),
             V(self.identf[0:32, 0:32], 'identf'), ALU.subtract)
        rm = V(self.rmask[:], 'rmask')
        P.memset(rm, 1.0)
        P.memset(V(self.rmask[:, 0:TP].rearrange("p (c n) -> p c n", n=128)[:, :, 0], 'rmask'), 0.0)
        P.memset(V(self.rmask[:, TP:T].rearrange("p (s j) -> p s j", j=4)[:, :, 0], 'rmask'), 0.0)
        stg = self.R[0]
        for st in range(PNST):
            for name, n in PVECS:
                r0 = POFF[name]
                if r0 // 128 != st:
                    continue
                if name == 'a_norm_g':
                    src = I[name]
                elif name == 'b_r_k':
                    src = I[name].rearrange("l (r h2) n -> (l r) (h2 n)", h2=2)
                else:
                    src = I[name].rearrange(PREARR[name], p=128)
                P.dma(V(stg[r0 % 128:r0 % 128 + n, 0:128], 'R0'), src, 'R0')
            nrows = max((POFF[nm] + n - st * 128) for nm, n in PVECS if POFF[nm] // 128 == st)
            pn, ps = P.ps()
            P.tr(V(ps[:, 0:nrows], pn), V(stg[0:nrows, 0:128], 'R0'), V(self.identf[0:nrows, 0:nrows], 'identf'))
            P.copy(V(self.PT[:, st * 128:st * 128 + nrows], 'PT'), V(ps[:, 0:nrows], pn), eng='act')
        P.dma(V(self.a4[:, :, 0], 'a4'), I['a_A_log'].rearrange("l h -> h l"), 'a4', nc_ok=True)
        P.dma(V(self.a4[:, :, 1], 'a4'), I['a_dt_bias'].rearrange("l h -> h l"), 'a4', nc_ok=True)
        P.act(V(self.a4[:, :, 0], 'a4'), V(self.a4[:, :, 0], 'a4'), AF.Exp)
        P.ts(V(self.a4[:, :, 0], 'a4'), V(self.a4[:, :, 0], 'a4'), -1.0, None, ALU.mult)
        for nm in ['SA', 'HB', 'hC', 'tailA', 'tailB', 'tailC']:
            P.memset(V(getattr(self, nm)[:], nm), 0.0)
        P.memset(V(self.cgate[:], 'cgate'), 0.0)

    def weight_schedule(self):
        I = self.I
        sched = []
        for blk in range(NBLK):
            for l in range(K_LAYERS):
                win = I['w_in'][l].rearrange("(kc p) n -> p kc n", p=128)

                def w512(c0, win=win):
                    return [(lambda t: t[:, :].rearrange("p (kc n) -> p kc n", kc=8), win[:, :, c0:c0 + 512])]

                if EN_A:
                    sched.append((('A_ba', blk, l),
                                  [(lambda t: t[:, 0:64].rearrange("p (kc n) -> p kc n", kc=8),
                                    win[:, :, A_OFF + 2048:A_OFF + 2056])]))
                    for hd in range(4):
                        pcs = []
                        for cc in range(4):
                            pcs.append((lambda t, cc=cc: t[:, :].rearrange("p (kc c n) -> p kc c n", kc=8, c=4)[:, :, cc, :],
                                        win[:, :, A_OFF + cc * 512 + hd * 128:A_OFF + cc * 512 + (hd + 1) * 128]))
                        sched.append((('A_head', blk, l, hd), pcs))
                if EN_B:
                    sched.append((('B_lora', blk, l),
                                  [(lambda t: t[:, 0:2048].rearrange("p (kc n) -> p kc n", kc=8),
                                    win[:, :, B_OFF + 1536:B_OFF + 1792])]))
                    for pr in range(4):
                        pcs = []
                        for cc in range(3):
                            pcs.append((lambda t, cc=cc: t[:, 0:3072].rearrange("p (kc c n) -> p kc c n", kc=8, c=3)[:, :, cc, :],
                                        win[:, :, B_OFF + cc * 512 + pr * 128:B_OFF + cc * 512 + (pr + 1) * 128]))
                        sched.append((('B_pair', blk, l, pr), pcs))
                if EN_C:
                    sched.append((('C_x', blk, l), w512(C_OFF)))
                    sched.append((('C_g', blk, l), w512(C_OFF + 512)))
                if not K_MERGE:
                    continue
                wbr = I['w_branch'][l]
                for jg in range(2):
                    for b in range(3):
                        sched.append((('gate', blk, l, jg, b), w512(G_OFF + b * 1024 + jg * 512)))
                        sched.append((('wbr', blk, l, jg, b),
                                      [(lambda t: t[:, 0:2048].rearrange("p (kc n) -> p kc n", kc=4),
                                        wbr[b].rearrange("(kc p) n -> p kc n", p=128)[:, :, jg * 512:(jg + 1) * 512])]))
                wo = I['w_out'][l].rearrange("(kc p) n -> p kc n", p=128)
                for jh in range(2):
                    sched.append((('wout', blk, l, jh),
                                  [(lambda t: t[:, :].rearrange("p (kc n) -> p kc n", kc=8), wo[:, :, jh * 512:(jh + 1) * 512])]))
                wu = I['w_up'][l].rearrange("(kc p) n -> p kc n", p=128)
                wd = I['w_down'][l].rearrange("(kc p) n -> p kc n", p=128)
                for q in range(4):
                    for g in range(2):
                        c0 = (2 * q + g) * 512
                        sched.append((('wup', blk, l, q, g),
                                      [(lambda t: t[:, :].rearrange("p (kc n) -> p kc n", kc=8), wu[:, :, c0:c0 + 512])]))
                    for jh in range(2):
                        sched.append((('wdn', blk, l, q, jh),
                                      [(lambda t: t[:, :].rearrange("p (kc n) -> p kc n", kc=8),
                                        wd[:, q * 8:(q + 1) * 8, jh * 512:(jh + 1) * 512])]))
        return sched

    def wnext(self, tag):
        P = self.P
        i = self.w_i
        if K_ASTOP:
            while self.sched[self.w_i][0] != tag:
                self.w_i += 1
            i = self.w_i
            self.w_issued = max(self.w_issued, i)
        assert self.sched[i][0] == tag, (self.sched[i][0], tag)
        while self.w_issued < min(len(self.sched), i + 2):
            j = self.w_issued
            slot = j % 3
            t = self.ring[slot]
            key = 'wr%d' % slot
            for (dst_fn, src) in self.sched[j][1]:
                P.dma(V(dst_fn(t), key), src, key, eng='pool')
            self.w_issued += 1
        self.w_i += 1
        return self.ring[i % 3], 'wr%d' % (i % 3)

    def load_x_block(self, blk):
        P = self.P
        identf = V(self.identf[:], 'identf')
        for i in range(8):
            stg = self.R[i % 2]
            sk = 'R%d' % (i % 2)
            P.dma(V(stg[:, 0:1024], sk), self.I['xp'][blk * TP + i * 128:blk * TP + (i + 1) * 128, :], sk)
            for half in range(2):
                pn, ps = P.ps()
                for c in range(4):
                    cc = half * 4 + c
                    P.tr(V(ps[:, c * 128:(c + 1) * 128], pn), V(stg[:, cc * 128:(cc + 1) * 128], sk), identf)
                P.copy(V(self.xT[:, half * 4:(half + 1) * 4, i * 128:(i + 1) * 128], 'xT'),
                       V(ps[:, :].rearrange("p (c n) -> p c n", c=4), pn), eng='act' if half else 'dve')
        stg = self.R[0]
        P.dma(V(stg[0:TS, 0:1024], 'R0'), self.I['xs'][blk * TS:(blk + 1) * TS, :], 'R0')
        pn, ps = P.ps()
        for c in range(8):
            P.tr(V(ps[:, c * 32:(c + 1) * 32], pn), V(stg[0:32, c * 128:(c + 1) * 128], 'R0'),
                 V(self.identf[0:32, 0:32], 'identf'))
        P.copy(V(self.xT[:, :, TP:T], 'xT'), V(ps[:, 0:256].rearrange("p (c n) -> p c n", c=8), pn))

    def xkeys(self, tt):
        return ('xT', tt)

    def rmsnorm_to_h(self, gname, l):
        P = self.P
        onesb = V(self.onesb[:], 'onesb')
        def g(ti, t0, t1, n):
            pn, ps = P.ps()
            sq = self.Bt[ti]
            sqk = 'B%d' % ti
            for g3, (c0, c1) in enumerate([(0, 3), (3, 6), (6, 8)]):
                P.act(V(sq[:, 0:(c1 - c0) * n].rearrange("p (c n) -> p c n", c=c1 - c0), sqk),
                      V(self.xT[:, c0:c1, t0:t1], ('xT', ti)), AF.Square)
                for c in range(c0, c1):
                    P.mm(V(ps[:, 0:n], pn), onesb, V(sq[:, (c - c0) * n:(c - c0 + 1) * n], sqk),
                         start=(c == 0), stop=(c == 7))
                yield
            rs = V(self.small[:, ti, 0:n], ('small', ti))
            P.act(rs, V(ps[:, 0:n], pn), AF.Ln, bias=self.epsc, scale=1.0 / D)
            P.act(rs, rs, AF.Exp, scale=-0.5)
            yield
            for c in range(KC):
                gcol = self.pcol(gname, (l * 8 + c) if l is not None else c)
                P.stt(V(self.hT[:, c, t0:t1], ('hT', ti)), V(self.xT[:, c, t0:t1], ('xT', ti)), gcol, rs,
                      ALU.mult, ALU.mult)
                if c % 4 == 3:
                    yield
        self.ti_pipe(g)

    def proj(self, wt, wkey, colsel, M, evac):
        P = self.P
        for ti, (t0, t1) in enumerate(TT):
            n = t1 - t0
            pn, ps = P.ps()
            for kc in range(KC):
                P.mm(V(ps[0:M, 0:n], pn), V(colsel(wt, kc), wkey), V(self.hT[:, kc, t0:t1], ('hT', ti)),
                     start=(kc == 0), stop=(kc == KC - 1))
            evac(ti, t0, t1, V(ps[0:M, 0:n], pn))

    def evac_to_X(self, Xt, Xk, hist):
        P = self.P
        w = hist + 4
        base = hist + TP

        def f(ti, t0, t1, psv):
            if ti < 2:
                P.copy(V(Xt[:, hist + t0:hist + t1], Xk), psv, eng='act')
            else:
                npr = TP - t0
                P.copy(V(Xt[:, hist + t0:hist + TP], Xk), V(psv.ap[:, 0:npr], *psv.keys), eng='act')
                P.copy(V(Xt[:, base:base + NS * w].rearrange("p (s j) -> p s j", j=w)[:, :, hist:w], Xk),
                       V(psv.ap[:, npr:npr + TS].rearrange("p (s j) -> p s j", j=4), *psv.keys), eng='dve')
        return f

    def conv4(self, out, ok, Xt, Xk, wname, wrow_fn):
        P = self.P
        hist = 3
        base = hist + TP
        for j in range(3, -1, -1):
            wc = self.pcol(wname, wrow_fn(j))
            src = V(Xt[:, j:j + TP], Xk)
            dst = V(out[:, 0:TP], ok)
            if j == 3:
                P.ts(dst, src, wc, None, ALU.mult)
            else:
                P.stt(dst, src, wc, dst, ALU.mult, ALU.add)
            srcs = V(Xt[:, base:base + NS * 7].rearrange("p (s j) -> p s j", j=7)[:, :, j:j + 4], Xk)
            dsts = V(out[:, TP:T].rearrange("p (s j) -> p s j", j=4), ok)
            if j == 3:
                P.ts(dsts, srcs, wc, None, ALU.mult)
            else:
                P.stt(dsts, srcs, wc, dsts, ALU.mult, ALU.add)

    def load_hist_T(self, src_rows, nrows, ncol_chunks, dst_fn):
        P = self.P
        stg = self.stage[0]
        P.dma(V(stg[0:nrows, 0:ncol_chunks * 128], 'stg0'), src_rows, 'stg0')
        for c in range(ncol_chunks):
            pn, ps = P.ps()
            P.tr(V(ps[:, 0:nrows], pn), V(stg[0:nrows, c * 128:(c + 1) * 128], 'stg0'),
                 V(self.identf[0:nrows, 0:nrows], 'identf'))
            dst_fn(c, V(ps[:, 0:nrows], pn))

    def store_rows(self, dram_ap, psv, nrows, ncols):
        P = self.P
        stg = self.stage[0]
        P.copy(V(stg[0:nrows, 0:ncols], 'stg0'), psv, eng='act')
        P.dma(dram_ap, V(stg[0:nrows, 0:ncols], 'stg0'), 'stg0')

    def rows_mm(self, tok_ap_fn, M, wt, wkey, colsel_n, ncols):
        P = self.P
        pn, ps = P.ps()
        for kc in range(KC):
            P.mm(V(ps[0:M, 0:ncols], pn), V(tok_ap_fn(kc), 'hT'), V(colsel_n(wt, kc), wkey),
                 start=(kc == 0), stop=(kc == KC - 1))
        return V(ps[0:M, 0:ncols], pn)

    def neumann(self, Nb, NTb, n, G, levels, X32, Xb, keys, fp32=False):
        P = self.P
        kN, kNT, kX32, kXb = keys
        idb = V(self.identf[0:n, 0:n].unsqueeze(1).to_broadcast([n, G, n]), 'identf')
        P.tt(V(X32[0:n, 0:G * n].rearrange("p (g n) -> p g n", g=G), kX32),
             V(NTb[0:n, 0:G * n].rearrange("p (g n) -> p g n", g=G), kNT), idb, ALU.add)
        if fp32:
            Xop, kXop = X32, kX32
        else:
            Xop, kXop = Xb, kXb
            P.copy(V(Xb[0:n, 0:G * n], kXb), V(X32[0:n, 0:G * n], kX32), eng='act')
        for m in range(1, levels):
            last = (m == levels - 1)
            pn1, ps1 = P.ps()
            for g in range(G):
                sl = slice(g * n, (g + 1) * n)
                P.mm(V(ps1[0:n, sl], pn1), V(NTb[0:n, sl], kNT), V(Nb[0:n, sl], kN))
            if not last:
                pn2, ps2 = P.ps()
                for g in range(G):
                    sl = slice(g * n, (g + 1) * n)
                    P.mm(V(ps2[0:n, sl], pn2), V(Nb[0:n, sl], kN), V(NTb[0:n, sl], kNT))
            yield
            P.copy(V(Nb[0:n, 0:G * n], kN), V(ps1[0:n, 0:G * n], pn1), eng='act')
            if not last:
                P.copy(V(NTb[0:n, 0:G * n], kNT), V(ps2[0:n, 0:G * n], pn2), eng='dve')
            yield
            pn3, ps3 = P.ps()
            for g in range(G):
                sl = slice(g * n, (g + 1) * n)
                P.mm(V(ps3[0:n, sl], pn3), V(Nb[0:n, sl], kN), V(Xop[0:n, sl], kXop))
            if not fp32:
                P.tt(V(Xb[0:n, 0:G * n], kXb), V(ps3[0:n, 0:G * n], pn3), V(X32[0:n, 0:G * n], kX32), ALU.add)
                yield
                if not last:
                    P.tt(V(X32[0:n, 0:G * n], kX32), V(ps3[0:n, 0:G * n], pn3), V(X32[0:n, 0:G * n], kX32), ALU.add)
            else:
                P.tt(V(X32[0:n, 0:G * n], kX32), V(ps3[0:n, 0:G * n], pn3), V(X32[0:n, 0:G * n], kX32), ALU.add)
                yield
                if last:
                    P.copy(V(Xb[0:n, 0:G * n], kXb), V(X32[0:n, 0:G * n], kX32), eng='act')
            yield

    def branch_C(self, blk, l):
        P = self.P
        I, O = self.I, self.O
        R, Bt = self.R, self.Bt
        s0 = blk * NS
        for g in range(2):
            for ax, nm in enumerate(['c_wa', 'c_wx']):
                src = I[nm][l].rearrange("(c g) i j -> g i c j", g=2)[g]
                P.dma(V(self.cgate[g * 64:(g + 1) * 64, :, ax, g * 64:(g + 1) * 64], 'cgate'), src, 'cgate',
                      eng='pool')
        wx_t, wx_k = self.wnext(('C_x', blk, l))
        w512 = lambda t: t[:, :].rearrange("p (kc n) -> p kc n", kc=8)
        csm = self.csm
        self.load_hist_T(I['sc_h'][l, s0:s0 + NS, :], NS, 4,
                         lambda c, psv: P.copy(V(csm[:, 0, c * 8:(c + 1) * 8], ('csm', 0)), psv))
        self.load_hist_T(I['sc_conv'][l, s0:s0 + NS].rearrange("s j c -> (s j) c"), NS * 3, 4,
                         lambda c, psv: P.copy(V(csm[:, 1, c * 24:(c + 1) * 24], ('csm', 1)), psv))
        for j in range(3):
            psv = self.rows_mm(lambda kc, j=j: self.hT[:, kc, TP:T].rearrange("p (s j) -> p s j", j=4)[:, :, 1 + j],
                               NS, wx_t, wx_k, lambda t, kc: w512(t)[:, kc, :], 512)
            self.store_rows(O['s_c_conv'][l, s0:s0 + NS, j, :], psv, NS, 512)
        if blk == NBLK - 1:
            psv = self.rows_mm(lambda kc: self.hT[:, kc, TP - 3:TP], 3, wx_t, wx_k, lambda t, kc: w512(t)[:, kc, :], 512)
            self.store_rows(O['p_c_conv'][l], psv, 3, 512)
        wg_t, wg_k = self.wnext(('C_g', blk, l))
        for c in range(4):
            X, Xk = (R[0], 'R0') if c % 2 == 0 else (R[6], 'R6')
            P.copy(V(X[:, 0:3], Xk), V(self.tailC[:, l, c, :], 'tailC'))
            P.copy(V(X[:, 3 + TP:3 + TP + NS * 7].rearrange("p (s j) -> p s j", j=7)[:, :, 0:3], Xk),
                   V(csm[:, 1, c * 24:(c + 1) * 24].rearrange("p (s j) -> p s j", j=3), ('csm', 1)))
            self.proj(wx_t, wx_k, lambda t, kc, c=c: w512(t)[:, kc, c * 128:(c + 1) * 128], 128,
                      self.evac_to_X(X, Xk, 3))
            P.copy(V(self.tailC[:, l, c, :], 'tailC'), V(X[:, TP:TP + 3], Xk))
            xc, xck = R[1], 'R1'
            self.conv4(xc, xck, X, Xk, 'c_conv_w', lambda j, c=c: (l * 4 + j) * 4 + c)
            P.ts(V(xc[:, 0:T], xck), V(xc[:, 0:T], xck), self.pcol('c_conv_b', l * 4 + c), None, ALU.add)
            xcb, xcbk = Bt[2], 'B2'
            P.copy(V(xcb[:, 0:T], xcbk), V(xc[:, 0:T], xck), eng='act')
            cl = V(csm[:, 2, 0:1], ('csm', 2))
            cl2 = V(csm[:, 2, 1:2], ('csm', 2))
            P.act(cl, self.pcol('c_L', l * 4 + c), AF.Exp, scale=-1.0)
            P.act(cl, cl, AF.Ln, bias=self.onec)
            P.ts(cl2, cl, -16.0, None, ALU.mult)
            P.ts(cl, cl, -8.0, None, ALU.mult)
            ra, rak = R[2], 'R2'
            ri, rik = R[3], 'R3'
            for ti, (t0, t1) in enumerate(TT):
                n = t1 - t0
                pn, ps = P.ps()
                P.mm(V(ps[:, 0:n], pn), V(self.cgate[:, c, 0, :], 'cgate'), V(xcb[:, t0:t1], xcbk))
                P.act(V(ra[:, t0:t1], rak), V(ps[:, 0:n], pn), AF.Sigmoid, bias=self.pcol('c_ba', l * 4 + c))
                pn, ps = P.ps()
                P.mm(V(ps[:, 0:n], pn), V(self.cgate[:, c, 1, :], 'cgate'), V(xcb[:, t0:t1], xcbk))
                P.act(V(ri[:, t0:t1], rik), V(ps[:, 0:n], pn), AF.Sigmoid, bias=self.pcol('c_bx', l * 4 + c))
            s_, sk = R[4], 'R4'
            P.act(V(s_[:, 0:T], sk), V(ra[:, 0:T], rak), AF.Exp, scale=cl2)
            P.act(V(s_[:, 0:T], sk), V(s_[:, 0:T], sk), AF.Sqrt, scale=-1.0, bias=self.onec)
            P.act(V(ra[:, 0:T], rak), V(ra[:, 0:T], rak), AF.Exp, scale=cl)
            P.tt(V(ri[:, 0:T], rik), V(ri[:, 0:T], rik), V(xc[:, 0:T], xck), ALU.mult)
            P.tt(V(s_[:, 0:T], sk), V(s_[:, 0:T], sk), V(ri[:, 0:T], rik), ALU.mult)
            a_, ak = ra, rak
            a_s = V(a_[:, TP:T].rearrange("p (s j) -> p s j", j=4)[:, :, 0], ak)
            b_s = V(s_[:, TP:T].rearrange("p (s j) -> p s j", j=4)[:, :, 0], sk)
            h0s = V(csm[:, 0, c * 8:(c + 1) * 8], ('csm', 0))
            tmp8 = V(csm[:, 2, 8:16], ('csm', 2))
            P.tt(tmp8, a_s, h0s, ALU.mult)
            P.tt(b_s, b_s, tmp8, ALU.add)
            P.memset(a_s, 0.0, eng='dve')
            hh, hk = R[5], 'R5'
            P.scan(V(hh[:, 0:T], hk), V(a_[:, 0:T], ak), V(s_[:, 0:T], sk), V(self.hC[:, l, c:c + 1], 'hC'),
                   ALU.mult, ALU.add)
            P.copy(V(self.hC[:, l, c:c + 1], 'hC'), V(hh[:, TP - 1:TP], hk))
            P.copy(V(csm[:, 3, c * 8:(c + 1) * 8], ('csm', 3)),
                   V(hh[:, TP:T].rearrange("p (s j) -> p s j", j=4)[:, :, 3], hk))

            pss = []
            for ti, (t0, t1) in enumerate(TT):
                n = t1 - t0
                pn, ps = P.ps()
                for kc in range(KC):
                    P.mm(V(ps[:, 0:n], pn), V(w512(wg_t)[:, kc, c * 128:(c + 1) * 128], wg_k),
                         V(self.hT[:, kc, t0:t1], ('hT', ti)), start=(kc == 0), stop=(kc == KC - 1))
                pss.append(V(ps[:, 0:n], pn))

            def gg(ti, t0, t1, n, c=c, pss=pss):
                psv = pss[ti]
                tA = V(self.small[:, ti, 0:n], ('small', ti))
                P.act(tA, psv, AF.Square)
                yield
                P.ts(tA, tA, 0.044715, 1.0, ALU.mult, ALU.add)
                P.tt(tA, psv, tA, ALU.mult)
                yield
                P.act(tA, tA, AF.Sigmoid, scale=1.5957691216)
                yield
                P.tt(tA, V(hh[:, t0:t1], hk), tA, ALU.mult)
                P.tt(V(self.yb[:, 8 + c, t0:t1], ('yb', 8 + c, ti)), psv, tA, ALU.mult)
                yield
            self.ti_pipe(gg)
        pn, ps = P.ps()
        for c in range(4):
            P.tr(V(ps[0:NS, c * 128:(c + 1) * 128], pn), V(csm[:, 3, c * 8:(c + 1) * 8], ('csm', 3)),
                 V(self.identf[:], 'identf'))
        self.store_rows(O['s_c_h'][l, s0:s0 + NS, :], V(ps[0:NS, 0:512], pn), NS, 512)
        if blk == NBLK - 1:
            pn, ps = P.ps()
            P.tr(V(ps[0:4, 0:128], pn), V(self.hC[:, l, :], 'hC'), V(self.identf[:], 'identf'))
            self.store_rows(O['p_c_h'][l].rearrange("(c p) -> c p", p=128), V(ps[0:4, 0:128], pn), 4, 128)

    def branch_A(self, blk, l):
        P = self.P
        I, O = self.I, self.O
        R, Bt = self.R, self.Bt
        s0 = blk * NS
        identf = V(self.identf[:], 'identf')
        wb_t, wb_k = self.wnext(('A_ba', blk, l))
        wba = lambda t: t[:, 0:64].rearrange("p (kc n) -> p kc n", kc=8)
        rows, rowsk = R[1], 'R1'
        negA = V(self.a4[:, l, 0:1], 'a4')
        dtb = V(self.a4[:, l, 1:2], 'a4')
        one4 = V(self.onec.ap[0:4, :], 'consts')
        colsA = self.colsA

        self.proj(wb_t, wb_k, lambda t, kc: wba(t)[:, kc, 0:4], 4,
                  lambda ti, t0, t1, psv: P.act(V(rows[0:4, t0:t1], rowsk), psv, AF.Sigmoid))
        for c in range(NCHUNK):
            t0, n, nseq = chunk_info(c)
            pn, ps = P.ps()
            P.tr(V(ps[0:n, 0:4], pn), V(rows[0:4, t0:t0 + n], rowsk), V(self.identf[0:4, 0:4], 'identf'))
            P.copy(V(colsA[0:n, c, 0:4], ('colsA', c)), V(ps[0:n, 0:4], pn))

        def ev_g(ti, t0, t1, psv):
            gv = V(rows[0:4, t0:t1], rowsk)
            P.act(gv, psv, AF.Exp, bias=dtb)
            P.act(gv, gv, AF.Ln, bias=one4)
            P.ts(gv, gv, negA, None, ALU.mult)
        self.proj(wb_t, wb_k, lambda t, kc: wba(t)[:, kc, 4:8], 4, ev_g)
        for c in range(NCHUNK):
            t0, n, nseq = chunk_info(c)
            mi = 0 if c < 8 else 1
            ck = ('colsA', c)
            pn, ps = P.ps()
            P.tr(V(ps[0:n, 0:4], pn), V(rows[0:4, t0:t0 + n], rowsk), V(self.identf[0:4, 0:4], 'identf'))
            P.copy(V(colsA[0:n, c, 4:8], ck), V(ps[0:n, 0:4], pn))
            pn, ps = P.ps()
            P.mm(V(ps[0:n, 0:4], pn), V(self.m_uincl[0:n, mi, 0:n], 'm_uincl'), V(colsA[0:n, c, 4:8], ck))
            P.mm(V(ps[0:n, 4:8], pn), V(self.m_lstr[0:n, mi, 0:n], 'm_lstr'), V(colsA[0:n, c, 4:8], ck))
            P.copy(V(colsA[0:n, c, 8:16], ck), V(ps[0:n, 0:8], pn))
            P.act(V(colsA[0:n, c, 16:24], ck), V(colsA[0:n, c, 8:16], ck), AF.Exp)
            P.tt(V(colsA[0:n, c, 16:20], ck), V(colsA[0:n, c, 16:20], ck), V(colsA[0:n, c, 0:4], ck), ALU.mult)
            P.ts(V(colsA[0:n, c, 4:8], ck), V(colsA[0:n, c, 0:4], ck), -1.0, None, ALU.mult)

        self.astop(1)
        for hd in range(4):
            wt, wk = self.wnext(('A_head', blk, l, hd))
            wv = lambda t: t[:, :].rearrange("p (kc c n) -> p kc c n", kc=8, c=4)
            P.dma(V(self.Ss[:, :].rearrange("p (s e) -> p s e", s=NS), 'Ss'),
                  I['sa_S'][l, s0:s0 + NS, hd].rearrange("s d e -> d s e"), 'Ss')
            stg = self.stage[0]
            for cc in range(3):
                P.dma(V(stg[0:24, cc * 128:(cc + 1) * 128], 'stg0'),
                      I['sa_conv'][l, s0:s0 + NS].rearrange("s j c -> (s j) c")[:, cc * 512 + hd * 128:cc * 512 + (hd + 1) * 128],
                      'stg0')
            pnh, psh = P.ps()
            for cc in range(3):
                P.tr(V(psh[:, cc * 24:(cc + 1) * 24], pnh), V(stg[0:24, cc * 128:(cc + 1) * 128], 'stg0'),
                     V(self.identf[0:24, 0:24], 'identf'))
            hist = V(self.csm[:, 0, 0:72], ('csm', 0))
            P.copy(hist, V(psh[:, 0:72], pnh))
            Cs = [(R[3], 'R3'), (R[4], 'R4'), (R[5], 'R5')]
            for cc in range(3):
                X, Xk = (R[0], 'R0') if cc % 2 == 0 else (R[2], 'R2')
                P.copy(V(X[:, 3 + TP:3 + TP + NS * 7].rearrange("p (s j) -> p s j", j=7)[:, :, 0:3], Xk),
                       V(self.csm[:, 0, cc * 24:(cc + 1) * 24].rearrange("p (s j) -> p s j", j=3), ('csm', 0)))
                P.copy(V(X[:, 0:3], Xk), V(self.tailA[:, l, cc * 4 + hd, :], 'tailA'))
                self.proj(wt, wk, lambda t, kc, cc=cc: wv(t)[:, kc, cc, :], 128, self.evac_to_X(X, Xk, 3))
                P.copy(V(self.tailA[:, l, cc * 4 + hd, :], 'tailA'), V(X[:, TP:TP + 3], Xk))
                Cc, Ck = Cs[cc]
                self.conv4(Cc, Ck, X, Xk, 'a_conv_w', lambda j, cc=cc: (l * 4 + j) * 12 + cc * 4 + hd)
                P.act(V(Cc[:, 0:T], Ck), V(Cc[:, 0:T], Ck), AF.Silu)
            self.astop(2)
            for j in range(3):
                psv = self.rows_mm(lambda kc, j=j: self.hT[:, kc, TP:T].rearrange("p (s j) -> p s j", j=4)[:, :, 1 + j],
                                   NS, wt, wk, lambda t, kc: t[:, kc * 512:kc * 512 + 384], 384)
                P.copy(V(stg[0:NS, 0:384], 'stg0'), psv, eng='act')
                for cc in range(3):
                    P.dma(O['s_a_conv'][l, s0:s0 + NS, j, cc * 512 + hd * 128:cc * 512 + (hd + 1) * 128],
                          V(stg[0:NS, cc * 128:(cc + 1) * 128], 'stg0'), 'stg0')
            if blk == NBLK - 1:
                psv = self.rows_mm(lambda kc: self.hT[:, kc, TP - 3:TP], 3, wt, wk,
                                   lambda t, kc: t[:, kc * 512:kc * 512 + 384], 384)
                P.copy(V(stg[0:3, 0:384], 'stg0'), psv, eng='act')
                for cc in range(3):
                    P.dma(O['p_a_conv'][l, :, cc * 512 + hd * 128:cc * 512 + (hd + 1) * 128],
                          V(stg[0:3, cc * 128:(cc + 1) * 128], 'stg0'), 'stg0')
            self.astop(3)
            zg, zgk = R[6], 'R6'
            self.proj(wt, wk, lambda t, kc: wv(t)[:, kc, 3, :], 128,
                      lambda ti, t0, t1, psv: P.act(V(zg[:, t0:t1], zgk), psv, AF.Silu))
            qn, qnk = Bt[0], 'B0'
            kn, knk = Bt[1], 'B1'
            knf, knfk = Cs[1]
            vf, vfk = Cs[2]
            for which, (Cc, Ck) in enumerate(Cs[0:2]):
                def g(ti, t0, t1, n, which=which, Cc=Cc, Ck=Ck):
                    sq = V(self.sqts[ti][:, 0:n], self.sqtk[ti])
                    P.act(sq, V(Cc[:, t0:t1], (Ck, ti)), AF.Square)
                    yield
                    pn, ps = P.ps()
                    P.mm(V(ps[:, 0:n], pn), V(self.onesb[:], 'onesb'), sq)
                    yield
                    rs = V(self.small[:, ti, 0:n], ('small', ti))
                    P.act(rs, V(ps[:, 0:n], pn), AF.Ln, bias=self.epsc)
                    P.act(rs, rs, AF.Exp, scale=-0.5)
                    yield
                    if which == 0:
                        P.stt(V(qn[:, t0:t1], (qnk, ti)), V(Cc[:, t0:t1], (Ck, ti)), 128.0 ** -0.5, rs, ALU.mult, ALU.mult)
                    else:
                        P.tt(V(Cc[:, t0:t1], (Ck, ti)), V(Cc[:, t0:t1], (Ck, ti)), rs, ALU.mult)
                        P.copy(V(kn[:, t0:t1], (knk, ti)), V(Cc[:, t0:t1], (Ck, ti)), eng='act')
                    yield
                self.ti_pipe(g)
            lc, lck = R[7], 'R7'
            for ti, (t0, t1) in enumerate(TT):
                n = t1 - t0
                gmv = V(self.small[0:4, 2, 0:n], ('small', 2))
                P.ts(gmv, V(rows[0:4, t0:t1], rowsk), V(self.identf[0:4, hd:hd + 1], 'identf'), None, ALU.mult)
                pn, ps = P.ps()
                P.mm(V(ps[:, 0:n], pn), V(self.onesf[0:4, :], 'onesf'), gmv)
                P.copy(V(lc[:, t0:t1], lck), V(ps[:, 0:n], pn), eng='act')
            P.scan(V(lc[:, 0:T], lck), V(self.rmask[:, 0:T], 'rmask'), V(lc[:, 0:T], lck), 0.0, ALU.mult, ALU.add)
            qg, qgk = Bt[2], 'B2'
            for ti, (t0, t1) in enumerate(TT):
                n = t1 - t0
                ev = V(self.small[:, 3, 0:n], ('small', 3))
                P.act(ev, V(lc[:, t0:t1], lck), AF.Exp)
                P.tt(V(qg[:, t0:t1], qgk), V(qn[:, t0:t1], qnk), ev, ALU.mult)
            P.act(V(self.glc[:, 0:8], 'glc'), V(lc[:, 0:TP].rearrange("p (c n) -> p c n", n=128)[:, :, 127], lck), AF.Exp)
            P.act(V(self.glc[:, 8:16], 'glc'), V(lc[:, TP:T].rearrange("p (s j) -> p s j", j=4)[:, :, 3], lck), AF.Exp)
            self.astop(4)
            rw, rwk = Bt[4], 'B4'
            kd, kdk = Bt[5], 'B5'
            rv, rvk = Bt[6], 'B6'
            aT, aTk = Bt[7], 'B7'
            nw, nwk = Bt[8], 'B8'
            Xb, Xbk = Bt[9], 'B9'
            for c in range(NCHUNK):
                t0, n, nseq = chunk_info(c)
                co = c * 128
                ck = ('colsA', c)
                pn, ps = P.ps()
                P.tr(V(ps[0:n, 0:128], pn), V(knf[:, t0:t0 + n], knfk), identf)
                P.tr(V(ps[0:n, 128:256], pn), V(vf[:, t0:t0 + n], vfk), identf)
                P.ts(V(rw[0:n, co:co + 128], rwk), V(ps[0:n, 0:128], pn), V(colsA[0:n, c, 16 + hd:17 + hd], ck), None, ALU.mult)
                P.act(V(kd[0:n, co:co + 128], kdk), V(ps[0:n, 0:128], pn), AF.Identity, scale=V(colsA[0:n, c, 20 + hd:21 + hd], ck))
                P.act(V(rv[0:n, co:co + 128], rvk), V(ps[0:n, 128:256], pn), AF.Identity, scale=V(colsA[0:n, c, hd:hd + 1], ck))
            self.astop(5)
            nsets = [(R[0][:, 0:512], 'R0', R[0][:, 512:1024], 'R0', self.c128f[0], 'cf0', self.c128b[2], 'cb2',
                      self.c128f[1], 'cf1', self.c128f[2], 'cf2'),
                     (R[2][:, 0:512], 'R2', R[2][:, 512:1024], 'R2', R[3][:, 512:1024], ('R3', 1), self.c128b[0], 'cb0',
                      self.c128f[3], 'cf3', self.c128f[4], 'cf4')]
            def gen_GN(cs, G, ns):
                Nb, Nbk, NTb, NTbk, X32, X32k, XbT, XbTk, d1t, d1k, d2t, d2k = ns
                n = 128 if G == 4 else TS
                mi = 0 if G == 4 else 1
                for gi, c in enumerate(cs):
                    t0 = chunk_info(c)[0]
                    ck = ('colsA', c)
                    sl = slice(gi * n, (gi + 1) * n)
                    pn, ps = P.ps()
                    P.mm(V(ps[0:n, 0:n], (pn, 0)), V(kn[:, t0:t0 + n], knk), V(kn[:, t0:t0 + n], knk))
                    P.mm(V(ps[0:n, 128:128 + n], (pn, 1)), V(kn[:, t0:t0 + n], knk), V(qn[:, t0:t0 + n], qnk))
                    d1 = V(d1t[0:n, 0:n], d1k)
                    P.stt(d1, V(lc[0:n, t0:t0 + n], lck), V(colsA[0:n, c, 8 + hd:9 + hd], ck),
                          V(self.m_bigL[0:n, mi, 0:n], 'm_bigL'), ALU.subtract, ALU.max)
                    P.act(d1, d1, AF.Exp, scale=-1.0)
                    P.stt(V(Nb[0:n, sl], Nbk), V(ps[0:n, 0:n], (pn, 0)), V(colsA[0:n, c, 4 + hd:5 + hd], ck), d1,
                          ALU.mult, ALU.mult)
                    d2 = V(d2t[0:n, 0:n], d2k)
                    P.stt(d2, V(lc[0:n, t0:t0 + n], lck), V(colsA[0:n, c, 8 + hd:9 + hd], ck),
                          V(self.m_negU[0:n, mi, 0:n], 'm_negU'), ALU.subtract, ALU.min)
                    P.act(d2, d2, AF.Exp)
                    P.tt(V(aT[0:n, c * 128:c * 128 + n], (aTk, c)), V(ps[0:n, 128:128 + n], (pn, 1)), d2, ALU.mult)
                    yield
                pn, ps = P.ps()
                for gi in range(G):
                    sl = slice(gi * n, (gi + 1) * n)
                    P.tr(V(ps[0:n, sl], pn), V(Nb[0:n, sl], Nbk), V(self.identf[0:n, 0:n], 'identf'))
                P.copy(V(NTb[0:n, 0:G * n], NTbk), V(ps[0:n, 0:G * n], pn))
                yield
                yield from self.neumann(Nb, NTb, n, G, NEU_LEVELS_P if G == 4 else NEU_LEVELS_S, X32, XbT,
                                        (Nbk, NTbk, X32k, XbTk), fp32=True)
                for gi, c in enumerate(cs):
                    sl = slice(gi * n, (gi + 1) * n)
                    P.copy(V(Xb[0:n, c * 128:c * 128 + n], (Xbk, c)), V(XbT[0:n, sl], XbTk), eng='dve')
                pn, ps = P.ps()
                for gi, c in enumerate(cs):
                    P.mm(V(ps[:, gi * n:(gi + 1) * n], pn), V(rw[0:n, c * 128:(c + 1) * 128], rwk),
                         V(XbT[0:n, gi * n:(gi + 1) * n], XbTk))
                t0 = chunk_info(cs[0])[0]
                P.act(V(nw[:, t0:t0 + G * n], *[(nwk, c_) for c_ in cs]), V(ps[:, 0:G * n], pn), AF.Identity, scale=-1.0)
                yield
            self.astop(6)
            oT, oTk = R[3], 'R3'
            S32 = V(self.SA[:, l, hd, :], ('SA', l, hd))
            Sb = V(self.SAb[:, hd, :], ('SAb', hd))
            P.copy(Sb, S32, eng='act')
            vn, vnk = self.c128b[3], 'cb3'
            def gen_chain(c_list):
              for c in c_list:
                t0 = c * 128
                co = c * 128
                pn1, ps1 = P.ps()
                pv = V(ps1[:, 0:128], pn1)
                P.mm(pv, V(Xb[:, co:co + 128], (Xbk, c)), V(rv[:, co:co + 128], rvk), start=True, stop=False)
                P.mm(pv, V(nw[:, t0:t0 + 128], (nwk, c)), Sb, start=False, stop=True)
                vnv = V(vn[:, (c % 2) * 128:(c % 2 + 1) * 128], (vnk, c % 2))
                P.copy(vnv, pv, eng='act')
                yield
                pn2, ps2 = P.ps()
                po = V(ps2[:, 0:128], pn2)
                P.mm(po, Sb, V(qg[:, t0:t0 + 128], qgk), start=True, stop=False)
                P.mm(po, vnv, V(aT[:, co:co + 128], (aTk, c)), start=False, stop=True)
                P.copy(V(oT[:, t0:t0 + 128], (oTk, c // 4)), po, eng='dve')
                yield
                pn3, ps3 = P.ps()
                pS = V(ps3[:, 0:128], pn3)
                P.mm(pS, V(kd[:, co:co + 128], kdk), vnv)
                P.stt(Sb, S32, V(self.glc[:, c:c + 1], 'glc'), pS, ALU.mult, ALU.add)
                P.stt(S32, S32, V(self.glc[:, c:c + 1], 'glc'), pS, ALU.mult, ALU.add)
                yield
            self.astop(61)
            pipe = Pipe()
            g0 = gen_GN((0, 1, 2, 3), 4, nsets[0])
            g1 = gen_GN((4, 5, 6, 7), 4, nsets[1])
            pipe.add(g0)
            pipe.add(g1)
            pipe.finish(g0)
            self.astop(62)
            pipe.run_with(gen_chain([0, 1, 2, 3]))
            pipe.finish(g1)
            self.astop(63)
            pipe.add(gen_GN((8,), 1, nsets[0]))
            pipe.run_with(gen_chain([4, 5, 6, 7]))
            pipe.drain_all()
            if blk == NBLK - 1:
                P.dma(O['p_a_S'][l, hd], S32, ('SA', l, hd))
            self.astop(7)
            c = 8
            t0 = TP
            co = c * 128
            Ss = V(self.Ss[:, :], 'Ss')
            ssb_t = R[2][:, 0:512].bitcast(BF16)
            Ssb = V(ssb_t, 'R2')
            P.copy(Ssb, Ss, eng='act')
            ex, exk = R[0], 'R0'
            red1 = V(self.c128f[3][0:TS, 0:128], 'cf3')
            red2 = V(self.c128f[4][0:TS, 0:128], 'cf4')

            def state_apply(lhsT_v, out_red):
                pn1, ps1 = P.ps()
                pn2, ps2 = P.ps()
                P.mm(V(ps1[0:TS, 0:512], pn1), lhsT_v, V(ssb_t[:, 0:512], 'R2'))
                P.mm(V(ps2[0:TS, 0:512], pn2), lhsT_v, V(ssb_t[:, 512:1024], 'R2'))
                P.tt(V(ex[0:TS, 0:512].rearrange("p (s e) -> p s e", s=4), exk),
                     V(ps1[0:TS, 0:512].rearrange("p (s e) -> p s e", s=4), pn1),
                     V(self.seqm[:, 0:4].unsqueeze(2).to_broadcast([TS, 4, 128]), 'seqm'), ALU.mult)
                P.tt(V(ex[0:TS, 512:1024].rearrange("p (s e) -> p s e", s=4), exk),
                     V(ps2[0:TS, 0:512].rearrange("p (s e) -> p s e", s=4), pn2),
                     V(self.seqm[:, 4:8].unsqueeze(2).to_broadcast([TS, 4, 128]), 'seqm'), ALU.mult)
                P.red(out_red, V(ex[0:TS, 0:1024].rearrange("p (s e) -> p e s", s=NS), exk))
            state_apply(V(nw[:, t0:t0 + TS], (nwk, 8)), red1)
            pn, ps = P.ps()
            P.mm(V(ps[0:TS, 0:128], pn), V(Xb[0:TS, co:co + TS], (Xbk, 8)), V(rv[0:TS, co:co + 128], rvk))
            vns32 = V(self.c128f[5][0:TS, 0:128], 'cf5')
            P.tt(vns32, V(ps[0:TS, 0:128], pn), red1, ALU.add)
            vns = V(vn[0:TS, 256:384], (vnk, 2))
            P.copy(vns, vns32, eng='act')
            state_apply(V(qg[:, t0:t0 + TS], qgk), red2)
            pn, ps = P.ps()
            P.mm(V(ps[0:TS, 0:128], pn), V(aT[0:TS, co:co + TS], (aTk, 8)), vns)
            P.tt(red2, V(ps[0:TS, 0:128], pn), red2, ALU.add)
            pn, ps = P.ps()
            P.tr(V(ps[:, 0:TS], pn), red2, V(self.identf[0:TS, 0:TS], 'identf'))
            P.copy(V(oT[:, t0:t0 + TS], oTk), V(ps[:, 0:TS], pn), eng='act')
            vex_t = R[0][0:TS, 0:512].bitcast(BF16)
            P.tt(V(vex_t.rearrange("p (s e) -> p s e", s=NS), exk),
                 V(vns32.ap.unsqueeze(1).to_broadcast([TS, NS, 128]), 'cf5'),
                 V(self.seqm[:, :].unsqueeze(2).to_broadcast([TS, NS, 128]), 'seqm'), ALU.mult)
            for hf in range(2):
                pn, ps = P.ps()
                P.mm(V(ps[:, 0:512], pn), V(kd[0:TS, co:co + 128], kdk), V(vex_t[:, hf * 512:(hf + 1) * 512], exk))
                ssl = V(self.Ss[:, hf * 512:(hf + 1) * 512].rearrange("p (s e) -> p s e", s=4), 'Ss')
                glb = V(self.glc[:, 8 + hf * 4:8 + (hf + 1) * 4].unsqueeze(2).to_broadcast([128, 4, 128]), 'glc')
                P.tt(ssl, ssl, glb, ALU.mult)
                P.tt(ssl, ssl, V(ps[:, 0:512].rearrange("p (s e) -> p s e", s=4), pn), ALU.add)
            P.dma(O['s_a_S'][l, s0:s0 + NS, hd].rearrange("s d e -> d s e"),
                  V(self.Ss[:, :].rearrange("p (s e) -> p s e", s=NS), 'Ss'), 'Ss')
            self.astop(8)
            def g8(ti, t0, t1, n):
                sq = V(self.sqts[ti][:, 0:n], self.sqtk[ti])
                P.act(sq, V(oT[:, t0:t1], oTk), AF.Square)
                yield
                pn, ps = P.ps()
                P.mm(V(ps[:, 0:n], pn), V(self.onesb[:], 'onesb'), sq)
                yield
                rs = V(self.small[:, ti, 0:n], ('small', ti))
                P.act(rs, V(ps[:, 0:n], pn), AF.Ln, bias=self.epsc, scale=1.0 / 128)
                P.act(rs, rs, AF.Exp, scale=-0.5)
                yield
                tmp = V(self.small[:, 3, 0:n], ('small', 3))
                P.stt(tmp, V(oT[:, t0:t1], oTk), self.pcol('a_norm_g', l), rs, ALU.mult, ALU.mult)
                P.tt(V(self.yb[:, hd, t0:t1], ('yb', hd, ti)), tmp, V(zg[:, t0:t1], zgk), ALU.mult)
                yield
            self.ti_pipe(g8)

    def branch_B(self, blk, l):
        P = self.P
        I, O = self.I, self.O
        R, Bt = self.R, self.Bt
        s0 = blk * NS
        identf = V(self.identf[:], 'identf')
        csm = self.csm
        P.dma(V(self.lorW[0:64, :], 'lorW'), I['b_w_up'][l], 'lorW', eng='pool')
        P.dma(V(self.lorW[64:128, :], 'lorW'), I['b_a_up'][l], 'lorW', eng='pool')
        P.dma(V(self.gupW[:, :], 'gupW'), I['b_g_up'][l], 'gupW', eng='pool')
        for p in range(4):
            P.ts(V(self.rkbd[:, p, :], 'rkbd'), V(self.ones64f[:], 'ones64f'), self.pcol('b_r_k', l * 4 + p), None, ALU.mult)
        stg = self.stage[0]
        for pc in range(4):
            c0 = pc * 512
            w = min(512, 1792 - c0)
            P.dma(V(stg[0:NS, 0:w], 'stg0'), I['sb_shift'][l, s0:s0 + NS, 0, c0:c0 + w], 'stg0')
            for cc in range(w // 128):
                ch = pc * 4 + cc
                pn, ps = P.ps()
                P.tr(V(ps[:, 0:NS], pn), V(stg[0:NS, cc * 128:(cc + 1) * 128], 'stg0'), V(self.identf[0:NS, 0:NS], 'identf'))
                if ch < 12:
                    P.copy(V(csm[:, 0, ch * 8:(ch + 1) * 8], ('csm', 0)), V(ps[:, 0:NS], pn))
                else:
                    P.copy(V(csm[:, 1, (ch - 12) * 8:(ch - 11) * 8], ('csm', 1)), V(ps[:, 0:NS], pn))

        def hist_of(ch):
            if ch < 12:
                return V(csm[:, 0, ch * 8:(ch + 1) * 8], ('csm', 0))
            return V(csm[:, 1, (ch - 12) * 8:(ch - 11) * 8], ('csm', 1))

        ZB = 1 + TP
        zsel = [0]

        def shift_proj(wt, wk, colsel, ch, out_t, out_k):
            Z, Zk = (R[0], 'R0') if zsel[0] % 2 == 0 else (R[4], 'R4')
            zsel[0] += 1
            P.copy(V(Z[:, 0:1], Zk), V(self.tailB[:, l, ch:ch + 1], 'tailB'))
            P.copy(V(Z[:, ZB:ZB + NS * 5].rearrange("p (s j) -> p s j", j=5)[:, :, 0], Zk), hist_of(ch))
            self.proj(wt, wk, colsel, 128, self.evac_to_X(Z, Zk, 1))
            P.copy(V(self.tailB[:, l, ch:ch + 1], 'tailB'), V(Z[:, TP:TP + 1], Zk))
            mu = self.pcol('b_mu', l * 14 + ch)
            P.tt(V(out_t[:, 0:TP], out_k), V(Z[:, 0:TP], Zk), V(Z[:, 1:1 + TP], Zk), ALU.subtract)
            P.stt(V(out_t[:, 0:TP], out_k), V(out_t[:, 0:TP], out_k), mu, V(Z[:, 1:1 + TP], Zk), ALU.mult, ALU.add)
            zs3 = Z[:, ZB:ZB + NS * 5].rearrange("p (s j) -> p s j", j=5)
            o3 = V(out_t[:, TP:T].rearrange("p (s j) -> p s j", j=4), out_k)
            P.tt(o3, V(zs3[:, :, 0:4], Zk), V(zs3[:, :, 1:5], Zk), ALU.subtract)
            P.stt(o3, o3, mu, V(zs3[:, :, 1:5], Zk), ALU.mult, ALU.add)

        wl_t, wl_k = self.wnext(('B_lora', blk, l))
        wl = lambda t: t[:, 0:2048].rearrange("p (kc n) -> p kc n", kc=8)
        lx, lxk = Bt[0], 'B0'
        sgx, sgxk = Bt[1], 'B1'
        tmpz, tmpzk = R[1], 'R1'
        shift_proj(wl_t, wl_k, lambda t, kc: wl(t)[:, kc, 0:128], 12, tmpz, tmpzk)
        P.act(V(lx[0:64, 0:T], lxk), V(tmpz[0:64, 0:T], tmpzk), AF.Tanh)
        P.copy(V(lx[64:128, 0:T], lxk), V(tmpz[64:128, 0:T], tmpzk), eng='act')
        shift_proj(wl_t, wl_k, lambda t, kc: wl(t)[:, kc, 128:256], 13, tmpz, tmpzk)
        P.act(V(sgx[:, 0:T], sgxk), V(tmpz[:, 0:T], tmpzk), AF.Sigmoid)
        psv = self.rows_mm(lambda kc: self.hT[:, kc, TP:T].rearrange("p (s j) -> p s j", j=4)[:, :, 3], NS, wl_t, wl_k,
                           lambda t, kc: wl(t)[:, kc, :], 256)
        self.store_rows(O['s_b_shift'][l, s0:s0 + NS, 0, 1536:1792], psv, NS, 256)
        if blk == NBLK - 1:
            psv = self.rows_mm(lambda kc: self.hT[:, kc, TP - 1:TP], 1, wl_t, wl_k, lambda t, kc: wl(t)[:, kc, :], 256)
            self.store_rows(O['p_b_shift'][l, :, 1536:1792], psv, 1, 256)

        cb = self.c128b
        cf = self.c128f
        for p in range(4):
            wt, wk = self.wnext(('B_pair', blk, l, p))
            wv = lambda t: t[:, 0:3072].rearrange("p (kc c n) -> p kc c n", kc=8, c=3)
            stS = R[5]
            for h_ in range(2):
                P.dma(V(stS[0:64, 0:1024].rearrange("p (s h k) -> p s h k", s=NS, h=2)[:, :, h_, :], 'R5'),
                      I['sb_S'][l, s0:s0 + NS, 2 * p + h_].rearrange("s v k -> v s k"), 'R5')
            if hasattr(self, 'marks'):
                self.marks.append(('  b-proj', P.cnt['pe'], P.cnt['act'], P.cnt['dve']))
            r32, r32k = R[1], 'R1'
            k32, k32k = R[2], 'R2'
            v32, v32k = R[3], 'R3'
            shift_proj(wt, wk, lambda t, kc: wv(t)[:, kc, 0, :], p, r32, r32k)
            shift_proj(wt, wk, lambda t, kc: wv(t)[:, kc, 1, :], 4 + p, k32, k32k)
            shift_proj(wt, wk, lambda t, kc: wv(t)[:, kc, 2, :], 8 + p, v32, v32k)
            pn, ps = P.ps()
            for sq_ in range(NS):
                P.tr(V(ps[:, sq_ * 64:(sq_ + 1) * 64], pn), V(stS[0:64, sq_ * 128:(sq_ + 1) * 128], 'R5'),
                     V(self.identf[0:64, 0:64], 'identf'))
            Hs = V(self.Ss[:, 0:512], 'Ss')
            P.copy(Hs, V(ps[:, 0:512], pn))
            Hsb_t = cb[7]
            Hsb = V(Hsb_t[:, 0:512], 'cb7')
            P.copy(Hsb, Hs, eng='act')
            psv = self.rows_mm(lambda kc: self.hT[:, kc, TP:T].rearrange("p (s j) -> p s j", j=4)[:, :, 3], NS, wt, wk,
                               lambda t, kc: t[:, kc * 384:(kc + 1) * 384], 384)
            P.copy(V(stg[0:NS, 0:384], 'stg0'), psv, eng='act')
            for cc in range(3):
                P.dma(O['s_b_shift'][l, s0:s0 + NS, 0, cc * 512 + p * 128:cc * 512 + (p + 1) * 128],
                      V(stg[0:NS, cc * 128:(cc + 1) * 128], 'stg0'), 'stg0')
            if blk == NBLK - 1:
                psv = self.rows_mm(lambda kc: self.hT[:, kc, TP - 1:TP], 1, wt, wk,
                                   lambda t, kc: t[:, kc * 384:(kc + 1) * 384], 384)
                P.copy(V(stg[0:1, 0:384], 'stg0'), psv, eng='act')
                for cc in range(3):
                    P.dma(O['p_b_shift'][l, :, cc * 512 + p * 128:cc * 512 + (p + 1) * 128],
                          V(stg[0:1, cc * 128:(cc + 1) * 128], 'stg0'), 'stg0')
            if hasattr(self, 'marks'):
                self.marks.append(('  b-lwag', P.cnt['pe'], P.cnt['act'], P.cnt['dve']))
            lw, lwk = R[4], 'R4'
            a32, a32k = R[5], 'R5'
            gb, gbk = Bt[9], 'B9'
            for ti, (t0, t1) in enumerate(TT):
                n = t1 - t0
                pn, ps = P.ps()
                P.mm(V(ps[:, 0:n], pn), V(self.lorW[0:64, p * 128:(p + 1) * 128], 'lorW'), V(lx[0:64, t0:t1], lxk))
                P.act(V(lw[:, t0:t1], lwk), V(ps[:, 0:n], pn), AF.Sigmoid, bias=self.pcol('b_w0', l * 4 + p))
                pn, ps = P.ps()
                P.mm(V(ps[:, 0:n], pn), V(self.lorW[64:128, p * 128:(p + 1) * 128], 'lorW'), V(lx[64:128, t0:t1], lxk))
                P.act(V(a32[:, t0:t1], a32k), V(ps[:, 0:n], pn), AF.Sigmoid, bias=self.pcol('b_a0', l * 4 + p))
                pn, ps = P.ps()
                P.mm(V(ps[:, 0:n], pn), V(self.gupW[:, p * 128:(p + 1) * 128], 'gupW'), V(sgx[:, t0:t1], sgxk))
                P.copy(V(gb[:, t0:t1], gbk), V(ps[:, 0:n], pn), eng='act')
            P.ts(V(lw[:, 0:T], lwk), V(lw[:, 0:T], lwk), -0.6065306597126334, None, ALU.mult)
            kkn, kknk = R[7], 'R7'
            kkc = self.pcol('b_k_k', l * 4 + p)
            def gk(ti, t0, t1, n):
                sq = V(self.sqts[ti][:, 0:n], self.sqtk[ti])
                P.act(sq, V(k32[:, t0:t1], k32k), AF.Square, scale=kkc)
                yield
                pn, ps = P.ps()
                P.mm(V(ps[:, 0:n], pn), V(self.ones64b[:], 'ones64b'), sq)
                yield
                rs = V(self.small[:, ti, 0:n], ('small', ti))
                P.act(rs, V(ps[:, 0:n], pn), AF.Ln, bias=self.epsc)
                P.act(rs, rs, AF.Exp, scale=-0.5)
                yield
                P.stt(V(kkn[:, t0:t1], (kknk, ti)), V(k32[:, t0:t1], k32k), kkc, rs, ALU.mult, ALU.mult)
                yield
            self.ti_pipe(gk)
            if hasattr(self, 'marks'):
                self.marks.append(('  b-elem', P.cnt['pe'], P.cnt['act'], P.cnt['dve']))
            kac = self.pcol('b_k_a', l * 4 + p)
            omk = V(csm[:, 2, 0:1], ('csm', 2))
            P.ts(omk, kac, -1.0, 1.0, ALU.mult, ALU.add)
            tmp, tmpk = R[6], 'R6'
            P.ts(V(tmp[:, 0:T], tmpk), V(a32[:, 0:T], a32k), kac, omk, ALU.mult, ALU.add)
            P.tt(V(k32[:, 0:T], k32k), V(k32[:, 0:T], k32k), V(tmp[:, 0:T], tmpk), ALU.mult)
            bon, bonk = R[6], 'R6'
            for ti, (t0, t1) in enumerate(TT):
                n = t1 - t0
                sq = V(self.sqt[:, 0:n], 'sqt')
                P.tt(sq, V(r32[:, t0:t1], r32k), V(k32[:, t0:t1], k32k), ALU.mult)
                pn, ps = P.ps()
                P.mm(V(ps[:, 0:n], pn), V(self.rkbd[:, p, :], 'rkbd'), sq)
                P.tt(V(bon[:, t0:t1], bonk), V(ps[:, 0:n], pn), V(v32[:, t0:t1], v32k), ALU.mult)
            lc, lck = R[0], 'R0'
            P.scan(V(lc[:, 0:T], lck), V(self.rmask[:, 0:T], 'rmask'), V(lw[:, 0:T], lwk), 0.0, ALU.mult, ALU.add)
            P.tt(V(lw[:, 0:T], lwk), V(lc[:, 0:T], lck), V(lw[:, 0:T], lwk), ALU.subtract)
            qt, qtk = Bt[2], 'B2'
            at, atk = Bt[3], 'B3'
            bt_, btk = Bt[4], 'B4'
            kt, ktk = Bt[5], 'B5'
            bh, bhk = R[1], 'R1'
            for ti, (t0, t1) in enumerate(TT):
                n = t1 - t0
                ev = V(self.small[:, 2, 0:n], ('small', 2))
                P.act(ev, V(lc[:, t0:t1], lck), AF.Exp)
                P.tt(V(qt[:, t0:t1], qtk), V(r32[:, t0:t1], r32k), ev, ALU.mult)
                ev2 = V(self.small[:, 3, 0:n], ('small', 3))
                P.act(ev2, V(lw[:, t0:t1], lwk), AF.Exp)
                P.stt(V(at[:, t0:t1], atk), V(kkn[:, t0:t1], kknk), -1.0, ev2, ALU.mult, ALU.mult)
            P.tt(V(bh[:, 0:T], bhk), V(kkn[:, 0:T], kknk), V(a32[:, 0:T], a32k), ALU.mult)
            for ti, (t0, t1) in enumerate(TT):
                n = t1 - t0
                ev = V(self.small[:, 2, 0:n], ('small', 2))
                P.act(ev, V(lc[:, t0:t1], lck), AF.Exp, scale=-1.0)
                P.tt(V(bt_[:, t0:t1], btk), V(bh[:, t0:t1], bhk), ev, ALU.mult)
                P.tt(V(kt[:, t0:t1], ktk), V(k32[:, t0:t1], k32k), ev, ALU.mult)
            P.act(V(self.glc[:, 0:8], 'glc'), V(lc[:, 0:TP].rearrange("p (c n) -> p c n", n=128)[:, :, 127], lck), AF.Exp)
            P.act(V(self.glc[:, 8:16], 'glc'), V(lc[:, TP:T].rearrange("p (s j) -> p s j", j=4)[:, :, 3], lck), AF.Exp)
            if hasattr(self, 'marks'):
                self.marks.append(('  b-tm', P.cnt['pe'], P.cnt['act'], P.cnt['dve']))
            bd, bdk = Bt[6], 'B6'
            kdt, kdtk = Bt[7], 'B7'
            vt, vtk = Bt[8], 'B8'
            for c in range(NCHUNK):
                t0, n, nseq = chunk_info(c)
                co = c * 128
                if c % 2 == 0:
                    edt, edk, f1t, f1k, f2t, f2k = cf[1], 'cf1', cf[2], 'cf2', cf[3], 'cf3'
                else:
                    edt, edk, f1t, f1k, f2t, f2k = cf[4], 'cf4', cf[5], 'cf5', cf[0], 'cf0'
                ed = V(edt[:, 0:n], edk)
                if c < 8:
                    P.act(ed, V(lc[:, t0:t0 + n], lck), AF.Exp, scale=-1.0, bias=V(lc[:, t0 + n - 1:t0 + n], lck))
                else:
                    lc3 = lc[:, TP:T].rearrange("p (s j) -> p s j", j=4)
                    P.tt(V(edt[:, 0:n].rearrange("p (s j) -> p s j", j=4), edk),
                         V(lc3[:, :, 3:4].to_broadcast([128, NS, 4]), lck), V(lc3, lck), ALU.subtract)
                    P.act(ed, ed, AF.Exp)
                f1 = V(f1t[:, 0:n], f1k)
                f2 = V(f2t[:, 0:n], f2k)
                P.tt(f1, V(bh[:, t0:t0 + n], bhk), ed, ALU.mult)
                P.tt(f2, V(k32[:, t0:t0 + n], k32k), ed, ALU.mult)
                pn, ps = P.ps()
                P.tr(V(ps[0:n, 0:128], pn), f1, identf)
                P.tr(V(ps[0:n, 128:256], pn), f2, identf)
                P.tr(V(ps[0:n, 256:384], pn), V(v32[:, t0:t0 + n], v32k), identf)
                P.copy(V(bd[0:n, co:co + 128], bdk), V(ps[0:n, 0:128], pn), eng='act')
                P.copy(V(kdt[0:n, co:co + 128], kdtk), V(ps[0:n, 128:256], pn), eng='act')
                P.copy(V(vt[0:n, co:co + 128], vtk), V(ps[0:n, 256:384], pn), eng='act')
            if hasattr(self, 'marks'):
                self.marks.append(('  b-groups', P.cnt['pe'], P.cnt['act'], P.cnt['dve']))
            oT, oTk = R[4], 'R4'
            H32 = V(self.HB[:, l, p, :], ('HB', l, p))
            Hb = V(self.HBb[:, p, :], ('HBb', p))
            P.copy(Hb, H32, eng='act')
            xc_t, xck_ = cb[6], 'cb6'
            def gen_GN(cs, bs, nsb):
                XbT, XbTk, LkT, LkTk, AqbT, AqbTk, AqkT, AqkTk = bs
                Nb, Nbk, NTb, NTbk, X32, X32k = nsb
                n = 128 if len(cs) == 2 else TS
                mi = 0 if len(cs) == 2 else 1
                G = 2 * len(cs)
                for (gi_type, (Lt, Ltk, Rt, Rtk, mask, mk, dst, dstk)) in enumerate([
                        (at, atk, bt_, btk, self.m_lstr, 'm_lstr', Nb, Nbk),
                        (kt, ktk, at, atk, self.m_ustr, 'm_ustr', LkT, LkTk),
                        (bt_, btk, qt, qtk, self.m_uincl, 'm_uincl', AqbT, AqbTk),
                        (kt, ktk, qt, qtk, self.m_uincl, 'm_uincl', AqkT, AqkTk)]):
                    for hp in range(2):
                        pn, ps = P.ps()
                        hs = slice(hp * 64, (hp + 1) * 64)
                        for ci, c in enumerate(cs):
                            t0 = chunk_info(c)[0]
                            P.mm(V(ps[0:n, ci * n:(ci + 1) * n], pn), V(Lt[hs, t0:t0 + n], Ltk), V(Rt[hs, t0:t0 + n], Rtk))
                        nci = len(cs)
                        dv = dst[0:n, 0:G * n].rearrange("p (ci hp n) -> p ci hp n", ci=nci, hp=2)[:, :, hp, :]
                        P.tt(V(dv, dstk), V(ps[0:n, 0:nci * n].rearrange("p (ci n) -> p ci n", ci=nci), pn),
                             V(mask[0:n, mi, 0:n].unsqueeze(1).to_broadcast([n, nci, n]), mk), ALU.mult)
                        yield
                pn, ps = P.ps()
                psb = ps[:, :].bitcast(BF16)
                for g in range(G):
                    sl = slice(g * n, (g + 1) * n)
                    P.tr(V(psb[0:n, sl], pn), V(Nb[0:n, sl], Nbk), V(self.identb[0:n, 0:n], 'identb'))
                P.copy(V(NTb[0:n, 0:G * n], NTbk), V(psb[0:n, 0:G * n], pn))
                yield
                yield from self.neumann(Nb, NTb, n, G, NEU_LEVELS_P if mi == 0 else NEU_LEVELS_S, X32, XbT, (Nbk, NTbk, X32k, XbTk))
            def gen_chain(cs, bs):
                XbT, XbTk, LkT, LkTk, AqbT, AqbTk, AqkT, AqkTk = bs
                for ci, c in enumerate(cs):
                    t0 = chunk_info(c)[0]
                    co = c * 128
                    if c < 8:
                        xcv = V(xc_t[:, 0:128], xck_)
                        for hp in range(2):
                            g = ci * 2 + hp
                            hs = slice(hp * 64, (hp + 1) * 64)
                            pn, ps = P.ps()
                            P.mm(V(ps[:, 0:64], pn), V(at[hs, t0:t0 + 128], atk), V(self.HBb[hs, p, :], ('HBb', p)),
                                 start=True, stop=False)
                            P.mm(V(ps[:, 0:64], pn), V(LkT[:, g * 128:(g + 1) * 128], LkTk), V(vt[:, co + hp * 64:co + (hp + 1) * 64], vtk),
                                 start=False, stop=True)
                            P.copy(V(xc_t[:, hp * 64:(hp + 1) * 64], xck_), V(ps[:, 0:64], pn), eng='act' if hp else 'dve')
                        yield
                        pn, ps = P.ps()
                        for hp in range(2):
                            g = ci * 2 + hp
                            P.mm(V(ps[:, hp * 64:(hp + 1) * 64], pn), V(XbT[:, g * 128:(g + 1) * 128], XbTk),
                                 V(xc_t[:, hp * 64:(hp + 1) * 64], xck_))
                        uv = V(xc_t[:, 128:256], xck_)
                        P.copy(uv, V(ps[:, 0:128], pn), eng='act')
                        yield
                        for hp in range(2):
                            g = ci * 2 + hp
                            hs = slice(hp * 64, (hp + 1) * 64)
                            pn, ps = P.ps()
                            po = V(ps[hs, 0:128], pn)
                            P.mm(po, V(self.HBb[hs, p, :], ('HBb', p)), V(qt[hs, t0:t0 + 128], qtk), start=True, stop=False)
                            P.mm(po, V(xc_t[:, 128 + hp * 64:128 + (hp + 1) * 64], xck_), V(AqbT[:, g * 128:(g + 1) * 128], AqbTk),
                                 start=False, stop=False)
                            P.mm(po, V(vt[:, co + hp * 64:co + (hp + 1) * 64], vtk), V(AqkT[:, g * 128:(g + 1) * 128], AqkTk),
                                 start=False, stop=True)
                            P.copy(V(oT[hs, t0:t0 + 128], oTk), po, eng='dve' if hp else 'act')
                        yield
                        pn, ps = P.ps()
                        for hp in range(2):
                            hs = slice(hp * 64, (hp + 1) * 64)
                            ph = V(ps[hs, 0:64], pn)
                            P.mm(ph, V(bd[:, co + hp * 64:co + (hp + 1) * 64], bdk), V(xc_t[:, 128 + hp * 64:128 + (hp + 1) * 64], xck_),
                                 start=True, stop=False)
                            P.mm(ph, V(kdt[:, co + hp * 64:co + (hp + 1) * 64], kdtk), V(vt[:, co + hp * 64:co + (hp + 1) * 64], vtk),
                                 start=False, stop=True)
                        P.stt(Hb, H32, V(self.glc[:, c:c + 1], 'glc'), V(ps[:, 0:64], pn), ALU.mult, ALU.add)
                        P.stt(H32, H32, V(self.glc[:, c:c + 1], 'glc'), V(ps[:, 0:64], pn), ALU.mult, ALU.add)
                        yield
                    else:
                        self.b_sample_chunk(p, l, blk, at, atk, qt, qtk, LkT, LkTk, AqbT, AqbTk, AqkT, AqkTk, XbT, XbTk,
                                            bd, bdk, kdt, kdtk, vt, vtk, oT, oTk, Hsb_t)
            r5b = R[5][:, 0:1024].bitcast(BF16)
            r7b = R[7][:, 0:1024].bitcast(BF16)
            r0b = R[0][:, 0:512].bitcast(BF16)
            bsets = [(cb[2], 'cb2', cb[3], 'cb3', cb[4], 'cb4', cb[5], 'cb5'),
                     (r5b[:, 0:512], ('R5', 0), r5b[:, 512:1024], ('R5', 1), r5b[:, 1024:1536], ('R5', 2),
                      r5b[:, 1536:2048], ('R5', 3)),
                     (r7b[:, 0:512], ('R7', 0), r7b[:, 512:1024], ('R7', 1), r7b[:, 1024:1536], ('R7', 2),
                      r7b[:, 1536:2048], ('R7', 3))]
            nsets = [(cb[0], 'cb0', cb[1], 'cb1', cf[0], 'cf0'),
                     (r0b[:, 0:512], ('R0', 0), r0b[:, 512:1024], ('R0', 1), R[0][:, 512:1024], ('R0', 2))]
            groups = [(0, 1), (2, 3), (4, 5), (6, 7), (8,)]
            pipe = Pipe()
            gens = {}

            def start(k):
                gens[k] = gen_GN(groups[k], bsets[k % 3], nsets[k % 2])
                pipe.add(gens[k])
            start(0)
            start(1)
            for gi_ in range(len(groups)):
                pipe.finish(gens[gi_])
                if gi_ + 2 < len(groups):
                    start(gi_ + 2)
                pipe.run_with(gen_chain(groups[gi_], bsets[gi_ % 3]))
            pipe.drain_all()
            if blk == NBLK - 1:
                pn, ps = P.ps()
                P.tr(V(ps[0:64, 0:128], pn), H32, identf)
                P.copy(V(cf[4][0:64, 0:128], 'cf4'), V(ps[0:64, 0:128], pn))
                P.dma(O['p_b_S'][l, 2 * p:2 * p + 2].rearrange("h v k -> v h k"),
                      V(cf[4][0:64, 0:128].rearrange("p (h k) -> p h k", h=2), 'cf4'), 'cf4')
            if hasattr(self, 'marks'):
                self.marks.append(('  b-post', P.cnt['pe'], P.cnt['act'], P.cnt['dve']))
            stS = R[5]
            for hf in range(2):
                pn, ps = P.ps()
                for sq_ in range(4):
                    s_ = hf * 4 + sq_
                    P.tr(V(ps[0:64, sq_ * 128:(sq_ + 1) * 128], pn), V(self.Ss[:, s_ * 64:(s_ + 1) * 64], 'Ss'), identf)
                P.copy(V(stS[0:64, hf * 512:(hf + 1) * 512], 'R5'), V(ps[0:64, 0:512], pn), eng='act' if hf else 'dve')
            for h_ in range(2):
                P.dma(O['s_b_S'][l, s0:s0 + NS, 2 * p + h_].rearrange("s v k -> v s k"),
                      V(stS[0:64, 0:1024].rearrange("p (s h k) -> p s h k", s=NS, h=2)[:, :, h_, :], 'R5'), 'R5')
            for ti, (t0, t1) in enumerate(TT):
                n = t1 - t0
                tA = V(self.small[:, 2, 0:n], ('small', 2))
                tB = V(self.small[:, 3, 0:n], ('small', 3))
                P.act(tA, V(oT[:, t0:t1], oTk), AF.Square)
                pn1, ps1 = P.ps()
                P.mm(V(ps1[:, 0:n], pn1), V(self.ones64f[:], 'ones64f'), V(oT[:, t0:t1], oTk))
                pn2, ps2 = P.ps()
                P.mm(V(ps2[:, 0:n], pn2), V(self.ones64f[:], 'ones64f'), tA)
                P.act(tA, V(ps1[:, 0:n], pn1), AF.Identity, scale=1.0 / 64)
                P.tt(tB, tA, tA, ALU.mult)
                P.stt(tB, V(ps2[:, 0:n], pn2), 1.0 / 64, tB, ALU.mult, ALU.subtract)
                P.act(tB, tB, AF.Ln, bias=self.lnepsc)
                P.act(tB, tB, AF.Exp, scale=-0.5)
                ov = V(oT[:, t0:t1], oTk)
                P.tt(ov, ov, tA, ALU.subtract)
                P.tt(ov, ov, tB, ALU.mult)
                P.ts(ov, ov, self.pcol('b_ln_w', l * 4 + p), self.pcol('b_ln_b', l * 4 + p), ALU.mult, ALU.add)
                P.tt(ov, ov, V(bon[:, t0:t1], bonk), ALU.add)
                P.tt(V(self.yb[:, 4 + p, t0:t1], ('yb', 4 + p)), ov, V(gb[:, t0:t1], gbk), ALU.mult)

    def b_sample_chunk(self, p, l, blk, at, atk, qt, qtk, LkT, LkTk, AqbT, AqbTk, AqkT, AqkTk, XbT, XbTk,
                       bd, bdk, kdt, kdtk, vt, vtk, oT, oTk, Hsb_t):
        P = self.P
        cf = self.c128f
        cb = self.c128b
        t0 = TP
        co = 8 * 128
        n = TS
        ex, exk = self.R[7], 'R7'
        seq2 = V(self.seqm[:, :].unsqueeze(2).to_broadcast([TS, NS, 64]), 'seqm')

        def state_apply(src_t, src_k, out_red):
            for hp in range(2):
                hs = slice(hp * 64, (hp + 1) * 64)
                pn, ps = P.ps()
                P.mm(V(ps[0:n, 0:512], pn), V(src_t[hs, t0:t0 + n], src_k), V(Hsb_t[hs, 0:512], 'cb7'))
                P.tt(V(ex[0:n, hp * 512:(hp + 1) * 512].rearrange("p (s e) -> p s e", s=NS), exk),
                     V(ps[0:n, 0:512].rearrange("p (s e) -> p s e", s=NS), pn), seq2, ALU.mult)
            P.red(out_red, V(ex[0:n, 0:1024].rearrange("p (h s e) -> p h e s", h=2, s=NS), exk))
        xc32 = V(cf[1][0:n, 0:128].rearrange("p (h e) -> p h e", h=2), 'cf1')
        state_apply(at, atk, xc32)
        pn, ps = P.ps()
        for hp in range(2):
            P.mm(V(ps[0:n, hp * 64:(hp + 1) * 64], pn), V(LkT[0:n, hp * n:(hp + 1) * n], LkTk),
                 V(vt[0:n, co + hp * 64:co + (hp + 1) * 64], vtk))
        xcb = V(cb[6][0:n, 0:128], 'cb6')
        P.tt(xcb, V(ps[0:n, 0:128], pn), V(cf[1][0:n, 0:128], 'cf1'), ALU.add)
        pn, ps = P.ps()
        for hp in range(2):
            P.mm(V(ps[0:n, hp * 64:(hp + 1) * 64], pn), V(XbT[0:n, hp * n:(hp + 1) * n], XbTk),
                 V(cb[6][0:n, hp * 64:(hp + 1) * 64], 'cb6'))
        u32 = V(cf[2][0:n, 0:128], 'cf2')
        P.copy(u32, V(ps[0:n, 0:128], pn), eng='act')
        ub = V(cb[6][0:n, 128:256], 'cb6')
        P.copy(ub, u32, eng='dve')
        o32 = V(cf[3][0:n, 0:128], 'cf3')
        state_apply(qt, qtk, V(cf[3][0:n, 0:128].rearrange("p (h e) -> p h e", h=2), 'cf3'))
        pn, ps = P.ps()
        for hp in range(2):
            po = V(ps[0:n, hp * 64:(hp + 1) * 64], pn)
            P.mm(po, V(AqbT[0:n, hp * n:(hp + 1) * n], AqbTk), V(cb[6][0:n, 128 + hp * 64:128 + (hp + 1) * 64], 'cb6'),
                 start=True, stop=False)
            P.mm(po, V(AqkT[0:n, hp * n:(hp + 1) * n], AqkTk), V(vt[0:n, co + hp * 64:co + (hp + 1) * 64], vtk),
                 start=False, stop=True)
        P.tt(o32, V(ps[0:n, 0:128], pn), o32, ALU.add)
        pn, ps = P.ps()
        P.tr(V(ps[:, 0:n], pn), o32, V(self.identf[0:n, 0:n], 'identf'))
        P.copy(V(oT[:, t0:t0 + n], oTk), V(ps[:, 0:n], pn), eng='act')
        uex = self.R[7][0:n, 0:512].bitcast(BF16)
        vex = self.R[7][0:n, 512:1024].bitcast(BF16)
        for hp in range(2):
            P.tt(V(uex[:, hp * 512:(hp + 1) * 512].rearrange("p (s e) -> p s e", s=NS), exk),
                 V(cf[2][0:n, hp * 64:(hp + 1) * 64].unsqueeze(1).to_broadcast([n, NS, 64]), 'cf2'), seq2, ALU.mult)
            P.tt(V(vex[:, hp * 512:(hp + 1) * 512].rearrange("p (s e) -> p s e", s=NS), exk),
                 V(vt[0:n, co + hp * 64:co + (hp + 1) * 64].unsqueeze(1).to_broadcast([n, NS, 64]), vtk), seq2, ALU.mult)
        pn, ps = P.ps()
        for hp in range(2):
            hs = slice(hp * 64, (hp + 1) * 64)
            ph = V(ps[hs, 0:512], pn)
            P.mm(ph, V(bd[0:n, co + hp * 64:co + (hp + 1) * 64], bdk), V(uex[:, hp * 512:(hp + 1) * 512], exk),
                 start=True, stop=False)
            P.mm(ph, V(kdt[0:n, co + hp * 64:co + (hp + 1) * 64], kdtk), V(vex[:, hp * 512:(hp + 1) * 512], exk),
                 start=False, stop=True)
        Hs3 = V(self.Ss[:, 0:512].rearrange("p (s e) -> p s e", s=NS), 'Ss')
        P.tt(Hs3, Hs3, V(self.glc[:, 8:16].unsqueeze(2).to_broadcast([128, NS, 64]), 'glc'), ALU.mult)
        P.tt(Hs3, Hs3, V(ps[:, 0:512].rearrange("p (s e) -> p s e", s=NS), pn), ALU.add)

    def merge_and_ffn(self, blk, l):
        P = self.P
        R = self.R
        w512 = lambda t: t[:, :].rearrange("p (kc n) -> p kc n", kc=8)
        w4 = lambda t: t[:, 0:2048].rearrange("p (kc n) -> p kc n", kc=4)
        for jg in range(2):
            for b in range(3):
                gt, gk = self.wnext(('gate', blk, l, jg, b))
                bt, bk = self.wnext(('wbr', blk, l, jg, b))
                for jj in range(4):
                    j = jg * 4 + jj
                    acc, acck = R[jj], 'R%d' % jj
                    for ti, (t0, t1) in enumerate(TT):
                        n = t1 - t0
                        png, psg = P.ps()
                        for kc in range(KC):
                            P.mm(V(psg[:, 0:n], png), V(w512(gt)[:, kc, jj * 128:(jj + 1) * 128], gk),
                                 V(self.hT[:, kc, t0:t1], ('hT', ti)), start=(kc == 0), stop=(kc == KC - 1))
                        sg = V(self.small[:, 2 + (ti % 2), 0:n], ('small', 2 + (ti % 2)))
                        P.act(sg, V(psg[:, 0:n], png), AF.Sigmoid)
                        pnp, psp = P.ps()
                        for kc in range(4):
                            P.mm(V(psp[:, 0:n], pnp), V(w4(bt)[:, kc, jj * 128:(jj + 1) * 128], bk),
                                 V(self.yb[:, b * 4 + kc, t0:t1], ('yb', b * 4 + kc)), start=(kc == 0), stop=(kc == 3))
                        av = V(acc[:, t0:t1], (acck, ti))
                        if b == 0:
                            P.tt(av, V(psp[:, 0:n], pnp), sg, ALU.mult)
                        else:
                            P.tt(sg, V(psp[:, 0:n], pnp), sg, ALU.mult)
                            if b == 1:
                                P.tt(av, av, sg, ALU.add)
                            else:
                                P.tt(self.mch(j, t0, t1), av, sg, ALU.add)
        for jh in range(2):
            wt, wk = self.wnext(('wout', blk, l, jh))
            for jj in range(4):
                j = jh * 4 + jj
                for ti, (t0, t1) in enumerate(TT):
                    n = t1 - t0
                    pn, ps = P.ps()
                    for kc in range(KC):
                        P.mm(V(ps[:, 0:n], pn), V(w512(wt)[:, kc, jj * 128:(jj + 1) * 128], wk),
                             self.mch(kc, t0, t1), start=(kc == 0), stop=(kc == KC - 1))
                    xv = V(self.xT[:, j, t0:t1], ('xT', ti))
                    P.tt(xv, V(ps[:, 0:n], pn), xv, ALU.add)
        if hasattr(self, 'marks'):
            self.marks.append(('ffn b%d l%d' % (blk, l), P.cnt['pe'], P.cnt['act'], P.cnt['dve']))
        self.rmsnorm_to_h('norm2_g', l)
        for q in range(4):
            for g in range(2):
                wt, wk = self.wnext(('wup', blk, l, q, g))
                for jj in range(4):
                    uc = g * 4 + jj
                    for ti, (t0, t1) in enumerate(TT):
                        n = t1 - t0
                        pn, ps = P.ps()
                        for kc in range(KC):
                            P.mm(V(ps[:, 0:n], pn), V(w512(wt)[:, kc, jj * 128:(jj + 1) * 128], wk),
                                 V(self.hT[:, kc, t0:t1], ('hT', ti)), start=(kc == 0), stop=(kc == KC - 1))
                        rl = V(self.small[:, 2 + (ti % 2), 0:n], ('small', 2 + (ti % 2)))
                        P.act(rl, V(ps[:, 0:n], pn), AF.Relu)
                        P.tt(self.mch(uc, t0, t1), rl, rl, ALU.mult)
            for jh in range(2):
                wt, wk = self.wnext(('wdn', blk, l, q, jh))
                for jj in range(4):
                    j = jh * 4 + jj
                    for ti, (t0, t1) in enumerate(TT):
                        n = t1 - t0
                        pn, ps = P.ps()
                        for kc in range(KC):
                            P.mm(V(ps[:, 0:n], pn), V(w512(wt)[:, kc, jj * 128:(jj + 1) * 128], wk),
                                 self.mch(kc, t0, t1), start=(kc == 0), stop=(kc == KC - 1))
                        xv = V(self.xT[:, j, t0:t1], ('xT', ti))
                        P.tt(xv, V(ps[:, 0:n], pn), xv, ALU.add)

    def final_norm_store(self, blk):
        P = self.P
        for ti, (t0, t1) in enumerate(TT):
            n = t1 - t0
            pn, ps = P.ps()
            sq, sqk = self.Bt[ti % 2], 'B%d' % (ti % 2)
            for g3, (c0, c1) in enumerate([(0, 3), (3, 6), (6, 8)]):
                P.act(V(sq[:, 0:(c1 - c0) * n].rearrange("p (c n) -> p c n", c=c1 - c0), sqk),
                      V(self.xT[:, c0:c1, t0:t1], ('xT', ti)), AF.Square)
                for c in range(c0, c1):
                    P.mm(V(ps[:, 0:n], pn), V(self.onesb[:], 'onesb'), V(sq[:, (c - c0) * n:(c - c0 + 1) * n], sqk),
                         start=(c == 0), stop=(c == 7))
            rs = V(self.small[:, ti % 2, 0:n], ('small', ti % 2))
            P.act(rs, V(ps[:, 0:n], pn), AF.Ln, bias=self.epsc, scale=1.0 / D)
            P.act(rs, rs, AF.Exp, scale=-0.5)
            for c in range(KC):
                xv = V(self.xT[:, c, t0:t1], ('xT', ti))
                P.stt(xv, xv, self.pcol('final_norm_g', c), rs, ALU.mult, ALU.mult)
        identf = V(self.identf[:], 'identf')
        for i in range(8):
            stg = self.R[i % 2]
            sk = 'R%d' % (i % 2)
            for half in range(2):
                pn, ps = P.ps()
                for c in range(4):
                    cc = half * 4 + c
                    P.tr(V(ps[:, c * 128:(c + 1) * 128], pn), V(self.xT[:, cc, i * 128:(i + 1) * 128], 'xT'), identf)
                P.copy(V(stg[:, half * 512:(half + 1) * 512], sk), V(ps[:, :], pn), eng='act' if half else 'dve')
            P.dma(self.O['y_p'][blk * TP + i * 128:blk * TP + (i + 1) * 128, :], V(stg[:, 0:1024], sk), sk)
        stg = self.R[0]
        for half in range(2):
            pn, ps = P.ps()
            for c in range(4):
                cc = half * 4 + c
                P.tr(V(ps[0:TS, c * 128:(c + 1) * 128], pn), V(self.xT[:, cc, TP:T], 'xT'), identf)
            P.copy(V(stg[0:TS, half * 512:(half + 1) * 512], 'R0'), V(ps[0:TS, :], pn))
        P.dma(self.O['y_s'][blk * TS:(blk + 1) * TS, :], V(stg[0:TS, 0:1024], 'R0'), 'R0')

    def build(self):
        with ExitStack() as es:
            self.P = Prog(self.nc, es)
            P = self.P
            P.init_psum(reserve=1 if K_WARM else 0)
            self.alloc()
            consts = P.sb('consts', [128, 4], F32)
            P.memset(V(consts[:, 0:1], 'consts'), EPS)
            P.memset(V(consts[:, 1:2], 'consts'), 1.0)
            P.memset(V(consts[:, 2:3], 'consts'), B_LN_EPS)
            self.epsc = V(consts[:, 0:1], 'consts')
            self.onec = V(consts[:, 1:2], 'consts')
            self.lnepsc = V(consts[:, 2:3], 'consts')
            self.setup_consts()
            self.sched = self.weight_schedule()
            self.w_i = 0
            self.w_issued = 0
            def warm_fn():
                pnw, psw = P.psum_extra[0]
                for _ in range(K_WARM):
                    P.mm(V(psw[:, 0:512], pnw), V(self.identb[:], 'identb'), V(self.rmask[:, 0:512], 'rmask'))
            self.marks = []
            mk = lambda lab: self.marks.append((lab, P.cnt['pe'], P.cnt['act'], P.cnt['dve']))
            for blk in range(NBLK):
                mk('load%d' % blk)
                self.load_x_block(blk)
                for l in range(K_LAYERS):
                    mk('norm1 b%d l%d' % (blk, l))
                    self.rmsnorm_to_h('norm1_g', l)
                    if not (EN_A and EN_B and EN_C):
                        P.memset(V(self.yb[:], 'yb'), 0.0, eng='dve')
                    mk('A b%d l%d' % (blk, l))
                    if K_WARM:
                        P.warm_fn = warm_fn
                    if EN_A:
                        try:
                            self.branch_A(blk, l)
                        except _Stop:
                            pass
                    mk('B b%d l%d' % (blk, l))
                    if EN_B:
                        self.branch_B(blk, l)
                    mk('C b%d l%d' % (blk, l))
                    if EN_C:
                        self.branch_C(blk, l)
                    P.warm_fn = None
                    mk('merge b%d l%d' % (blk, l))
                    if K_MERGE:
                        self.merge_and_ffn(blk, l)
                mk('final%d' % blk)
                self.final_norm_store(blk)
            mk('end')
            assert K_ASTOP or self.w_i == len(self.sched)
            P.final_wait_all()
            P.build()
        return self.nc


_CACHE = {}


def kernel(**inputs):
    inp = {k: np.ascontiguousarray(np.asarray(v, dtype=np.float32)) for k, v in inputs.items()}
    if 'nc' not in _CACHE:
        _CACHE['nc'] = Builder().build()
    nc = _CACHE['nc']
    wnames = [k for k in INPUT_SHAPES if k not in ('xp', 'xs', 'sa_S', 'sa_conv', 'sb_S', 'sb_shift', 'sc_h', 'sc_conv')]
    in_maps = []
    for c in range(NCORES):
        s = slice(c * NSEQ_CORE, (c + 1) * NSEQ_CORE)
        m = {
            'xp': inp['x_prompt'][c],
            'xs': inp['x_sample'][s].reshape(NSEQ_CORE * 4, D),
            'sa_S': inp['state_a_S'][:, s], 'sa_conv': inp['state_a_conv'][:, s],
            'sb_S': inp['state_b_S'][:, s], 'sb_shift': inp['state_b_shift'][:, s],
            'sc_h': inp['state_c_h'][:, s], 'sc_conv': inp['state_c_conv'][:, s],
        }
        for k in wnames:
            m[k] = inp[k]
        in_maps.append({k: np.ascontiguousarray(v) for k, v in m.items()})
    res = run_bass_kernel_spmd(nc, in_maps, core_ids=list(range(NCORES)))
    rs = res.results
    y_prompt = np.stack([rs[c]['y_p'] for c in range(NCORES)], axis=0)
    y_sample = np.concatenate([rs[c]['y_s'].reshape(NSEQ_CORE, 4, D) for c in range(NCORES)], axis=0)
    outs = [y_prompt, y_sample]
    for nm in ['p_a_S', 'p_a_conv', 'p_b_S', 'p_b_shift', 'p_c_h', 'p_c_conv']:
        outs.append(np.stack([rs[c][nm] for c in range(NCORES)], axis=1))
    for nm in ['s_a_S', 's_a_conv', 's_b_S', 's_b_shift', 's_c_h', 's_c_conv']:
        outs.append(np.concatenate([rs[c][nm] for c in range(NCORES)], axis=1))
    return tuple(np.ascontiguousarray(o.astype(np.float32)) for o in outs)
```

```python
import os
import numpy as np
from contextlib import ExitStack
import concourse.bass as bass
import concourse.mybir as mybir
from concourse.bass_utils import run_bass_kernel_spmd

F32 = mybir.dt.float32
BF16 = mybir.dt.bfloat16
AF = mybir.ActivationFunctionType
ALU = mybir.AluOpType
AX = mybir.AxisListType

ENGS = ['pe', 'act', 'dve', 'pool', 'sp']

EN_A = os.environ.get('K_EN_A', '1') == '1'
EN_B = os.environ.get('K_EN_B', '1') == '1'
EN_C = os.environ.get('K_EN_C', '1') == '1'
K_LAYERS = int(os.environ.get('K_LAYERS', '2'))
K_MERGE = os.environ.get('K_MERGE', '1') == '1'
K_ASTOP = int(os.environ.get('K_ASTOP', '0'))
VC_REDUCE = os.environ.get('K_VC', '1') == '1'
SKIP_SELF_WAW = os.environ.get('K_SELF_WAW', '0') == '0'
K_WARM = int(os.environ.get('K_WARM', '0'))


class _Stop(Exception):
    pass


def interleave(*gens):
    gens = [g for g in gens if g is not None]
    while gens:
        for g in list(gens):
            try:
                next(g)
            except StopIteration:
                gens.remove(g)


def drain(g):
    for _ in g:
        pass


class Pipe:
    def __init__(self):
        self.active = []

    def add(self, g):
        self.active.append(g)

    def step_all(self):
        for g in list(self.active):
            try:
                next(g)
            except StopIteration:
                self.active.remove(g)

    def finish(self, g):
        while g in self.active:
            self.step_all()

    def run_with(self, main):
        self.active.insert(0, main)
        self.finish(main)

    def drain_all(self):
        while self.active:
            self.step_all()


class V:
    __slots__ = ('ap', 'keys')

    def __init__(self, ap, *keys):
        self.ap = ap
        self.keys = [k if isinstance(k, tuple) else (k,) for k in keys]


class Prog:
    def __init__(self, nc, es):
        self.nc = nc
        self.es = es
        self.q = {e: [] for e in ENGS}
        self.cnt = {e: 0 for e in ENGS}
        self.waited = {e: {} for e in ENGS}
        self.semh = {}
        self.rec = {}
        self.children = {}
        self.dma_cnt = {}
        self.psum_tiles = []
        self.psum_i = 0
        self.warm_fn = None
        self._in_warm = False
        self.ev_vc = {}
        self.psum_extra = []
        for e in ['pe', 'act', 'dve', 'pool']:
            self.sem(e)

    def sem(self, name):
        if name not in self.semh:
            self.semh[name] = self.es.enter_context(
                self.nc.semaphore('s_' + name.replace(':', '_').replace('/', '_')))
        return self.semh[name]

    def sb(self, name, shape, dtype=F32):
        return self.es.enter_context(self.nc.sbuf_tensor(name, list(shape), dtype))

    def init_psum(self, n=8, reserve=0):
        for i in range(n):
            t = self.es.enter_context(self.nc.psum_tensor('ps%d' % i, [128, 512], F32))
            if i < n - reserve:
                self.psum_tiles.append(('ps%d' % i, t))
            else:
                self.psum_extra.append(('ps%d' % i, t))

    def ps(self):
        name, t = self.psum_tiles[self.psum_i % len(self.psum_tiles)]
        self.psum_i += 1
        return name, t

    @staticmethod
    def _k(k):
        return k if isinstance(k, tuple) else (k,)

    def _conflicts(self, key):
        out = []
        for i in range(1, len(key) + 1):
            r = self.rec.get(key[:i])
            if r is not None:
                out.append(r)
        stack = list(self.children.get(key, ()))
        while stack:
            c = stack.pop()
            r = self.rec.get(c)
            if r is not None:
                out.append(r)
            stack.extend(self.children.get(c, ()))
        return out

    def _get(self, key):
        r = self.rec.get(key)
        if r is None:
            r = [None, []]
            self.rec[key] = r
            for i in range(1, len(key)):
                self.children.setdefault(key[:i], set()).add(key[:i + 1])
        return r

    def emit(self, eng, fn, r=(), w=(), dma_tile=None):
        r = [self._k(k) for k in r]
        w = [self._k(k) for k in w]
        psr = [(k[0],) for k in r if k[0].startswith('ps')]
        r = [k for k in r if not k[0].startswith('ps')]
        w = [((k[0],) if k[0].startswith('ps') else k) for k in w] + psr
        is_dma = dma_tile is not None
        semname = None
        if is_dma:
            semname = 'dma:' + '/'.join(map(str, self._k(dma_tile)))
        deps = []
        for k in r:
            for rc in self._conflicts(k):
                if rc[0] is not None:
                    deps.append((rc[0], 'raw', False))
        for k in w:
            isps = k[0].startswith('ps')
            for rc in self._conflicts(k):
                if rc[0] is not None:
                    deps.append((rc[0], 'waw', isps))
                for ev in rc[1]:
                    deps.append((ev, 'war', isps))
        need = {}
        for (ev, kind, isps) in deps:
            sem, val, src_eng, src_dma = ev
            if src_eng == eng and not src_dma and not is_dma:
                if eng == 'pe' or isps:
                    continue
                if kind == 'war' or (kind == 'waw' and SKIP_SELF_WAW):
                    continue
            if is_dma and src_dma and sem == semname and kind == 'waw':
                continue
            if src_dma:
                val = 16 * self.dma_cnt[sem]
            if need.get(sem, 0) < val:
                need[sem] = val
        waits = []
        wd = self.waited[eng]
        for sem, val in sorted(need.items(), key=lambda kv: -kv[1]):
            if wd.get(sem, 0) >= val:
                continue
            wd[sem] = val
            waits.append((sem, val))
            snap = self.ev_vc.get((sem, val))
            if snap is not None and VC_REDUCE:
                for s2, v2 in snap.items():
                    if wd.get(s2, 0) < v2:
                        wd[s2] = v2
        if eng == 'pe' and self.warm_fn is not None and waits and not self._in_warm:
            self._in_warm = True
            self.warm_fn()
            self._in_warm = False
        if is_dma:
            self.sem(semname)
            self.dma_cnt[semname] = self.dma_cnt.get(semname, 0) + 1
            ev = (semname, 16 * self.dma_cnt[semname], eng, True)
            inc = (semname, 16)
        else:
            self.cnt[eng] += 1
            ev = (eng, self.cnt[eng], eng, False)
            inc = (eng, 1)
        self.q[eng].append((waits, fn, inc))
        snap = dict(self.waited[eng])
        if not is_dma:
            snap[eng] = ev[1]
        self.ev_vc[(ev[0], ev[1])] = snap
        for k in w:
            rc = self._get(k)
            rc[0] = ev
            rc[1] = []
            stack = list(self.children.get(k, ()))
            while stack:
                c = stack.pop()
                if c in self.rec:
                    self.rec[c] = [None, []]
                stack.extend(self.children.get(c, ()))
        for k in r:
            rc = self._get(k)
            rc[1].append(ev)
        return ev

    def final_wait_all(self, eng='sp'):
        waits = []
        for sem, n in self.dma_cnt.items():
            waits.append((sem, 16 * n))
        for e in ['pe', 'act', 'dve', 'pool']:
            if self.cnt[e]:
                waits.append((e, self.cnt[e]))
        self.q[eng].append((waits, None, None))

    def build(self):
        nc = self.nc
        with nc.Block() as block:
            def replay(ename):
                def f(engine):
                    for (waits, fn, inc) in self.q[ename]:
                        for (sem, val) in waits:
                            engine.wait_ge(self.semh[sem], val)
                        if fn is None:
                            continue
                        ins = fn(engine)
                        ins.then_inc(self.semh[inc[0]], inc[1])
                return f
            block.tensor(replay('pe'))
            block.scalar(replay('act'))
            block.vector(replay('dve'))
            block.gpsimd(replay('pool'))
            block.sync(replay('sp'))

    @staticmethod
    def _rk(*ops):
        ks = []
        for o in ops:
            if isinstance(o, V):
                ks += o.keys
        return ks

    @staticmethod
    def _a(o):
        return o.ap if isinstance(o, V) else o

    def mm(self, out, lhsT, rhs, start=True, stop=True):
        self.emit('pe', lambda e: e.matmul(out.ap, lhsT=lhsT.ap, rhs=rhs.ap, start=start, stop=stop),
                  r=self._rk(lhsT, rhs), w=out.keys)

    def tr(self, out, in_, ident):
        self.emit('pe', lambda e: e.transpose(out=out.ap, in_=in_.ap, identity=ident.ap),
                  r=self._rk(in_, ident), w=out.keys)

    def act(self, out, in_, func, bias=None, scale=None):
        kw = {}
        if bias is not None:
            kw['bias'] = self._a(bias)
        if scale is not None:
            kw['scale'] = self._a(scale)
        self.emit('act', lambda e: e.activation(out=out.ap, in_=in_.ap, func=func, **kw),
                  r=self._rk(in_, bias, scale), w=out.keys)

    def tt(self, out, in0, in1, op, eng='dve'):
        self.emit(eng, lambda e: e.tensor_tensor(out=out.ap, in0=in0.ap, in1=in1.ap, op=op),
                  r=self._rk(in0, in1), w=out.keys)

    def ts(self, out, in0, s1, s2, op0, op1=None, eng='dve'):
        if op1 is None:
            fn = lambda e: e.tensor_scalar(out=out.ap, in0=in0.ap, scalar1=self._a(s1), scalar2=None, op0=op0)
        else:
            fn = lambda e: e.tensor_scalar(out=out.ap, in0=in0.ap, scalar1=self._a(s1), scalar2=self._a(s2),
                                           op0=op0, op1=op1)
        self.emit(eng, fn, r=self._rk(in0, s1, s2), w=out.keys)

    def stt(self, out, in0, scalar, in1, op0, op1):
        self.emit('dve', lambda e: e.scalar_tensor_tensor(out=out.ap, in0=in0.ap, scalar=self._a(scalar),
                                                          in1=in1.ap, op0=op0, op1=op1),
                  r=self._rk(in0, scalar, in1), w=out.keys)

    def scan(self, out, d0, d1, initial, op0, op1):
        self.emit('dve', lambda e: e.tensor_tensor_scan(out=out.ap, data0=d0.ap, data1=d1.ap,
                                                        initial=self._a(initial), op0=op0, op1=op1),
                  r=self._rk(d0, d1, initial), w=out.keys)

    def recip(self, out, in_):
        self.emit('dve', lambda e: e.reciprocal(out=out.ap, in_=in_.ap), r=in_.keys, w=out.keys)

    def red(self, out, in_, op=ALU.add):
        self.emit('dve', lambda e: e.tensor_reduce(out=out.ap, in_=in_.ap, axis=AX.X, op=op),
                  r=in_.keys, w=out.keys)

    def copy(self, out, in_, eng='dve'):
        if eng == 'act':
            self.emit('act', lambda e: e.activation(out=out.ap, in_=in_.ap, func=AF.Copy), r=in_.keys, w=out.keys)
        else:
            self.emit(eng, lambda e: e.tensor_copy(out=out.ap, in_=in_.ap), r=in_.keys, w=out.keys)

    def memset(self, out, val, eng='pool'):
        self.emit(eng, lambda e: e.memset(out.ap, val), w=out.keys)

    def asel(self, out, in_, pattern, cmp, fill, base, cm):
        self.emit('pool', lambda e: e.affine_select(out=out.ap, in_=in_.ap, pattern=pattern, compare_op=cmp,
                                                    fill=fill, base=base, channel_multiplier=cm),
                  r=in_.keys, w=out.keys)

    def dma(self, out, in_, tile_key, eng='sp', out_is_sb=True, nc_ok=False):
        kw = {}
        if nc_ok:
            kw['allow_slow_non_contiguous'] = True
        oa = self._a(out)
        ia = self._a(in_)
        self.emit(eng, lambda e: e.dma_start(out=oa, in_=ia, **kw),
                  r=self._rk(in_), w=self._rk(out), dma_tile=tile_key)


NCORES = 8
D = 1024
KC = 8
SEQ = 2048
DEPTH = 2
NSEQ_CORE = 16
TP = 1024
NS = 8
TS = 32
T = TP + TS
NBLK = 2
TT = [(0, 352), (352, 704), (704, 1056)]
N_IN = 7944
A_OFF, B_OFF, C_OFF, G_OFF = 0, 2056, 3848, 4872
EPS = 1e-6
B_LN_EPS = 64e-5
RW = 1088
BW = 1152
NCHUNK = 9
NEU_LEVELS_P = 7
NEU_LEVELS_S = 2


def chunk_info(c):
    if c < 8:
        return c * 128, 128, 1
    return TP, TS, NS


PVECS = [
    ('norm1_g', 16), ('norm2_g', 16), ('final_norm_g', 8), ('a_conv_w', 96), ('a_norm_g', 2),
    ('b_mu', 28), ('b_w0', 8), ('b_a0', 8), ('b_k_k', 8), ('b_k_a', 8), ('b_ln_w', 8), ('b_ln_b', 8),
    ('b_r_k', 8), ('c_conv_w', 32), ('c_conv_b', 8), ('c_ba', 8), ('c_bx', 8), ('c_L', 8),
]
PREARR = {
    'norm1_g': "l (r p) -> (l r) p", 'norm2_g': "l (r p) -> (l r) p", 'final_norm_g': "(r p) -> r p",
    'a_conv_w': "l j (r p) -> (l j r) p", 'a_norm_g': "l p -> l p", 'b_mu': "l (r p) -> (l r) p",
    'b_w0': "l (r p) -> (l r) p", 'b_a0': "l (r p) -> (l r) p", 'b_k_k': "l (r p) -> (l r) p",
    'b_k_a': "l (r p) -> (l r) p", 'b_ln_w': "l (r p) -> (l r) p", 'b_ln_b': "l (r p) -> (l r) p",
    'b_r_k': "l (r h2) n -> (l r) (h2 n)", 'c_conv_w': "l j (r p) -> (l j r) p",
    'c_conv_b': "l (r p) -> (l r) p", 'c_ba': "l (r p) -> (l r) p", 'c_bx': "l (r p) -> (l r) p",
    'c_L': "l (r p) -> (l r) p",
}


def param_rows():
    off = {}
    r = 0
    for name, n in PVECS:
        if (r % 128) + n > 128:
            r = (r // 128 + 1) * 128
        off[name] = r
        r += n
    nst = (r + 127) // 128
    return off, nst


POFF, PNST = param_rows()

INPUT_SHAPES = {
    'xp': [SEQ, D], 'xs': [NSEQ_CORE * 4, D],
    'sa_S': [DEPTH, NSEQ_CORE, 4, 128, 128], 'sa_conv': [DEPTH, NSEQ_CORE, 3, 1536],
    'sb_S': [DEPTH, NSEQ_CORE, 8, 64, 64], 'sb_shift': [DEPTH, NSEQ_CORE, 1, 1792],
    'sc_h': [DEPTH, NSEQ_CORE, 512], 'sc_conv': [DEPTH, NSEQ_CORE, 3, 512],
    'norm1_g': [DEPTH, D], 'w_in': [DEPTH, D, N_IN], 'a_conv_w': [DEPTH, 4, 1536], 'a_A_log': [DEPTH, 4],
    'a_dt_bias': [DEPTH, 4], 'a_norm_g': [DEPTH, 128], 'b_mu': [DEPTH, 1792], 'b_w0': [DEPTH, 512],
    'b_w_up': [DEPTH, 64, 512], 'b_a0': [DEPTH, 512], 'b_a_up': [DEPTH, 64, 512], 'b_g_up': [DEPTH, 128, 512],
    'b_k_k': [DEPTH, 512], 'b_k_a': [DEPTH, 512], 'b_r_k': [DEPTH, 8, 64], 'b_ln_w': [DEPTH, 512],
    'b_ln_b': [DEPTH, 512], 'c_conv_w': [DEPTH, 4, 512], 'c_conv_b': [DEPTH, 512], 'c_wa': [DEPTH, 8, 64, 64],
    'c_ba': [DEPTH, 512], 'c_wx': [DEPTH, 8, 64, 64], 'c_bx': [DEPTH, 512], 'c_L': [DEPTH, 512],
    'w_branch': [DEPTH, 3, 512, D], 'w_out': [DEPTH, D, D], 'norm2_g': [DEPTH, D], 'w_up': [DEPTH, D, 4 * D],
    'w_down': [DEPTH, 4 * D, D], 'final_norm_g': [D],
}
OUTPUT_SHAPES = {
    'y_p': [SEQ, D], 'y_s': [NSEQ_CORE * 4, D],
    'p_a_S': [DEPTH, 4, 128, 128], 'p_a_conv': [DEPTH, 3, 1536], 'p_b_S': [DEPTH, 8, 64, 64],
    'p_b_shift': [DEPTH, 1, 1792], 'p_c_h': [DEPTH, 512], 'p_c_conv': [DEPTH, 3, 512],
    's_a_S': [DEPTH, NSEQ_CORE, 4, 128, 128], 's_a_conv': [DEPTH, NSEQ_CORE, 3, 1536],
    's_b_S': [DEPTH, NSEQ_CORE, 8, 64, 64], 's_b_shift': [DEPTH, NSEQ_CORE, 1, 1792],
    's_c_h': [DEPTH, NSEQ_CORE, 512], 's_c_conv': [DEPTH, NSEQ_CORE, 3, 512],
}


class Builder:
    def __init__(self):
        self.nc = bass.Bass("TRN2", target_bir_lowering=False)
        nc = self.nc
        self.I = {k: nc.dram_tensor(k, s, F32, kind="ExternalInput").ap() for k, s in INPUT_SHAPES.items()}
        self.O = {k: nc.dram_tensor(k, s, F32, kind="ExternalOutput").ap() for k, s in OUTPUT_SHAPES.items()}

    def alloc(self):
        P = self.P
        self.xT = P.sb('xT', [128, KC, T], F32)
        self.hT = P.sb('hT', [128, KC, T], BF16)
        self.yb = P.sb('yb', [128, 12, T], BF16)
        self.ring = [P.sb('wr%d' % i, [128, 4096], BF16) for i in range(3)]
        self.R = [P.sb('R%d' % i, [128, RW], F32) for i in range(8)]
        self.Bt = [P.sb('B%d' % i, [128, BW], BF16) for i in range(10)]
        self.PT = P.sb('PT', [128, PNST * 128], F32)
        self.identf = P.sb('identf', [128, 128], F32)
        self.identb = P.sb('identb', [128, 128], BF16)
        self.onesf = P.sb('onesf', [128, 128], F32)
        self.onesb = P.sb('onesb', [128, 128], BF16)
        self.ones64b = P.sb('ones64b', [128, 128], BF16)
        self.ones64f = P.sb('ones64f', [128, 128], F32)
        self.m_uincl = P.sb('m_uincl', [128, 2, 128], F32)
        self.m_lstr = P.sb('m_lstr', [128, 2, 128], F32)
        self.m_bigL = P.sb('m_bigL', [128, 2, 128], F32)
        self.m_negU = P.sb('m_negU', [128, 2, 128], F32)
        self.m_ustr = P.sb('m_ustr', [128, 2, 128], F32)
        self.seqm = P.sb('seqm', [32, 8], F32)
        self.seqmT = P.sb('seqmT', [8, 32], F32)
        self.rmask = P.sb('rmask', [128, T], BF16)
        self.small = P.sb('small', [128, 4, 352], F32)
        self.sqt = P.sb('sqt', [128, 352], BF16)
        self.sqts = [self.sqt, P.sb('sqt1', [128, 352], BF16), P.sb('sqt2', [128, 352], BF16)]
        self.sqtk = ['sqt', 'sqt1', 'sqt2']
        self.csm = P.sb('csm', [128, 4, 96], F32)
        self.c128f = [P.sb('cf0', [128, 512], F32)] + [P.sb('cf%d' % i, [128, 128], F32) for i in range(1, 6)]
        self.c128b = [P.sb('cb%d' % i, [128, 512], BF16) for i in range(8)]
        self.SA = P.sb('SA', [128, DEPTH, 4, 128], F32)
        self.SAb = P.sb('SAb', [128, 4, 128], BF16)
        self.HB = P.sb('HB', [128, DEPTH, 4, 64], F32)
        self.HBb = P.sb('HBb', [128, 4, 64], BF16)
        self.hC = P.sb('hC', [128, DEPTH, 4], F32)
        self.tailA = P.sb('tailA', [128, DEPTH, 12, 3], F32)
        self.tailB = P.sb('tailB', [128, DEPTH, 14], F32)
        self.tailC = P.sb('tailC', [128, DEPTH, 4, 3], F32)
        self.Ss = P.sb('Ss', [128, NS * 128], F32)
        self.a4 = P.sb('a4', [4, DEPTH, 2], F32)
        self.colsA = P.sb('colsA', [128, NCHUNK, 24], F32)
        self.glc = P.sb('glc', [128, 16], F32)
        self.lorW = P.sb('lorW', [128, 512], BF16)
        self.gupW = P.sb('gupW', [128, 512], BF16)
        self.cgate = P.sb('cgate', [128, 4, 2, 128], BF16)
        self.rkbd = P.sb('rkbd', [128, 4, 128], BF16)
        self.stage = [P.sb('stg0', [32, 512], F32)]
        self.stage.append(self.stage[0])

    def mch(self, j, t0, t1):
        r = 4 + j // 2
        v = self.R[r][:, 0:2 * 528].bitcast(BF16)
        o = (j % 2) * T
        return V(v[:, o + t0:o + t1], ('R%d' % r, j % 2))

    def astop(self, k):
        if hasattr(self, 'marks'):
            P = self.P
            self.marks.append(('  a-stage%d' % k, P.cnt['pe'], P.cnt['act'], P.cnt['dve']))
        if K_ASTOP == k:
            raise _Stop()

    def ti_pipe(self, gen_fn):
        interleave(*[gen_fn(ti, t0, t1, t1 - t0) for ti, (t0, t1) in enumerate(TT)])

    def pcol(self, name, idx):
        r = POFF[name] + idx
        return V(self.PT[:, r:r + 1], 'PT')

    def setup_consts(self):
        P = self.P
        I = self.I
        onesf = V(self.onesf[:], 'onesf')
        P.memset(onesf, 1.0)
        P.memset(V(self.onesb[:], 'onesb'), 1.0)
        P.asel(V(self.identf[:], 'identf'), onesf, [[-1, 128]], ALU.is_equal, 0.0, 0, 1)
        P.copy(V(self.identb[:], 'identb'), V(self.identf[:], 'identf'), eng='pool')
        o64 = V(self.ones64f[:], 'ones64f')
        P.memset(o64, 0.0)
        P.memset(V(self.ones64f[0:64, 0:64], 'ones64f'), 1.0)
        P.memset(V(self.ones64f[64:128, 64:128], 'ones64f'), 1.0)
        P.copy(V(self.ones64b[:], 'ones64b'), o64, eng='pool')
        P.asel(V(self.seqm[:], 'seqm'), V(self.onesf[0:32, 0:8], 'onesf'), [[-4, 8]], ALU.is_ge, 0.0, 0, 1)
        P.asel(V(self.seqm[:], 'seqm'), V(self.seqm[:], 'seqm'), [[4, 8]], ALU.is_ge, 0.0, 3, -1)
        P.asel(V(self.seqmT[:], 'seqmT'), V(self.onesf[0:8, 0:32], 'onesf'), [[1, 32]], ALU.is_ge, 0.0, 0, -4)
        P.asel(V(self.seqmT[:], 'seqmT'), V(self.seqmT[:], 'seqmT'), [[-1, 32]], ALU.is_ge, 0.0, 3, 4)
        ui = V(self.m_uincl[:, 0, :], 'm_uincl')
        P.asel(ui, onesf, [[1, 128]], ALU.is_ge, 0.0, 0, -1)
        ls = V(self.m_lstr[:, 0, :], 'm_lstr')
        P.asel(ls, onesf, [[-1, 128]], ALU.is_gt, 0.0, 0, 1)
        pn, ps = P.ps()
        same = V(ps[0:32, 0:32], pn)
        P.mm(same, V(self.seqmT[:], 'seqmT'), V(self.seqmT[:], 'seqmT'))
        P.memset(V(self.m_uincl[:, 1, :], 'm_uincl'), 0.0)
        P.memset(V(self.m_lstr[:, 1, :], 'm_lstr'), 0.0)
        P.tt(V(self.m_uincl[0:32, 1, 0:32], 'm_uincl'), same, V(self.m_uincl[0:32, 0, 0:32], 'm_uincl'), ALU.mult)
        P.tt(V(self.m_lstr[0:32, 1, 0:32], 'm_lstr'), same, V(self.m_lstr[0:32, 0, 0:32], 'm_lstr'), ALU.mult)
        P.ts(V(self.m_bigL[:], 'm_bigL'), V(self.m_lstr[:], 'm_lstr'), -1e4, 1e4, ALU.mult, ALU.add)
        P.ts(V(self.m_negU[:], 'm_negU'), V(self.m_uincl[:], 'm_uincl'), 1e4, -1e4, ALU.mult, ALU.add)
        P.tt(V(self.m_ustr[:, 0, :], 'm_ustr'), V(self.m_uincl[:, 0, :], 'm_uincl'), V(self.identf[:], 'identf'), ALU.subtract)
        P.memset(V(self.m_ustr[:, 1, :], 'm_ustr'), 0.0)
        P.tt(V(self.m_ustr[0:32, 1, 0:32], 'm_ustr'), V(self.m_uincl[0:32, 1, 0:32], 'm_uincl'),
             V(self.identf[0:32, 0:32], 'identf'), ALU.subtract)
        rm = V(self.rmask[:], 'rmask')
        P.memset(rm, 1.0)
        P.memset(V(self.rmask[:, 0:TP].rearrange("p (c n) -> p c n", n=128)[:, :, 0], 'rmask'), 0.0)
        P.memset(V(self.rmask[:, TP:T].rearrange("p (s j) -> p s j", j=4)[:, :, 0], 'rmask'), 0.0)
        stg = self.R[0]
        for st in range(PNST):
            for name, n in PVECS:
                r0 = POFF[name]
                if r0 // 128 != st:
                    continue
                if name == 'a_norm_g':
                    src = I[name]
                elif name == 'b_r_k':
                    src = I[name].rearrange("l (r h2) n -> (l r) (h2 n)", h2=2)
                else:
                    src = I[name].rearrange(PREARR[name], p=128)
                P.dma(V(stg[r0 % 128:r0 % 128 + n, 0:128], 'R0'), src, 'R0')
            nrows = max((POFF[nm] + n - st * 128) for nm, n in PVECS if POFF[nm] // 128 == st)
            pn, ps = P.ps()
            P.tr(V(ps[:, 0:nrows], pn), V(stg[0:nrows, 0:128], 'R0'), V(self.identf[0:nrows, 0:nrows], 'identf'))
            P.copy(V(self.PT[:, st * 128:st * 128 + nrows], 'PT'), V(ps[:, 0:nrows], pn), eng='act')
        P.dma(V(self.a4[:, :, 0], 'a4'), I['a_A_log'].rearrange("l h -> h l"), 'a4', nc_ok=True)
        P.dma(V(self.a4[:, :, 1], 'a4'), I['a_dt_bias'].rearrange("l h -> h l"), 'a4', nc_ok=True)
        P.act(V(self.a4[:, :, 0], 'a4'), V(self.a4[:, :, 0], 'a4'), AF.Exp)
        P.ts(V(self.a4[:, :, 0], 'a4'), V(self.a4[:, :, 0], 'a4'), -1.0, None, ALU.mult)
        for nm in ['SA', 'HB', 'hC', 'tailA', 'tailB', 'tailC']:
            P.memset(V(getattr(self, nm)[:], nm), 0.0)
        P.memset(V(self.cgate[:], 'cgate'), 0.0)

    def weight_schedule(self):
        I = self.I
        sched = []
        for blk in range(NBLK):
            for l in range(K_LAYERS):
                win = I['w_in'][l].rearrange("(kc p) n -> p kc n", p=128)

                def w512(c0, win=win):
                    return [(lambda t: t[:, :].rearrange("p (kc n) -> p kc n", kc=8), win[:, :, c0:c0 + 512])]

                if EN_A:
                    sched.append((('A_ba', blk, l),
                                  [(lambda t: t[:, 0:64].rearrange("p (kc n) -> p kc n", kc=8),
                                    win[:, :, A_OFF + 2048:A_OFF + 2056])]))
                    for hd in range(4):
                        pcs = []
                        for cc in range(4):
                            pcs.append((lambda t, cc=cc: t[:, :].rearrange("p (kc c n) -> p kc c n", kc=8, c=4)[:, :, cc, :],
                                        win[:, :, A_OFF + cc * 512 + hd * 128:A_OFF + cc * 512 + (hd + 1) * 128]))
                        sched.append((('A_head', blk, l, hd), pcs))
                if EN_B:
                    sched.append((('B_lora', blk, l),
                                  [(lambda t: t[:, 0:2048].rearrange("p (kc n) -> p kc n", kc=8),
                                    win[:, :, B_OFF + 1536:B_OFF + 1792])]))
                    for pr in range(4):
                        pcs = []
                        for cc in range(3):
                            pcs.append((lambda t, cc=cc: t[:, 0:3072].rearrange("p (kc c n) -> p kc c n", kc=8, c=3)[:, :, cc, :],
                                        win[:, :, B_OFF + cc * 512 + pr * 128:B_OFF + cc * 512 + (pr + 1) * 128]))
                        sched.append((('B_pair', blk, l, pr), pcs))
                if EN_C:
                    sched.append((('C_x', blk, l), w512(C_OFF)))
                    sched.append((('C_g', blk, l), w512(C_OFF + 512)))
                if not K_MERGE:
                    continue
                wbr = I['w_branch'][l]
                for jg in range(2):
                    for b in range(3):
                        sched.append((('gate', blk, l, jg, b), w512(G_OFF + b * 1024 + jg * 512)))
                        sched.append((('wbr', blk, l, jg, b),
                                      [(lambda t: t[:, 0:2048].rearrange("p (kc n) -> p kc n", kc=4),
                                        wbr[b].rearrange("(kc p) n -> p kc n", p=128)[:, :, jg * 512:(jg + 1) * 512])]))
                wo = I['w_out'][l].rearrange("(kc p) n -> p kc n", p=128)
                for jh in range(2):
                    sched.append((('wout', blk, l, jh),
                                  [(lambda t: t[:, :].rearrange("p (kc n) -> p kc n", kc=8), wo[:, :, jh * 512:(jh + 1) * 512])]))
                wu = I['w_up'][l].rearrange("(kc p) n -> p kc n", p=128)
                wd = I['w_down'][l].rearrange("(kc p) n -> p kc n", p=128)
                for q in range(4):
                    for g in range(2):
                        c0 = (2 * q + g) * 512
                        sched.append((('wup', blk, l, q, g),
                                      [(lambda t: t[:, :].rearrange("p (kc n) -> p kc n", kc=8), wu[:, :, c0:c0 + 512])]))
                    for jh in range(2):
                        sched.append((('wdn', blk, l, q, jh),
                                      [(lambda t: t[:, :].rearrange("p (kc n) -> p kc n", kc=8),
                                        wd[:, q * 8:(q + 1) * 8, jh * 512:(jh + 1) * 512])]))
        return sched

    def wnext(self, tag):
        P = self.P
        i = self.w_i
        if K_ASTOP:
            while self.sched[self.w_i][0] != tag:
                self.w_i += 1
            i = self.w_i
            self.w_issued = max(self.w_issued, i)
        assert self.sched[i][0] == tag, (self.sched[i][0], tag)
        while self.w_issued < min(len(self.sched), i + 2):
            j = self.w_issued
            slot = j % 3
            t = self.ring[slot]
            key = 'wr%d' % slot
            for (dst_fn, src) in self.sched[j][1]:
                P.dma(V(dst_fn(t), key), src, key, eng='pool')
            self.w_issued += 1
        self.w_i += 1
        return self.ring[i % 3], 'wr%d' % (i % 3)

    def load_x_block(self, blk):
        P = self.P
        identf = V(self.identf[:], 'identf')
        for i in range(8):
            stg = self.R[i % 2]
            sk = 'R%d' % (i % 2)
            P.dma(V(stg[:, 0:1024], sk), self.I['xp'][blk * TP + i * 128:blk * TP + (i + 1) * 128, :], sk)
            for half in range(2):
                pn, ps = P.ps()
                for c in range(4):
                    cc = half * 4 + c
                    P.tr(V(ps[:, c * 128:(c + 1) * 128], pn), V(stg[:, cc * 128:(cc + 1) * 128], sk), identf)
                P.copy(V(self.xT[:, half * 4:(half + 1) * 4, i * 128:(i + 1) * 128], 'xT'),
                       V(ps[:, :].rearrange("p (c n) -> p c n", c=4), pn), eng='act' if half else 'dve')
        stg = self.R[0]
        P.dma(V(stg[0:TS, 0:1024], 'R0'), self.I['xs'][blk * TS:(blk + 1) * TS, :], 'R0')
        pn, ps = P.ps()
        for c in range(8):
            P.tr(V(ps[:, c * 32:(c + 1) * 32], pn), V(stg[0:32, c * 128:(c + 1) * 128], 'R0'),
                 V(self.identf[0:32, 0:32], 'identf'))
        P.copy(V(self.xT[:, :, TP:T], 'xT'), V(ps[:, 0:256].rearrange("p (c n) -> p c n", c=8), pn))

    def xkeys(self, tt):
        return ('xT', tt)

    def rmsnorm_to_h(self, gname, l):
        P = self.P
        onesb = V(self.onesb[:], 'onesb')
        def g(ti, t0, t1, n):
            pn, ps = P.ps()
            sq = self.Bt[ti]
            sqk = 'B%d' % ti
            for g3, (c0, c1) in enumerate([(0, 3), (3, 6), (6, 8)]):
                P.act(V(sq[:, 0:(c1 - c0) * n].rearrange("p (c n) -> p c n", c=c1 - c0), sqk),
                      V(self.xT[:, c0:c1, t0:t1], ('xT', ti)), AF.Square)
                for c in range(c0, c1):
                    P.mm(V(ps[:, 0:n], pn), onesb, V(sq[:, (c - c0) * n:(c - c0 + 1) * n], sqk),
                         start=(c == 0), stop=(c == 7))
                yield
            rs = V(self.small[:, ti, 0:n], ('small', ti))
            P.act(rs, V(ps[:, 0:n], pn), AF.Ln, bias=self.epsc, scale=1.0 / D)
            P.act(rs, rs, AF.Exp, scale=-0.5)
            yield
            for c in range(KC):
                gcol = self.pcol(gname, (l * 8 + c) if l is not None else c)
                P.stt(V(self.hT[:, c, t0:t1], ('hT', ti)), V(self.xT[:, c, t0:t1], ('xT', ti)), gcol, rs,
                      ALU.mult, ALU.mult)
                if c % 4 == 3:
                    yield
        self.ti_pipe(g)

    def proj(self, wt, wkey, colsel, M, evac):
        P = self.P
        for ti, (t0, t1) in enumerate(TT):
            n = t1 - t0
            pn, ps = P.ps()
            for kc in range(KC):
                P.mm(V(ps[0:M, 0:n], pn), V(colsel(wt, kc), wkey), V(self.hT[:, kc, t0:t1], ('hT', ti)),
                     start=(kc == 0), stop=(kc == KC - 1))
            evac(ti, t0, t1, V(ps[0:M, 0:n], pn))

    def evac_to_X(self, Xt, Xk, hist):
        P = self.P
        w = hist + 4
        base = hist + TP

        def f(ti, t0, t1, psv):
            if ti < 2:
                P.copy(V(Xt[:, hist + t0:hist + t1], Xk), psv, eng='act')
            else:
                npr = TP - t0
                P.copy(V(Xt[:, hist + t0:hist + TP], Xk), V(psv.ap[:, 0:npr], *psv.keys), eng='act')
                P.copy(V(Xt[:, base:base + NS * w].rearrange("p (s j) -> p s j", j=w)[:, :, hist:w], Xk),
                       V(psv.ap[:, npr:npr + TS].rearrange("p (s j) -> p s j", j=4), *psv.keys), eng='dve')
        return f

    def conv4(self, out, ok, Xt, Xk, wname, wrow_fn):
        P = self.P
        hist = 3
        base = hist + TP
        for j in range(3, -1, -1):
            wc = self.pcol(wname, wrow_fn(j))
            src = V(Xt[:, j:j + TP], Xk)
            dst = V(out[:, 0:TP], ok)
            if j == 3:
                P.ts(dst, src, wc, None, ALU.mult)
            else:
                P.stt(dst, src, wc, dst, ALU.mult, ALU.add)
            srcs = V(Xt[:, base:base + NS * 7].rearrange("p (s j) -> p s j", j=7)[:, :, j:j + 4], Xk)
            dsts = V(out[:, TP:T].rearrange("p (s j) -> p s j", j=4), ok)
            if j == 3:
                P.ts(dsts, srcs, wc, None, ALU.mult)
            else:
                P.stt(dsts, srcs, wc, dsts, ALU.mult, ALU.add)

    def load_hist_T(self, src_rows, nrows, ncol_chunks, dst_fn):
        P = self.P
        stg = self.stage[0]
        P.dma(V(stg[0:nrows, 0:ncol_chunks * 128], 'stg0'), src_rows, 'stg0')
        for c in range(ncol_chunks):
            pn, ps = P.ps()
            P.tr(V(ps[:, 0:nrows], pn), V(stg[0:nrows, c * 128:(c + 1) * 128], 'stg0'),
                 V(self.identf[0:nrows, 0:nrows], 'identf'))
            dst_fn(c, V(ps[:, 0:nrows], pn))

    def store_rows(self, dram_ap, psv, nrows, ncols):
        P = self.P
        stg = self.stage[0]
        P.copy(V(stg[0:nrows, 0:ncols], 'stg0'), psv, eng='act')
        P.dma(dram_ap, V(stg[0:nrows, 0:ncols], 'stg0'), 'stg0')

    def rows_mm(self, tok_ap_fn, M, wt, wkey, colsel_n, ncols):
        P = self.P
        pn, ps = P.ps()
        for kc in range(KC):
            P.mm(V(ps[0:M, 0:ncols], pn), V(tok_ap_fn(kc), 'hT'), V(colsel_n(wt, kc), wkey),
                 start=(kc == 0), stop=(kc == KC - 1))
        return V(ps[0:M, 0:ncols], pn)

    def neumann(self, Nb, NTb, n, G, levels, X32, Xb, keys, fp32=False):
        P = self.P
        kN, kNT, kX32, kXb = keys
        idb = V(self.identf[0:n, 0:n].unsqueeze(1).to_broadcast([n, G, n]), 'identf')
        P.tt(V(X32[0:n, 0:G * n].rearrange("p (g n) -> p g n", g=G), kX32),
             V(NTb[0:n, 0:G * n].rearrange("p (g n) -> p g n", g=G), kNT), idb, ALU.add)
        if fp32:
            Xop, kXop = X32, kX32
        else:
            Xop, kXop = Xb, kXb
            P.copy(V(Xb[0:n, 0:G * n], kXb), V(X32[0:n, 0:G * n], kX32), eng='act')
        for m in range(1, levels):
            last = (m == levels - 1)
            pn1, ps1 = P.ps()
            for g in range(G):
                sl = slice(g * n, (g + 1) * n)
                P.mm(V(ps1[0:n, sl], pn1), V(NTb[0:n, sl], kNT), V(Nb[0:n, sl], kN))
            if not last:
                pn2, ps2 = P.ps()
                for g in range(G):
                    sl = slice(g * n, (g + 1) * n)
                    P.mm(V(ps2[0:n, sl], pn2), V(Nb[0:n, sl], kN), V(NTb[0:n, sl], kNT))
            yield
            P.copy(V(Nb[0:n, 0:G * n], kN), V(ps1[0:n, 0:G * n], pn1), eng='act')
            if not last:
                P.copy(V(NTb[0:n, 0:G * n], kNT), V(ps2[0:n, 0:G * n], pn2), eng='dve')
            yield
            pn3, ps3 = P.ps()
            for g in range(G):
                sl = slice(g * n, (g + 1) * n)
                P.mm(V(ps3[0:n, sl], pn3), V(Nb[0:n, sl], kN), V(Xop[0:n, sl], kXop))
            if not fp32:
                P.tt(V(Xb[0:n, 0:G * n], kXb), V(ps3[0:n, 0:G * n], pn3), V(X32[0:n, 0:G * n], kX32), ALU.add)
                yield
                if not last:
                    P.tt(V(X32[0:n, 0:G * n], kX32), V(ps3[0:n, 0:G * n], pn3), V(X32[0:n, 0:G * n], kX32), ALU.add)
            else:
                P.tt(V(X32[0:n, 0:G * n], kX32), V(ps3[0:n, 0:G * n], pn3), V(X32[0:n, 0:G * n], kX32), ALU.add)
                yield
                if last:
                    P.copy(V(Xb[0:n, 0:G * n], kXb), V(X32[0:n, 0:G * n], kX32), eng='act')
            yield

    def branch_C(self, blk, l):
        P = self.P
        I, O = self.I, self.O
        R, Bt = self.R, self.Bt
        s0 = blk * NS
        for g in range(2):
            for ax, nm in enumerate(['c_wa', 'c_wx']):
                src = I[nm][l].rearrange("(c g) i j -> g i c j", g=2)[g]
                P.dma(V(self.cgate[g * 64:(g + 1) * 64, :, ax, g * 64:(g + 1) * 64], 'cgate'), src, 'cgate',
                      eng='pool')
        wx_t, wx_k = self.wnext(('C_x', blk, l))
        w512 = lambda t: t[:, :].rearrange("p (kc n) -> p kc n", kc=8)
        csm = self.csm
        self.load_hist_T(I['sc_h'][l, s0:s0 + NS, :], NS, 4,
                         lambda c, psv: P.copy(V(csm[:, 0, c * 8:(c + 1) * 8], ('csm', 0)), psv))
        self.load_hist_T(I['sc_conv'][l, s0:s0 + NS].rearrange("s j c -> (s j) c"), NS * 3, 4,
                         lambda c, psv: P.copy(V(csm[:, 1, c * 24:(c + 1) * 24], ('csm', 1)), psv))
        for j in range(3):
            psv = self.rows_mm(lambda kc, j=j: self.hT[:, kc, TP:T].rearrange("p (s j) -> p s j", j=4)[:, :, 1 + j],
                               NS, wx_t, wx_k, lambda t, kc: w512(t)[:, kc, :], 512)
            self.store_rows(O['s_c_conv'][l, s0:s0 + NS, j, :], psv, NS, 512)
        if blk == NBLK - 1:
            psv = self.rows_mm(lambda kc: self.hT[:, kc, TP - 3:TP], 3, wx_t, wx_k, lambda t, kc: w512(t)[:, kc, :], 512)
            self.store_rows(O['p_c_conv'][l], psv, 3, 512)
        wg_t, wg_k = self.wnext(('C_g', blk, l))
        for c in range(4):
            X, Xk = (R[0], 'R0') if c % 2 == 0 else (R[6], 'R6')
            P.copy(V(X[:, 0:3], Xk), V(self.tailC[:, l, c, :], 'tailC'))
            P.copy(V(X[:, 3 + TP:3 + TP + NS * 7].rearrange("p (s j) -> p s j", j=7)[:, :, 0:3], Xk),
                   V(csm[:, 1, c * 24:(c + 1) * 24].rearrange("p (s j) -> p s j", j=3), ('csm', 1)))
            self.proj(wx_t, wx_k, lambda t, kc, c=c: w512(t)[:, kc, c * 128:(c + 1) * 128], 128,
                      self.evac_to_X(X, Xk, 3))
            P.copy(V(self.tailC[:, l, c, :], 'tailC'), V(X[:, TP:TP + 3], Xk))
            xc, xck = R[1], 'R1'
            self.conv4(xc, xck, X, Xk, 'c_conv_w', lambda j, c=c: (l * 4 + j) * 4 + c)
            P.ts(V(xc[:, 0:T], xck), V(xc[:, 0:T], xck), self.pcol('c_conv_b', l * 4 + c), None, ALU.add)
            xcb, xcbk = Bt[2], 'B2'
            P.copy(V(xcb[:, 0:T], xcbk), V(xc[:, 0:T], xck), eng='act')
            cl = V(csm[:, 2, 0:1], ('csm', 2))
            cl2 = V(csm[:, 2, 1:2], ('csm', 2))
            P.act(cl, self.pcol('c_L', l * 4 + c), AF.Exp, scale=-1.0)
            P.act(cl, cl, AF.Ln, bias=self.onec)
            P.ts(cl2, cl, -16.0, None, ALU.mult)
            P.ts(cl, cl, -8.0, None, ALU.mult)
            ra, rak = R[2], 'R2'
            ri, rik = R[3], 'R3'
            for ti, (t0, t1) in enumerate(TT):
                n = t1 - t0
                pn, ps = P.ps()
                P.mm(V(ps[:, 0:n], pn), V(self.cgate[:, c, 0, :], 'cgate'), V(xcb[:, t0:t1], xcbk))
                P.act(V(ra[:, t0:t1], rak), V(ps[:, 0:n], pn), AF.Sigmoid, bias=self.pcol('c_ba', l * 4 + c))
                pn, ps = P.ps()
                P.mm(V(ps[:, 0:n], pn), V(self.cgate[:, c, 1, :], 'cgate'), V(xcb[:, t0:t1], xcbk))
                P.act(V(ri[:, t0:t1], rik), V(ps[:, 0:n], pn), AF.Sigmoid, bias=self.pcol('c_bx', l * 4 + c))
            s_, sk = R[4], 'R4'
            P.act(V(s_[:, 0:T], sk), V(ra[:, 0:T], rak), AF.Exp, scale=cl2)
            P.act(V(s_[:, 0:T], sk), V(s_[:, 0:T], sk), AF.Sqrt, scale=-1.0, bias=self.onec)
            P.act(V(ra[:, 0:T], rak), V(ra[:, 0:T], rak), AF.Exp, scale=cl)
            P.tt(V(ri[:, 0:T], rik), V(ri[:, 0:T], rik), V(xc[:, 0:T], xck), ALU.mult)
            P.tt(V(s_[:, 0:T], sk), V(s_[:, 0:T], sk), V(ri[:, 0:T], rik), ALU.mult)
            a_, ak = ra, rak
            a_s = V(a_[:, TP:T].rearrange("p (s j) -> p s j", j=4)[:, :, 0], ak)
            b_s = V(s_[:, TP:T].rearrange("p (s j) -> p s j", j=4)[:, :, 0], sk)
            h0s = V(csm[:, 0, c * 8:(c + 1) * 8], ('csm', 0))
            tmp8 = V(csm[:, 2, 8:16], ('csm', 2))
            P.tt(tmp8, a_s, h0s, ALU.mult)
            P.tt(b_s, b_s, tmp8, ALU.add)
            P.memset(a_s, 0.0, eng='dve')
            hh, hk = R[5], 'R5'
            P.scan(V(hh[:, 0:T], hk), V(a_[:, 0:T], ak), V(s_[:, 0:T], sk), V(self.hC[:, l, c:c + 1], 'hC'),
                   ALU.mult, ALU.add)
            P.copy(V(self.hC[:, l, c:c + 1], 'hC'), V(hh[:, TP - 1:TP], hk))
            P.copy(V(csm[:, 3, c * 8:(c + 1) * 8], ('csm', 3)),
                   V(hh[:, TP:T].rearrange("p (s j) -> p s j", j=4)[:, :, 3], hk))

            pss = []
            for ti, (t0, t1) in enumerate(TT):
                n = t1 - t0
                pn, ps = P.ps()
                for kc in range(KC):
                    P.mm(V(ps[:, 0:n], pn), V(w512(wg_t)[:, kc, c * 128:(c + 1) * 128], wg_k),
                         V(self.hT[:, kc, t0:t1], ('hT', ti)), start=(kc == 0), stop=(kc == KC - 1))
                pss.append(V(ps[:, 0:n], pn))

            def gg(ti, t0, t1, n, c=c, pss=pss):
                psv = pss[ti]
                tA = V(self.small[:, ti, 0:n], ('small', ti))
                P.act(tA, psv, AF.Square)
                yield
                P.ts(tA, tA, 0.044715, 1.0, ALU.mult, ALU.add)
                P.tt(tA, psv, tA, ALU.mult)
                yield
                P.act(tA, tA, AF.Sigmoid, scale=1.5957691216)
                yield
                P.tt(tA, V(hh[:, t0:t1], hk), tA, ALU.mult)
                P.tt(V(self.yb[:, 8 + c, t0:t1], ('yb', 8 + c, ti)), psv, tA, ALU.mult)
                yield
            self.ti_pipe(gg)
        pn, ps = P.ps()
        for c in range(4):
            P.tr(V(ps[0:NS, c * 128:(c + 1) * 128], pn), V(csm[:, 3, c * 8:(c + 1) * 8], ('csm', 3)),
                 V(self.identf[:], 'identf'))
        self.store_rows(O['s_c_h'][l, s0:s0 + NS, :], V(ps[0:NS, 0:512], pn), NS, 512)
        if blk == NBLK - 1:
            pn, ps = P.ps()
            P.tr(V(ps[0:4, 0:128], pn), V(self.hC[:, l, :], 'hC'), V(self.identf[:], 'identf'))
            self.store_rows(O['p_c_h'][l].rearrange("(c p) -> c p", p=128), V(ps[0:4, 0:128], pn), 4, 128)

    def branch_A(self, blk, l):
        P = self.P
        I, O = self.I, self.O
        R, Bt = self.R, self.Bt
        s0 = blk * NS
        identf = V(self.identf[:], 'identf')
        wb_t, wb_k = self.wnext(('A_ba', blk, l))
        wba = lambda t: t[:, 0:64].rearrange("p (kc n) -> p kc n", kc=8)
        rows, rowsk = R[1], 'R1'
        negA = V(self.a4[:, l, 0:1], 'a4')
        dtb = V(self.a4[:, l, 1:2], 'a4')
        one4 = V(self.onec.ap[0:4, :], 'consts')
        colsA = self.colsA

        self.proj(wb_t, wb_k, lambda t, kc: wba(t)[:, kc, 0:4], 4,
                  lambda ti, t0, t1, psv: P.act(V(rows[0:4, t0:t1], rowsk), psv, AF.Sigmoid))
        for c in range(NCHUNK):
            t0, n, nseq = chunk_info(c)
            pn, ps = P.ps()
            P.tr(V(ps[0:n, 0:4], pn), V(rows[0:4, t0:t0 + n], rowsk), V(self.identf[0:4, 0:4], 'identf'))
            P.copy(V(colsA[0:n, c, 0:4], ('colsA', c)), V(ps[0:n, 0:4], pn))

        def ev_g(ti, t0, t1, psv):
            gv = V(rows[0:4, t0:t1], rowsk)
            P.act(gv, psv, AF.Exp, bias=dtb)
            P.act(gv, gv, AF.Ln, bias=one4)
            P.ts(gv, gv, negA, None, ALU.mult)
        self.proj(wb_t, wb_k, lambda t, kc: wba(t)[:, kc, 4:8], 4, ev_g)
        for c in range(NCHUNK):
            t0, n, nseq = chunk_info(c)
            mi = 0 if c < 8 else 1
            ck = ('colsA', c)
            pn, ps = P.ps()
            P.tr(V(ps[0:n, 0:4], pn), V(rows[0:4, t0:t0 + n], rowsk), V(self.identf[0:4, 0:4], 'identf'))
            P.copy(V(colsA[0:n, c, 4:8], ck), V(ps[0:n, 0:4], pn))
            pn, ps = P.ps()
            P.mm(V(ps[0:n, 0:4], pn), V(self.m_uincl[0:n, mi, 0:n], 'm_uincl'), V(colsA[0:n, c, 4:8], ck))
            P.mm(V(ps[0:n, 4:8], pn), V(self.m_lstr[0:n, mi, 0:n], 'm_lstr'), V(colsA[0:n, c, 4:8], ck))
            P.copy(V(colsA[0:n, c, 8:16], ck), V(ps[0:n, 0:8], pn))
            P.act(V(colsA[0:n, c, 16:24], ck), V(colsA[0:n, c, 8:16], ck), AF.Exp)
            P.tt(V(colsA[0:n, c, 16:20], ck), V(colsA[0:n, c, 16:20], ck), V(colsA[0:n, c, 0:4], ck), ALU.mult)
            P.ts(V(colsA[0:n, c, 4:8], ck), V(colsA[0:n, c, 0:4], ck), -1.0, None, ALU.mult)

        self.astop(1)
        for hd in range(4):
            wt, wk = self.wnext(('A_head', blk, l, hd))
            wv = lambda t: t[:, :].rearrange("p (kc c n) -> p kc c n", kc=8, c=4)
            P.dma(V(self.Ss[:, :].rearrange("p (s e) -> p s e", s=NS), 'Ss'),
                  I['sa_S'][l, s0:s0 + NS, hd].rearrange("s d e -> d s e"), 'Ss')
            stg = self.stage[0]
            for cc in range(3):
                P.dma(V(stg[0:24, cc * 128:(cc + 1) * 128], 'stg0'),
                      I['sa_conv'][l, s0:s0 + NS].rearrange("s j c -> (s j) c")[:, cc * 512 + hd * 128:cc * 512 + (hd + 1) * 128],
                      'stg0')
            pnh, psh = P.ps()
            for cc in range(3):
                P.tr(V(psh[:, cc * 24:(cc + 1) * 24], pnh), V(stg[0:24, cc * 128:(cc + 1) * 128], 'stg0'),
                     V(self.identf[0:24, 0:24], 'identf'))
            hist = V(self.csm[:, 0, 0:72], ('csm', 0))
            P.copy(hist, V(psh[:, 0:72], pnh))
            Cs = [(R[3], 'R3'), (R[4], 'R4'), (R[5], 'R5')]
            for cc in range(3):
                X, Xk = (R[0], 'R0') if cc % 2 == 0 else (R[2], 'R2')
                P.copy(V(X[:, 3 + TP:3 + TP + NS * 7].rearrange("p (s j) -> p s j", j=7)[:, :, 0:3], Xk),
                       V(self.csm[:, 0, cc * 24:(cc + 1) * 24].rearrange("p (s j) -> p s j", j=3), ('csm', 0)))
                P.copy(V(X[:, 0:3], Xk), V(self.tailA[:, l, cc * 4 + hd, :], 'tailA'))
                self.proj(wt, wk, lambda t, kc, cc=cc: wv(t)[:, kc, cc, :], 128, self.evac_to_X(X, Xk, 3))
                P.copy(V(self.tailA[:, l, cc * 4 + hd, :], 'tailA'), V(X[:, TP:TP + 3], Xk))
                Cc, Ck = Cs[cc]
                self.conv4(Cc, Ck, X, Xk, 'a_conv_w', lambda j, cc=cc: (l * 4 + j) * 12 + cc * 4 + hd)
                P.act(V(Cc[:, 0:T], Ck), V(Cc[:, 0:T], Ck), AF.Silu)
            self.astop(2)
            for j in range(3):
                psv = self.rows_mm(lambda kc, j=j: self.hT[:, kc, TP:T].rearrange("p (s j) -> p s j", j=4)[:, :, 1 + j],
                                   NS, wt, wk, lambda t, kc: t[:, kc * 512:kc * 512 + 384], 384)
                P.copy(V(stg[0:NS, 0:384], 'stg0'), psv, eng='act')
                for cc in range(3):
                    P.dma(O['s_a_conv'][l, s0:s0 + NS, j, cc * 512 + hd * 128:cc * 512 + (hd + 1) * 128],
                          V(stg[0:NS, cc * 128:(cc + 1) * 128], 'stg0'), 'stg0')
            if blk == NBLK - 1:
                psv = self.rows_mm(lambda kc: self.hT[:, kc, TP - 3:TP], 3, wt, wk,
                                   lambda t, kc: t[:, kc * 512:kc * 512 + 384], 384)
                P.copy(V(stg[0:3, 0:384], 'stg0'), psv, eng='act')
                for cc in range(3):
                    P.dma(O['p_a_conv'][l, :, cc * 512 + hd * 128:cc * 512 + (hd + 1) * 128],
                          V(stg[0:3, cc * 128:(cc + 1) * 128], 'stg0'), 'stg0')
            self.astop(3)
            zg, zgk = R[6], 'R6'
            self.proj(wt, wk, lambda t, kc: wv(t)[:, kc, 3, :], 128,
                      lambda ti, t0, t1, psv: P.act(V(zg[:, t0:t1], zgk), psv, AF.Silu))
            qn, qnk = Bt[0], 'B0'
            kn, knk = Bt[1], 'B1'
            knf, knfk = Cs[1]
            vf, vfk = Cs[2]
            for which, (Cc, Ck) in enumerate(Cs[0:2]):
                def g(ti, t0, t1, n, which=which, Cc=Cc, Ck=Ck):
                    sq = V(self.sqts[ti][:, 0:n], self.sqtk[ti])
                    P.act(sq, V(Cc[:, t0:t1], (Ck, ti)), AF.Square)
                    yield
                    pn, ps = P.ps()
                    P.mm(V(ps[:, 0:n], pn), V(self.onesb[:], 'onesb'), sq)
                    yield
                    rs = V(self.small[:, ti, 0:n], ('small', ti))
                    P.act(rs, V(ps[:, 0:n], pn), AF.Ln, bias=self.epsc)
                    P.act(rs, rs, AF.Exp, scale=-0.5)
                    yield
                    if which == 0:
                        P.stt(V(qn[:, t0:t1], (qnk, ti)), V(Cc[:, t0:t1], (Ck, ti)), 128.0 ** -0.5, rs, ALU.mult, ALU.mult)
                    else:
                        P.tt(V(Cc[:, t0:t1], (Ck, ti)), V(Cc[:, t0:t1], (Ck, ti)), rs, ALU.mult)
                        P.copy(V(kn[:, t0:t1], (knk, ti)), V(Cc[:, t0:t1], (Ck, ti)), eng='act')
                    yield
                self.ti_pipe(g)
            lc, lck = R[7], 'R7'
            for ti, (t0, t1) in enumerate(TT):
                n = t1 - t0
                gmv = V(self.small[0:4, 2, 0:n], ('small', 2))
                P.ts(gmv, V(rows[0:4, t0:t1], rowsk), V(self.identf[0:4, hd:hd + 1], 'identf'), None, ALU.mult)
                pn, ps = P.ps()
                P.mm(V(ps[:, 0:n], pn), V(self.onesf[0:4, :], 'onesf'), gmv)
                P.copy(V(lc[:, t0:t1], lck), V(ps[:, 0:n], pn), eng='act')
            P.scan(V(lc[:, 0:T], lck), V(self.rmask[:, 0:T], 'rmask'), V(lc[:, 0:T], lck), 0.0, ALU.mult, ALU.add)
            qg, qgk = Bt[2], 'B2'
            for ti, (t0, t1) in enumerate(TT):
                n = t1 - t0
                ev = V(self.small[:, 3, 0:n], ('small', 3))
                P.act(ev, V(lc[:, t0:t1], lck), AF.Exp)
                P.tt(V(qg[:, t0:t1], qgk), V(qn[:, t0:t1], qnk), ev, ALU.mult)
            P.act(V(self.glc[:, 0:8], 'glc'), V(lc[:, 0:TP].rearrange("p (c n) -> p c n", n=128)[:, :, 127], lck), AF.Exp)
            P.act(V(self.glc[:, 8:16], 'glc'), V(lc[:, TP:T].rearrange("p (s j) -> p s j", j=4)[:, :, 3], lck), AF.Exp)
            self.astop(4)
            rw, rwk = Bt[4], 'B4'
            kd, kdk = Bt[5], 'B5'
            rv, rvk = Bt[6], 'B6'
            aT, aTk = Bt[7], 'B7'
            nw, nwk = Bt[8], 'B8'
            Xb, Xbk = Bt[9], 'B9'
            for c in range(NCHUNK):
                t0, n, nseq = chunk_info(c)
                co = c * 128
                ck = ('colsA', c)
                pn, ps = P.ps()
                P.tr(V(ps[0:n, 0:128], pn), V(knf[:, t0:t0 + n], knfk), identf)
                P.tr(V(ps[0:n, 128:256], pn), V(vf[:, t0:t0 + n], vfk), identf)
                P.ts(V(rw[0:n, co:co + 128], rwk), V(ps[0:n, 0:128], pn), V(colsA[0:n, c, 16 + hd:17 + hd], ck), None, ALU.mult)
                P.act(V(kd[0:n, co:co + 128], kdk), V(ps[0:n, 0:128], pn), AF.Identity, scale=V(colsA[0:n, c, 20 + hd:21 + hd], ck))
                P.act(V(rv[0:n, co:co + 128], rvk), V(ps[0:n, 128:256], pn), AF.Identity, scale=V(colsA[0:n, c, hd:hd + 1], ck))
            self.astop(5)
            nsets = [(R[0][:, 0:512], 'R0', R[0][:, 512:1024], 'R0', self.c128f[0], 'cf0', self.c128b[2], 'cb2',
                      self.c128f[1], 'cf1', self.c128f[2], 'cf2'),
                     (R[2][:, 0:512], 'R2', R[2][:, 512:1024], 'R2', R[3][:, 512:1024], ('R3', 1), self.c128b[0], 'cb0',
                      self.c128f[3], 'cf3', self.c128f[4], 'cf4')]
            def gen_GN(cs, G, ns):
                Nb, Nbk, NTb, NTbk, X32, X32k, XbT, XbTk, d1t, d1k, d2t, d2k = ns
                n = 128 if G == 4 else TS
                mi = 0 if G == 4 else 1
                for gi, c in enumerate(cs):
                    t0 = chunk_info(c)[0]
                    ck = ('colsA', c)
                    sl = slice(gi * n, (gi + 1) * n)
                    pn, ps = P.ps()
                    P.mm(V(ps[0:n, 0:n], (pn, 0)), V(kn[:, t0:t0 + n], knk), V(kn[:, t0:t0 + n], knk))
                    P.mm(V(ps[0:n, 128:128 + n], (pn, 1)), V(kn[:, t0:t0 + n], knk), V(qn[:, t0:t0 + n], qnk))
                    d1 = V(d1t[0:n, 0:n], d1k)
                    P.stt(d1, V(lc[0:n, t0:t0 + n], lck), V(colsA[0:n, c, 8 + hd:9 + hd], ck),
                          V(self.m_bigL[0:n, mi, 0:n], 'm_bigL'), ALU.subtract, ALU.max)
                    P.act(d1, d1, AF.Exp, scale=-1.0)
                    P.stt(V(Nb[0:n, sl], Nbk), V(ps[0:n, 0:n], (pn, 0)), V(colsA[0:n, c, 4 + hd:5 + hd], ck), d1,
                          ALU.mult, ALU.mult)
                    d2 = V(d2t[0:n, 0:n], d2k)
                    P.stt(d2, V(lc[0:n, t0:t0 + n], lck), V(colsA[0:n, c, 8 + hd:9 + hd], ck),
                          V(self.m_negU[0:n, mi, 0:n], 'm_negU'), ALU.subtract, ALU.min)
                    P.act(d2, d2, AF.Exp)
                    P.tt(V(aT[0:n, c * 128:c * 128 + n], (aTk, c)), V(ps[0:n, 128:128 + n], (pn, 1)), d2, ALU.mult)
                    yield
                pn, ps = P.ps()
                for gi in range(G):
                    sl = slice(gi * n, (gi + 1) * n)
                    P.tr(V(ps[0:n, sl], pn), V(Nb[0:n, sl], Nbk), V(self.identf[0:n, 0:n], 'identf'))
                P.copy(V(NTb[0:n, 0:G * n], NTbk), V(ps[0:n, 0:G * n], pn))
                yield
                yield from self.neumann(Nb, NTb, n, G, NEU_LEVELS_P if G == 4 else NEU_LEVELS_S, X32, XbT,
                                        (Nbk, NTbk, X32k, XbTk), fp32=True)
                for gi, c in enumerate(cs):
                    sl = slice(gi * n, (gi + 1) * n)
                    P.copy(V(Xb[0:n, c * 128:c * 128 + n], (Xbk, c)), V(XbT[0:n, sl], XbTk), eng='dve')
                pn, ps = P.ps()
                for gi, c in enumerate(cs):
                    P.mm(V(ps[:, gi * n:(gi + 1) * n], pn), V(rw[0:n, c * 128:(c + 1) * 128], rwk),
                         V(XbT[0:n, gi * n:(gi + 1) * n], XbTk))
                t0 = chunk_info(cs[0])[0]
                P.act(V(nw[:, t0:t0 + G * n], *[(nwk, c_) for c_ in cs]), V(ps[:, 0:G * n], pn), AF.Identity, scale=-1.0)
                yield
            self.astop(6)
            oT, oTk = R[3], 'R3'
            S32 = V(self.SA[:, l, hd, :], ('SA', l, hd))
            Sb = V(self.SAb[:, hd, :], ('SAb', hd))
            P.copy(Sb, S32, eng='act')
            vn, vnk = self.c128b[3], 'cb3'
            def gen_chain(c_list):
              for c in c_list:
                t0 = c * 128
                co = c * 128
                pn1, ps1 = P.ps()
                pv = V(ps1[:, 0:128], pn1)
                P.mm(pv, V(Xb[:, co:co + 128], (Xbk, c)), V(rv[:, co:co + 128], rvk), start=True, stop=False)
                P.mm(pv, V(nw[:, t0:t0 + 128], (nwk, c)), Sb, start=False, stop=True)
                vnv = V(vn[:, (c % 2) * 128:(c % 2 + 1) * 128], (vnk, c % 2))
                P.copy(vnv, pv, eng='act')
                yield
                pn2, ps2 = P.ps()
                po = V(ps2[:, 0:128], pn2)
                P.mm(po, Sb, V(qg[:, t0:t0 + 128], qgk), start=True, stop=False)
                P.mm(po, vnv, V(aT[:, co:co + 128], (aTk, c)), start=False, stop=True)
                P.copy(V(oT[:, t0:t0 + 128], (oTk, c // 4)), po, eng='dve')
                yield
                pn3, ps3 = P.ps()
                pS = V(ps3[:, 0:128], pn3)
                P.mm(pS, V(kd[:, co:co + 128], kdk), vnv)
                P.stt(Sb, S32, V(self.glc[:, c:c + 1], 'glc'), pS, ALU.mult, ALU.add)
                P.stt(S32, S32, V(self.glc[:, c:c + 1], 'glc'), pS, ALU.mult, ALU.add)
                yield
            self.astop(61)
            pipe = Pipe()
            g0 = gen_GN((0, 1, 2, 3), 4, nsets[0])
            g1 = gen_GN((4, 5, 6, 7), 4, nsets[1])
            pipe.add(g0)
            pipe.add(g1)
            pipe.finish(g0)
            self.astop(62)
            pipe.run_with(gen_chain([0, 1, 2, 3]))
            pipe.finish(g1)
            self.astop(63)
            pipe.add(gen_GN((8,), 1, nsets[0]))
            pipe.run_with(gen_chain([4, 5, 6, 7]))
            pipe.drain_all()
            if blk == NBLK - 1:
                P.dma(O['p_a_S'][l, hd], S32, ('SA', l, hd))
            self.astop(7)
            c = 8
            t0 = TP
            co = c * 128
            Ss = V(self.Ss[:, :], 'Ss')
            ssb_t = R[2][:, 0:512].bitcast(BF16)
            Ssb = V(ssb_t, 'R2')
            P.copy(Ssb, Ss, eng='act')
            ex, exk = R[0], 'R0'
            red1 = V(self.c128f[3][0:TS, 0:128], 'cf3')
            red2 = V(self.c128f[4][0:TS, 0:128], 'cf4')

            def state_apply(lhsT_v, out_red):
                pn1, ps1 = P.ps()
                pn2, ps2 = P.ps()
                P.mm(V(ps1[0:TS, 0:512], pn1), lhsT_v, V(ssb_t[:, 0:512], 'R2'))
                P.mm(V(ps2[0:TS, 0:512], pn2), lhsT_v, V(ssb_t[:, 512:1024], 'R2'))
                P.tt(V(ex[0:TS, 0:512].rearrange("p (s e) -> p s e", s=4), exk),
                     V(ps1[0:TS, 0:512].rearrange("p (s e) -> p s e", s=4), pn1),
                     V(self.seqm[:, 0:4].unsqueeze(2).to_broadcast([TS, 4, 128]), 'seqm'), ALU.mult)
                P.tt(V(ex[0:TS, 512:1024].rearrange("p (s e) -> p s e", s=4), exk),
                     V(ps2[0:TS, 0:512].rearrange("p (s e) -> p s e", s=4), pn2),
                     V(self.seqm[:, 4:8].unsqueeze(2).to_broadcast([TS, 4, 128]), 'seqm'), ALU.mult)
                P.red(out_red, V(ex[0:TS, 0:1024].rearrange("p (s e) -> p e s", s=NS), exk))
            state_apply(V(nw[:, t0:t0 + TS], (nwk, 8)), red1)
            pn, ps = P.ps()
            P.mm(V(ps[0:TS, 0:128], pn), V(Xb[0:TS, co:co + TS], (Xbk, 8)), V(rv[0:TS, co:co + 128], rvk))
            vns32 = V(self.c128f[5][0:TS, 0:128], 'cf5')
            P.tt(vns32, V(ps[0:TS, 0:128], pn), red1, ALU.add)
            vns = V(vn[0:TS, 256:384], (vnk, 2))
            P.copy(vns, vns32, eng='act')
            state_apply(V(qg[:, t0:t0 + TS], qgk), red2)
            pn, ps = P.ps()
            P.mm(V(ps[0:TS, 0:128], pn), V(aT[0:TS, co:co + TS], (aTk, 8)), vns)
            P.tt(red2, V(ps[0:TS, 0:128], pn), red2, ALU.add)
            pn, ps = P.ps()
            P.tr(V(ps[:, 0:TS], pn), red2, V(self.identf[0:TS, 0:TS], 'identf'))
            P.copy(V(oT[:, t0:t0 + TS], oTk), V(ps[:, 0:TS], pn), eng='act')
            vex_t = R[0][0:TS, 0:512].bitcast(BF16)
            P.tt(V(vex_t.rearrange("p (s e) -> p s e", s=NS), exk),
                 V(vns32.ap.unsqueeze(1).to_broadcast([TS, NS, 128]), 'cf5'),
                 V(self.seqm[:, :].unsqueeze(2).to_broadcast([TS, NS, 128]), 'seqm'), ALU.mult)
            for hf in range(2):
                pn, ps = P.ps()
                P.mm(V(ps[:, 0:512], pn), V(kd[0:TS, co:co + 128], kdk), V(vex_t[:, hf * 512:(hf + 1) * 512], exk))
                ssl = V(self.Ss[:, hf * 512:(hf + 1) * 512].rearrange("p (s e) -> p s e", s=4), 'Ss')
                glb = V(self.glc[:, 8 + hf * 4:8 + (hf + 1) * 4].unsqueeze(2).to_broadcast([128, 4, 128]), 'glc')
                P.tt(ssl, ssl, glb, ALU.mult)
                P.tt(ssl, ssl, V(ps[:, 0:512].rearrange("p (s e) -> p s e", s=4), pn), ALU.add)
            P.dma(O['s_a_S'][l, s0:s0 + NS, hd].rearrange("s d e -> d s e"),
                  V(self.Ss[:, :].rearrange("p (s e) -> p s e", s=NS), 'Ss'), 'Ss')
            self.astop(8)
            def g8(ti, t0, t1, n):
                sq = V(self.sqts[ti][:, 0:n], self.sqtk[ti])
                P.act(sq, V(oT[:, t0:t1], oTk), AF.Square)
                yield
                pn, ps = P.ps()
                P.mm(V(ps[:, 0:n], pn), V(self.onesb[:], 'onesb'), sq)
                yield
                rs = V(self.small[:, ti, 0:n], ('small', ti))
                P.act(rs, V(ps[:, 0:n], pn), AF.Ln, bias=self.epsc, scale=1.0 / 128)
                P.act(rs, rs, AF.Exp, scale=-0.5)
                yield
                tmp = V(self.small[:, 3, 0:n], ('small', 3))
                P.stt(tmp, V(oT[:, t0:t1], oTk), self.pcol('a_norm_g', l), rs, ALU.mult, ALU.mult)
                P.tt(V(self.yb[:, hd, t0:t1], ('yb', hd, ti)), tmp, V(zg[:, t0:t1], zgk), ALU.mult)
                yield
            self.ti_pipe(g8)

    def branch_B(self, blk, l):
        P = self.P
        I, O = self.I, self.O
        R, Bt = self.R, self.Bt
        s0 = blk * NS
        identf = V(self.identf[:], 'identf')
        csm = self.csm
        P.dma(V(self.lorW[0:64, :], 'lorW'), I['b_w_up'][l], 'lorW', eng='pool')
        P.dma(V(self.lorW[64:128, :], 'lorW'), I['b_a_up'][l], 'lorW', eng='pool')
        P.dma(V(self.gupW[:, :], 'gupW'), I['b_g_up'][l], 'gupW', eng='pool')
        for p in range(4):
            P.ts(V(self.rkbd[:, p, :], 'rkbd'), V(self.ones64f[:], 'ones64f'), self.pcol('b_r_k', l * 4 + p), None, ALU.mult)
        stg = self.stage[0]
        for pc in range(4):
            c0 = pc * 512
            w = min(512, 1792 - c0)
            P.dma(V(stg[0:NS, 0:w], 'stg0'), I['sb_shift'][l, s0:s0 + NS, 0, c0:c0 + w], 'stg0')
            for cc in range(w // 128):
                ch = pc * 4 + cc
                pn, ps = P.ps()
                P.tr(V(ps[:, 0:NS], pn), V(stg[0:NS, cc * 128:(cc + 1) * 128], 'stg0'), V(self.identf[0:NS, 0:NS], 'identf'))
                if ch < 12:
                    P.copy(V(csm[:, 0, ch * 8:(ch + 1) * 8], ('csm', 0)), V(ps[:, 0:NS], pn))
                else:
                    P.copy(V(csm[:, 1, (ch - 12) * 8:(ch - 11) * 8], ('csm', 1)), V(ps[:, 0:NS], pn))

        def hist_of(ch):
            if ch < 12:
                return V(csm[:, 0, ch * 8:(ch + 1) * 8], ('csm', 0))
            return V(csm[:, 1, (ch - 12) * 8:(ch - 11) * 8], ('csm', 1))

        ZB = 1 + TP
        zsel = [0]

        def shift_proj(wt, wk, colsel, ch, out_t, out_k):
            Z, Zk = (R[0], 'R0') if zsel[0] % 2 == 0 else (R[4], 'R4')
            zsel[0] += 1
            P.copy(V(Z[:, 0:1], Zk), V(self.tailB[:, l, ch:ch + 1], 'tailB'))
            P.copy(V(Z[:, ZB:ZB + NS * 5].rearrange("p (s j) -> p s j", j=5)[:, :, 0], Zk), hist_of(ch))
            self.proj(wt, wk, colsel, 128, self.evac_to_X(Z, Zk, 1))
            P.copy(V(self.tailB[:, l, ch:ch + 1], 'tailB'), V(Z[:, TP:TP + 1], Zk))
            mu = self.pcol('b_mu', l * 14 + ch)
            P.tt(V(out_t[:, 0:TP], out_k), V(Z[:, 0:TP], Zk), V(Z[:, 1:1 + TP], Zk), ALU.subtract)
            P.stt(V(out_t[:, 0:TP], out_k), V(out_t[:, 0:TP], out_k), mu, V(Z[:, 1:1 + TP], Zk), ALU.mult, ALU.add)
            zs3 = Z[:, ZB:ZB + NS * 5].rearrange("p (s j) -> p s j", j=5)
            o3 = V(out_t[:, TP:T].rearrange("p (s j) -> p s j", j=4), out_k)
            P.tt(o3, V(zs3[:, :, 0:4], Zk), V(zs3[:, :, 1:5], Zk), ALU.subtract)
            P.stt(o3, o3, mu, V(zs3[:, :, 1:5], Zk), ALU.mult, ALU.add)

        wl_t, wl_k = self.wnext(('B_lora', blk, l))
        wl = lambda t: t[:, 0:2048].rearrange("p (kc n) -> p kc n", kc=8)
        lx, lxk = Bt[0], 'B0'
        sgx, sgxk = Bt[1], 'B1'
        tmpz, tmpzk = R[1], 'R1'
        shift_proj(wl_t, wl_k, lambda t, kc: wl(t)[:, kc, 0:128], 12, tmpz, tmpzk)
        P.act(V(lx[0:64, 0:T], lxk), V(tmpz[0:64, 0:T], tmpzk), AF.Tanh)
        P.copy(V(lx[64:128, 0:T], lxk), V(tmpz[64:128, 0:T], tmpzk), eng='act')
        shift_proj(wl_t, wl_k, lambda t, kc: wl(t)[:, kc, 128:256], 13, tmpz, tmpzk)
        P.act(V(sgx[:, 0:T], sgxk), V(tmpz[:, 0:T], tmpzk), AF.Sigmoid)
        psv = self.rows_mm(lambda kc: self.hT[:, kc, TP:T].rearrange("p (s j) -> p s j", j=4)[:, :, 3], NS, wl_t, wl_k,
                           lambda t, kc: wl(t)[:, kc, :], 256)
        self.store_rows(O['s_b_shift'][l, s0:s0 + NS, 0, 1536:1792], psv, NS, 256)
        if blk == NBLK - 1:
            psv = self.rows_mm(lambda kc: self.hT[:, kc, TP - 1:TP], 1, wl_t, wl_k, lambda t, kc: wl(t)[:, kc, :], 256)
            self.store_rows(O['p_b_shift'][l, :, 1536:1792], psv, 1, 256)

        cb = self.c128b
        cf = self.c128f
        for p in range(4):
            wt, wk = self.wnext(('B_pair', blk, l, p))
            wv = lambda t: t[:, 0:3072].rearrange("p (kc c n) -> p kc c n", kc=8, c=3)
            stS = R[5]
            for h_ in range(2):
                P.dma(V(stS[0:64, 0:1024].rearrange("p (s h k) -> p s h k", s=NS, h=2)[:, :, h_, :], 'R5'),
                      I['sb_S'][l, s0:s0 + NS, 2 * p + h_].rearrange("s v k -> v s k"), 'R5')
            if hasattr(self, 'marks'):
                self.marks.append(('  b-proj', P.cnt['pe'], P.cnt['act'], P.cnt['dve']))
            r32, r32k = R[1], 'R1'
            k32, k32k = R[2], 'R2'
            v32, v32k = R[3], 'R3'
            shift_proj(wt, wk, lambda t, kc: wv(t)[:, kc, 0, :], p, r32, r32k)
            shift_proj(wt, wk, lambda t, kc: wv(t)[:, kc, 1, :], 4 + p, k32, k32k)
            shift_proj(wt, wk, lambda t, kc: wv(t)[:, kc, 2, :], 8 + p, v32, v32k)
            pn, ps = P.ps()
            for sq_ in range(NS):
                P.tr(V(ps[:, sq_ * 64:(sq_ + 1) * 64], pn), V(stS[0:64, sq_ * 128:(sq_ + 1) * 128], 'R5'),
                     V(self.identf[0:64, 0:64], 'identf'))
            Hs = V(self.Ss[:, 0:512], 'Ss')
            P.copy(Hs, V(ps[:, 0:512], pn))
            Hsb_t = cb[7]
            Hsb = V(Hsb_t[:, 0:512], 'cb7')
            P.copy(Hsb, Hs, eng='act')
            psv = self.rows_mm(lambda kc: self.hT[:, kc, TP:T].rearrange("p (s j) -> p s j", j=4)[:, :, 3], NS, wt, wk,
                               lambda t, kc: t[:, kc * 384:(kc + 1) * 384], 384)
            P.copy(V(stg[0:NS, 0:384], 'stg0'), psv, eng='act')
            for cc in range(3):
                P.dma(O['s_b_shift'][l, s0:s0 + NS, 0, cc * 512 + p * 128:cc * 512 + (p + 1) * 128],
                      V(stg[0:NS, cc * 128:(cc + 1) * 128], 'stg0'), 'stg0')
            if blk == NBLK - 1:
                psv = self.rows_mm(lambda kc: self.hT[:, kc, TP - 1:TP], 1, wt, wk,
                                   lambda t, kc: t[:, kc * 384:(kc + 1) * 384], 384)
                P.copy(V(stg[0:1, 0:384], 'stg0'), psv, eng='act')
                for cc in range(3):
                    P.dma(O['p_b_shift'][l, :, cc * 512 + p * 128:cc * 512 + (p + 1) * 128],
                          V(stg[0:1, cc * 128:(cc + 1) * 128], 'stg0'), 'stg0')
            if hasattr(self, 'marks'):
                self.marks.append(('  b-lwag', P.cnt['pe'], P.cnt['act'], P.cnt['dve']))
            lw, lwk = R[4], 'R4'
            a32, a32k = R[5], 'R5'
            gb, gbk = Bt[9], 'B9'
            for ti, (t0, t1) in enumerate(TT):
                n = t1 - t0
                pn, ps = P.ps()
                P.mm(V(ps[:, 0:n], pn), V(self.lorW[0:64, p * 128:(p + 1) * 128], 'lorW'), V(lx[0:64, t0:t1], lxk))
                P.act(V(lw[:, t0:t1], lwk), V(ps[:, 0:n], pn), AF.Sigmoid, bias=self.pcol('b_w0', l * 4 + p))
                pn, ps = P.ps()
                P.mm(V(ps[:, 0:n], pn), V(self.lorW[64:128, p * 128:(p + 1) * 128], 'lorW'), V(lx[64:128, t0:t1], lxk))
                P.act(V(a32[:, t0:t1], a32k), V(ps[:, 0:n], pn), AF.Sigmoid, bias=self.pcol('b_a0', l * 4 + p))
                pn, ps = P.ps()
                P.mm(V(ps[:, 0:n], pn), V(self.gupW[:, p * 128:(p + 1) * 128], 'gupW'), V(sgx[:, t0:t1], sgxk))
                P.copy(V(gb[:, t0:t1], gbk), V(ps[:, 0:n], pn), eng='act')
            P.ts(V(lw[:, 0:T], lwk), V(lw[:, 0:T], lwk), -0.6065306597126334, None, ALU.mult)
            kkn, kknk = R[7], 'R7'
            kkc = self.pcol('b_k_k', l * 4 + p)
            def gk(ti, t0, t1, n):
                sq = V(self.sqts[ti][:, 0:n], self.sqtk[ti])
                P.act(sq, V(k32[:, t0:t1], k32k), AF.Square, scale=kkc)
                yield
                pn, ps = P.ps()
                P.mm(V(ps[:, 0:n], pn), V(self.ones64b[:], 'ones64b'), sq)
                yield
                rs = V(self.small[:, ti, 0:n], ('small', ti))
                P.act(rs, V(ps[:, 0:n], pn), AF.Ln, bias=self.epsc)
                P.act(rs, rs, AF.Exp, scale=-0.5)
                yield
                P.stt(V(kkn[:, t0:t1], (kknk, ti)), V(k32[:, t0:t1], k32k), kkc, rs, ALU.mult, ALU.mult)
                yield
            self.ti_pipe(gk)
            if hasattr(self, 'marks'):
                self.marks.append(('  b-elem', P.cnt['pe'], P.cnt['act'], P.cnt['dve']))
            kac = self.pcol('b_k_a', l * 4 + p)
            omk = V(csm[:, 2, 0:1], ('csm', 2))
            P.ts(omk, kac, -1.0, 1.0, ALU.mult, ALU.add)
            tmp, tmpk = R[6], 'R6'
            P.ts(V(tmp[:, 0:T], tmpk), V(a32[:, 0:T], a32k), kac, omk, ALU.mult, ALU.add)
            P.tt(V(k32[:, 0:T], k32k), V(k32[:, 0:T], k32k), V(tmp[:, 0:T], tmpk), ALU.mult)
            bon, bonk = R[6], 'R6'
            for ti, (t0, t1) in enumerate(TT):
                n = t1 - t0
                sq = V(self.sqt[:, 0:n], 'sqt')
                P.tt(sq, V(r32[:, t0:t1], r32k), V(k32[:, t0:t1], k32k), ALU.mult)
                pn, ps = P.ps()
                P.mm(V(ps[:, 0:n], pn), V(self.rkbd[:, p, :], 'rkbd'), sq)
                P.tt(V(bon[:, t0:t1], bonk), V(ps[:, 0:n], pn), V(v32[:, t0:t1], v32k), ALU.mult)
            lc, lck = R[0], 'R0'
            P.scan(V(lc[:, 0:T], lck), V(self.rmask[:, 0:T], 'rmask'), V(lw[:, 0:T], lwk), 0.0, ALU.mult, ALU.add)
            P.tt(V(lw[:, 0:T], lwk), V(lc[:, 0:T], lck), V(lw[:, 0:T], lwk), ALU.subtract)
            qt, qtk = Bt[2], 'B2'
            at, atk = Bt[3], 'B3'
            bt_, btk = Bt[4], 'B4'
            kt, ktk = Bt[5], 'B5'
            bh, bhk = R[1], 'R1'
            for ti, (t0, t1) in enumerate(TT):
                n = t1 - t0
                ev = V(self.small[:, 2, 0:n], ('small', 2))
                P.act(ev, V(lc[:, t0:t1], lck), AF.Exp)
                P.tt(V(qt[:, t0:t1], qtk), V(r32[:, t0:t1], r32k), ev, ALU.mult)
                ev2 = V(self.small[:, 3, 0:n], ('small', 3))
                P.act(ev2, V(lw[:, t0:t1], lwk), AF.Exp)
                P.stt(V(at[:, t0:t1], atk), V(kkn[:, t0:t1], kknk), -1.0, ev2, ALU.mult, ALU.mult)
            P.tt(V(bh[:, 0:T], bhk), V(kkn[:, 0:T], kknk), V(a32[:, 0:T], a32k), ALU.mult)
            for ti, (t0, t1) in enumerate(TT):
                n = t1 - t0
                ev = V(self.small[:, 2, 0:n], ('small', 2))
                P.act(ev, V(lc[:, t0:t1], lck), AF.Exp, scale=-1.0)
                P.tt(V(bt_[:, t0:t1], btk), V(bh[:, t0:t1], bhk), ev, ALU.mult)
                P.tt(V(kt[:, t0:t1], ktk), V(k32[:, t0:t1], k32k), ev, ALU.mult)
            P.act(V(self.glc[:, 0:8], 'glc'), V(lc[:, 0:TP].rearrange("p (c n) -> p c n", n=128)[:, :, 127], lck), AF.Exp)
            P.act(V(self.glc[:, 8:16], 'glc'), V(lc[:, TP:T].rearrange("p (s j) -> p s j", j=4)[:, :, 3], lck), AF.Exp)
            if hasattr(self, 'marks'):
                self.marks.append(('  b-tm', P.cnt['pe'], P.cnt['act'], P.cnt['dve']))
            bd, bdk = Bt[6], 'B6'
            kdt, kdtk = Bt[7], 'B7'
            vt, vtk = Bt[8], 'B8'
            for c in range(NCHUNK):
                t0, n, nseq = chunk_info(c)
                co = c * 128
                if c % 2 == 0:
                    edt, edk, f1t, f1k, f2t, f2k = cf[1], 'cf1', cf[2], 'cf2', cf[3], 'cf3'
                else:
                    edt, edk, f1t, f1k, f2t, f2k = cf[4], 'cf4', cf[5], 'cf5', cf[0], 'cf0'
                ed = V(edt[:, 0:n], edk)
                if c < 8:
                    P.act(ed, V(lc[:, t0:t0 + n], lck), AF.Exp, scale=-1.0, bias=V(lc[:, t0 + n - 1:t0 + n], lck))
                else:
                    lc3 = lc[:, TP:T].rearrange("p (s j) -> p s j", j=4)
                    P.tt(V(edt[:, 0:n].rearrange("p (s j) -> p s j", j=4), edk),
                         V(lc3[:, :, 3:4].to_broadcast([128, NS, 4]), lck), V(lc3, lck), ALU.subtract)
                    P.act(ed, ed, AF.Exp)
                f1 = V(f1t[:, 0:n], f1k)
                f2 = V(f2t[:, 0:n], f2k)
                P.tt(f1, V(bh[:, t0:t0 + n], bhk), ed, ALU.mult)
                P.tt(f2, V(k32[:, t0:t0 + n], k32k), ed, ALU.mult)
                pn, ps = P.ps()
                P.tr(V(ps[0:n, 0:128], pn), f1, identf)
                P.tr(V(ps[0:n, 128:256], pn), f2, identf)
                P.tr(V(ps[0:n, 256:384], pn), V(v32[:, t0:t0 + n], v32k), identf)
                P.copy(V(bd[0:n, co:co + 128], bdk), V(ps[0:n, 0:128], pn), eng='act')
                P.copy(V(kdt[0:n, co:co + 128], kdtk), V(ps[0:n, 128:256], pn), eng='act')
                P.copy(V(vt[0:n, co:co + 128], vtk), V(ps[0:n, 256:384], pn), eng='act')
            if hasattr(self, 'marks'):
                self.marks.append(('  b-groups', P.cnt['pe'], P.cnt['act'], P.cnt['dve']))
            oT, oTk = R[4], 'R4'
            H32 = V(self.HB[:, l, p, :], ('HB', l, p))
            Hb = V(self.HBb[:, p, :], ('HBb', p))
            P.copy(Hb, H32, eng='act')
            xc_t, xck_ = cb[6], 'cb6'
            def gen_GN(cs, bs, nsb):
                XbT, XbTk, LkT, LkTk, AqbT, AqbTk, AqkT, AqkTk = bs
                Nb, Nbk, NTb, NTbk, X32, X32k = nsb
                n = 128 if len(cs) == 2 else TS
                mi = 0 if len(cs) == 2 else 1
                G = 2 * len(cs)
                for (gi_type, (Lt, Ltk, Rt, Rtk, mask, mk, dst, dstk)) in enumerate([
                        (at, atk, bt_, btk, self.m_lstr, 'm_lstr', Nb, Nbk),
                        (kt, ktk, at, atk, self.m_ustr, 'm_ustr', LkT, LkTk),
                        (bt_, btk, qt, qtk, self.m_uincl, 'm_uincl', AqbT, AqbTk),
                        (kt, ktk, qt, qtk, self.m_uincl, 'm_uincl', AqkT, AqkTk)]):
                    for hp in range(2):
                        pn, ps = P.ps()
                        hs = slice(hp * 64, (hp + 1) * 64)
                        for ci, c in enumerate(cs):
                            t0 = chunk_info(c)[0]
                            P.mm(V(ps[0:n, ci * n:(ci + 1) * n], pn), V(Lt[hs, t0:t0 + n], Ltk), V(Rt[hs, t0:t0 + n], Rtk))
                        nci = len(cs)
                        dv = dst[0:n, 0:G * n].rearrange("p (ci hp n) -> p ci hp n", ci=nci, hp=2)[:, :, hp, :]
                        P.tt(V(dv, dstk), V(ps[0:n, 0:nci * n].rearrange("p (ci n) -> p ci n", ci=nci), pn),
                             V(mask[0:n, mi, 0:n].unsqueeze(1).to_broadcast([n, nci, n]), mk), ALU.mult)
                        yield
                pn, ps = P.ps()
                psb = ps[:, :].bitcast(BF16)
                for g in range(G):
                    sl = slice(g * n, (g + 1) * n)
                    P.tr(V(psb[0:n, sl], pn), V(Nb[0:n, sl], Nbk), V(self.identb[0:n, 0:n], 'identb'))
                P.copy(V(NTb[0:n, 0:G * n], NTbk), V(psb[0:n, 0:G * n], pn))
                yield
                yield from self.neumann(Nb, NTb, n, G, NEU_LEVELS_P if mi == 0 else NEU_LEVELS_S, X32, XbT, (Nbk, NTbk, X32k, XbTk))
            def gen_chain(cs, bs):
                XbT, XbTk, LkT, LkTk, AqbT, AqbTk, AqkT, AqkTk = bs
                for ci, c in enumerate(cs):
                    t0 = chunk_info(c)[0]
                    co = c * 128
                    if c < 8:
                        xcv = V(xc_t[:, 0:128], xck_)
                        for hp in range(2):
                            g = ci * 2 + hp
                            hs = slice(hp * 64, (hp + 1) * 64)
                            pn, ps = P.ps()
                            P.mm(V(ps[:, 0:64], pn), V(at[hs, t0:t0 + 128], atk), V(self.HBb[hs, p, :], ('HBb', p)),
                                 start=True, stop=False)
                            P.mm(V(ps[:, 0:64], pn), V(LkT[:, g * 128:(g + 1) * 128], LkTk), V(vt[:, co + hp * 64:co + (hp + 1) * 64], vtk),
                                 start=False, stop=True)
                            P.copy(V(xc_t[:, hp * 64:(hp + 1) * 64], xck_), V(ps[:, 0:64], pn), eng='act' if hp else 'dve')
                        yield
                        pn, ps = P.ps()
                        for hp in range(2):
                            g = ci * 2 + hp
                            P.mm(V(ps[:, hp * 64:(hp + 1) * 64], pn), V(XbT[:, g * 128:(g + 1) * 128], XbTk),
                                 V(xc_t[:, hp * 64:(hp + 1) * 64], xck_))
                        uv = V(xc_t[:, 128:256], xck_)
                        P.copy(uv, V(ps[:, 0:128], pn), eng='act')
                        yield
                        for hp in range(2):
                            g = ci * 2 + hp
                            hs = slice(hp * 64, (hp + 1) * 64)
                            pn, ps = P.ps()
                            po = V(ps[hs, 0:128], pn)
                            P.mm(po, V(self.HBb[hs, p, :], ('HBb', p)), V(qt[hs, t0:t0 + 128], qtk), start=True, stop=False)
                            P.mm(po, V(xc_t[:, 128 + hp * 64:128 + (hp + 1) * 64], xck_), V(AqbT[:, g * 128:(g + 1) * 128], AqbTk),
                                 start=False, stop=False)
                            P.mm(po, V(vt[:, co + hp * 64:co + (hp + 1) * 64], vtk), V(AqkT[:, g * 128:(g + 1) * 128], AqkTk),
                                 start=False, stop=True)
                            P.copy(V(oT[hs, t0:t0 + 128], oTk), po, eng='dve' if hp else 'act')
                        yield
                        pn, ps = P.ps()
                        for hp in range(2):
                            hs = slice(hp * 64, (hp + 1) * 64)
                            ph = V(ps[hs, 0:64], pn)
                            P.mm(ph, V(bd[:, co + hp * 64:co + (hp + 1) * 64], bdk), V(xc_t[:, 128 + hp * 64:128 + (hp + 1) * 64], xck_),
                                 start=True, stop=False)
                            P.mm(ph, V(kdt[:, co + hp * 64:co + (hp + 1) * 64], kdtk), V(vt[:, co + hp * 64:co + (hp + 1) * 64], vtk),
                                 start=False, stop=True)
                        P.stt(Hb, H32, V(self.glc[:, c:c + 1], 'glc'), V(ps[:, 0:64], pn), ALU.mult, ALU.add)
                        P.stt(H32, H32, V(self.glc[:, c:c + 1], 'glc'), V(ps[:, 0:64], pn), ALU.mult, ALU.add)
                        yield
                    else:
                        self.b_sample_chunk(p, l, blk, at, atk, qt, qtk, LkT, LkTk, AqbT, AqbTk, AqkT, AqkTk, XbT, XbTk,
                                            bd, bdk, kdt, kdtk, vt, vtk, oT, oTk, Hsb_t)
            r5b = R[5][:, 0:1024].bitcast(BF16)
            r7b = R[7][:, 0:1024].bitcast(BF16)
            r0b = R[0][:, 0:512].bitcast(BF16)
            bsets = [(cb[2], 'cb2', cb[3], 'cb3', cb[4], 'cb4', cb[5], 'cb5'),
                     (r5b[:, 0:512], ('R5', 0), r5b[:, 512:1024], ('R5', 1), r5b[:, 1024:1536], ('R5', 2),
                      r5b[:, 1536:2048], ('R5', 3)),
                     (r7b[:, 0:512], ('R7', 0), r7b[:, 512:1024], ('R7', 1), r7b[:, 1024:1536], ('R7', 2),
                      r7b[:, 1536:2048], ('R7', 3))]
            nsets = [(cb[0], 'cb0', cb[1], 'cb1', cf[0], 'cf0'),
                     (r0b[:, 0:512], ('R0', 0), r0b[:, 512:1024], ('R0', 1), R[0][:, 512:1024], ('R0', 2))]
            groups = [(0, 1), (2, 3), (4, 5), (6, 7), (8,)]
            pipe = Pipe()
            gens = {}

            def start(k):
                gens[k] = gen_GN(groups[k], bsets[k % 3], nsets[k % 2])
                pipe.add(gens[k])
            start(0)
            start(1)
            for gi_ in range(len(groups)):
                pipe.finish(gens[gi_])
                if gi_ + 2 < len(groups):
                    start(gi_ + 2)
                pipe.run_with(gen_chain(groups[gi_], bsets[gi_ % 3]))
            pipe.drain_all()
            if blk == NBLK - 1:
                pn, ps = P.ps()
                P.tr(V(ps[0:64, 0:128], pn), H32, identf)
                P.copy(V(cf[4][0:64, 0:128], 'cf4'), V(ps[0:64, 0:128], pn))
                P.dma(O['p_b_S'][l, 2 * p:2 * p + 2].rearrange("h v k -> v h k"),
                      V(cf[4][0:64, 0:128].rearrange("p (h k) -> p h k", h=2), 'cf4'), 'cf4')
            if hasattr(self, 'marks'):
                self.marks.append(('  b-post', P.cnt['pe'], P.cnt['act'], P.cnt['dve']))
            stS = R[5]
            for hf in range(2):
                pn, ps = P.ps()
                for sq_ in range(4):
                    s_ = hf * 4 + sq_
                    P.tr(V(ps[0:64, sq_ * 128:(sq_ + 1) * 128], pn), V(self.Ss[:, s_ * 64:(s_ + 1) * 64], 'Ss'), identf)
                P.copy(V(stS[0:64, hf * 512:(hf + 1) * 512], 'R5'), V(ps[0:64, 0:512], pn), eng='act' if hf else 'dve')
            for h_ in range(2):
                P.dma(O['s_b_S'][l, s0:s0 + NS, 2 * p + h_].rearrange("s v k -> v s k"),
                      V(stS[0:64, 0:1024].rearrange("p (s h k) -> p s h k", s=NS, h=2)[:, :, h_, :], 'R5'), 'R5')
            for ti, (t0, t1) in enumerate(TT):
                n = t1 - t0
                tA = V(self.small[:, 2, 0:n], ('small', 2))
                tB = V(self.small[:, 3, 0:n], ('small', 3))
                P.act(tA, V(oT[:, t0:t1], oTk), AF.Square)
                pn1, ps1 = P.ps()
                P.mm(V(ps1[:, 0:n], pn1), V(self.ones64f[:], 'ones64f'), V(oT[:, t0:t1], oTk))
                pn2, ps2 = P.ps()
                P.mm(V(ps2[:, 0:n], pn2), V(self.ones64f[:], 'ones64f'), tA)
                P.act(tA, V(ps1[:, 0:n], pn1), AF.Identity, scale=1.0 / 64)
                P.tt(tB, tA, tA, ALU.mult)
                P.stt(tB, V(ps2[:, 0:n], pn2), 1.0 / 64, tB, ALU.mult, ALU.subtract)
                P.act(tB, tB, AF.Ln, bias=self.lnepsc)
                P.act(tB, tB, AF.Exp, scale=-0.5)
                ov = V(oT[:, t0:t1], oTk)
                P.tt(ov, ov, tA, ALU.subtract)
                P.tt(ov, ov, tB, ALU.mult)
                P.ts(ov, ov, self.pcol('b_ln_w', l * 4 + p), self.pcol('b_ln_b', l * 4 + p), ALU.mult, ALU.add)
                P.tt(ov, ov, V(bon[:, t0:t1], bonk), ALU.add)
                P.tt(V(self.yb[:, 4 + p, t0:t1], ('yb', 4 + p)), ov, V(gb[:, t0:t1], gbk), ALU.mult)

    def b_sample_chunk(self, p, l, blk, at, atk, qt, qtk, LkT, LkTk, AqbT, AqbTk, AqkT, AqkTk, XbT, XbTk,
                       bd, bdk, kdt, kdtk, vt, vtk, oT, oTk, Hsb_t):
        P = self.P
        cf = self.c128f
        cb = self.c128b
        t0 = TP
        co = 8 * 128
        n = TS
        ex, exk = self.R[7], 'R7'
        seq2 = V(self.seqm[:, :].unsqueeze(2).to_broadcast([TS, NS, 64]), 'seqm')

        def state_apply(src_t, src_k, out_red):
            for hp in range(2):
                hs = slice(hp * 64, (hp + 1) * 64)
                pn, ps = P.ps()
                P.mm(V(ps[0:n, 0:512], pn), V(src_t[hs, t0:t0 + n], src_k), V(Hsb_t[hs, 0:512], 'cb7'))
                P.tt(V(ex[0:n, hp * 512:(hp + 1) * 512].rearrange("p (s e) -> p s e", s=NS), exk),
                     V(ps[0:n, 0:512].rearrange("p (s e) -> p s e", s=NS), pn), seq2, ALU.mult)
            P.red(out_red, V(ex[0:n, 0:1024].rearrange("p (h s e) -> p h e s", h=2, s=NS), exk))
        xc32 = V(cf[1][0:n, 0:128].rearrange("p (h e) -> p h e", h=2), 'cf1')
        state_apply(at, atk, xc32)
        pn, ps = P.ps()
        for hp in range(2):
            P.mm(V(ps[0:n, hp * 64:(hp + 1) * 64], pn), V(LkT[0:n, hp * n:(hp + 1) * n], LkTk),
                 V(vt[0:n, co + hp * 64:co + (hp + 1) * 64], vtk))
        xcb = V(cb[6][0:n, 0:128], 'cb6')
        P.tt(xcb, V(ps[0:n, 0:128], pn), V(cf[1][0:n, 0:128], 'cf1'), ALU.add)
        pn, ps = P.ps()
        for hp in range(2):
            P.mm(V(ps[0:n, hp * 64:(hp + 1) * 64], pn), V(XbT[0:n, hp * n:(hp + 1) * n], XbTk),
                 V(cb[6][0:n, hp * 64:(hp + 1) * 64], 'cb6'))
        u32 = V(cf[2][0:n, 0:128], 'cf2')
        P.copy(u32, V(ps[0:n, 0:128], pn), eng='act')
        ub = V(cb[6][0:n, 128:256], 'cb6')
        P.copy(ub, u32, eng='dve')
        o32 = V(cf[3][0:n, 0:128], 'cf3')
        state_apply(qt, qtk, V(cf[3][0:n, 0:128].rearrange("p (h e) -> p h e", h=2), 'cf3'))
        pn, ps = P.ps()
        for hp in range(2):
            po = V(ps[0:n, hp * 64:(hp + 1) * 64], pn)
            P.mm(po, V(AqbT[0:n, hp * n:(hp + 1) * n], AqbTk), V(cb[6][0:n, 128 + hp * 64:128 + (hp + 1) * 64], 'cb6'),
                 start=True, stop=False)
            P.mm(po, V(AqkT[0:n, hp * n:(hp + 1) * n], AqkTk), V(vt[0:n, co + hp * 64:co + (hp + 1) * 64], vtk),
                 start=False, stop=True)
        P.tt(o32, V(ps[0:n, 0:128], pn), o32, ALU.add)
        pn, ps = P.ps()
        P.tr(V(ps[:, 0:n], pn), o32, V(self.identf[0:n, 0:n], 'identf'))
        P.copy(V(oT[:, t0:t0 + n], oTk), V(ps[:, 0:n], pn), eng='act')
        uex = self.R[7][0:n, 0:512].bitcast(BF16)
        vex = self.R[7][0:n, 512:1024].bitcast(BF16)
        for hp in range(2):
            P.tt(V(uex[:, hp * 512:(hp + 1) * 512].rearrange("p (s e) -> p s e", s=NS), exk),
                 V(cf[2][0:n, hp * 64:(hp + 1) * 64].unsqueeze(1).to_broadcast([n, NS, 64]), 'cf2'), seq2, ALU.mult)
            P.tt(V(vex[:, hp * 512:(hp + 1) * 512].rearrange("p (s e) -> p s e", s=NS), exk),
                 V(vt[0:n, co + hp * 64:co + (hp + 1) * 64].unsqueeze(1).to_broadcast([n, NS, 64]), vtk), seq2, ALU.mult)
        pn, ps = P.ps()
        for hp in range(2):
            hs = slice(hp * 64, (hp + 1) * 64)
            ph = V(ps[hs, 0:512], pn)
            P.mm(ph, V(bd[0:n, co + hp * 64:co + (hp + 1) * 64], bdk), V(uex[:, hp * 512:(hp + 1) * 512], exk),
                 start=True, stop=False)
            P.mm(ph, V(kdt[0:n, co + hp * 64:co + (hp + 1) * 64], kdtk), V(vex[:, hp * 512:(hp + 1) * 512], exk),
                 start=False, stop=True)
        Hs3 = V(self.Ss[:, 0:512].rearrange("p (s e) -> p s e", s=NS), 'Ss')
        P.tt(Hs3, Hs3, V(self.glc[:, 8:16].unsqueeze(2).to_broadcast([128, NS, 64]), 'glc'), ALU.mult)
        P.tt(Hs3, Hs3, V(ps[:, 0:512].rearrange("p (s e) -> p s e", s=NS), pn), ALU.add)

    def merge_and_ffn(self, blk, l):
        P = self.P
        R = self.R
        w512 = lambda t: t[:, :].rearrange("p (kc n) -> p kc n", kc=8)
        w4 = lambda t: t[:, 0:2048].rearrange("p (kc n) -> p kc n", kc=4)
        for jg in range(2):
            for b in range(3):
                gt, gk = self.wnext(('gate', blk, l, jg, b))
                bt, bk = self.wnext(('wbr', blk, l, jg, b))
                for jj in range(4):
                    j = jg * 4 + jj
                    acc, acck = R[jj], 'R%d' % jj
                    for ti, (t0, t1) in enumerate(TT):
                        n = t1 - t0
                        png, psg = P.ps()
                        for kc in range(KC):
                            P.mm(V(psg[:, 0:n], png), V(w512(gt)[:, kc, jj * 128:(jj + 1) * 128], gk),
                                 V(self.hT[:, kc, t0:t1], ('hT', ti)), start=(kc == 0), stop=(kc == KC - 1))
                        sg = V(self.small[:, 2 + (ti % 2), 0:n], ('small', 2 + (ti % 2)))
                        P.act(sg, V(psg[:, 0:n], png), AF.Sigmoid)
                        pnp, psp = P.ps()
                        for kc in range(4):
                            P.mm(V(psp[:, 0:n], pnp), V(w4(bt)[:, kc, jj * 128:(jj + 1) * 128], bk),
                                 V(self.yb[:, b * 4 + kc, t0:t1], ('yb', b * 4 + kc)), start=(kc == 0), stop=(kc == 3))
                        av = V(acc[:, t0:t1], (acck, ti))
                        if b == 0:
                            P.tt(av, V(psp[:, 0:n], pnp), sg, ALU.mult)
                        else:
                            P.tt(sg, V(psp[:, 0:n], pnp), sg, ALU.mult)
                            if b == 1:
                                P.tt(av, av, sg, ALU.add)
                            else:
                                P.tt(self.mch(j, t0, t1), av, sg, ALU.add)
        for jh in range(2):
            wt, wk = self.wnext(('wout', blk, l, jh))
            for jj in range(4):
                j = jh * 4 + jj
                for ti, (t0, t1) in enumerate(TT):
                    n = t1 - t0
                    pn, ps = P.ps()
                    for kc in range(KC):
                        P.mm(V(ps[:, 0:n], pn), V(w512(wt)[:, kc, jj * 128:(jj + 1) * 128], wk),
                             self.mch(kc, t0, t1), start=(kc == 0), stop=(kc == KC - 1))
                    xv = V(self.xT[:, j, t0:t1], ('xT', ti))
                    P.tt(xv, V(ps[:, 0:n], pn), xv, ALU.add)
        if hasattr(self, 'marks'):
            self.marks.append(('ffn b%d l%d' % (blk, l), P.cnt['pe'], P.cnt['act'], P.cnt['dve']))
        self.rmsnorm_to_h('norm2_g', l)
        for q in range(4):
            for g in range(2):
                wt, wk = self.wnext(('wup', blk, l, q, g))
                for jj in range(4):
                    uc = g * 4 + jj
                    for ti, (t0, t1) in enumerate(TT):
                        n = t1 - t0
                        pn, ps = P.ps()
                        for kc in range(KC):
                            P.mm(V(ps[:, 0:n], pn), V(w512(wt)[:, kc, jj * 128:(jj + 1) * 128], wk),
                                 V(self.hT[:, kc, t0:t1], ('hT', ti)), start=(kc == 0), stop=(kc == KC - 1))
                        rl = V(self.small[:, 2 + (ti % 2), 0:n], ('small', 2 + (ti % 2)))
                        P.act(rl, V(ps[:, 0:n], pn), AF.Relu)
                        P.tt(self.mch(uc, t0, t1), rl, rl, ALU.mult)
            for jh in range(2):
                wt, wk = self.wnext(('wdn', blk, l, q, jh))
                for jj in range(4):
                    j = jh * 4 + jj
                    for ti, (t0, t1) in enumerate(TT):
                        n = t1 - t0
                        pn, ps = P.ps()
                        for kc in range(KC):
                            P.mm(V(ps[:, 0:n], pn), V(w512(wt)[:, kc, jj * 128:(jj + 1) * 128], wk),
                                 self.mch(kc, t0, t1), start=(kc == 0), stop=(kc == KC - 1))
                        xv = V(self.xT[:, j, t0:t1], ('xT', ti))
                        P.tt(xv, V(ps[:, 0:n], pn), xv, ALU.add)

    def final_norm_store(self, blk):
        P = self.P
        for ti, (t0, t1) in enumerate(TT):
            n = t1 - t0
            pn, ps = P.ps()
            sq, sqk = self.Bt[ti % 2], 'B%d' % (ti % 2)
            for g3, (c0, c1) in enumerate([(0, 3), (3, 6), (6, 8)]):
                P.act(V(sq[:, 0:(c1 - c0) * n].rearrange("p (c n) -> p c n", c=c1 - c0), sqk),
                      V(self.xT[:, c0:c1, t0:t1], ('xT', ti)), AF.Square)
                for c in range(c0, c1):
                    P.mm(V(ps[:, 0:n], pn), V(self.onesb[:], 'onesb'), V(sq[:, (c - c0) * n:(c - c0 + 1) * n], sqk),
                         start=(c == 0), stop=(c == 7))
            rs = V(self.small[:, ti % 2, 0:n], ('small', ti % 2))
            P.act(rs, V(ps[:, 0:n], pn), AF.Ln, bias=self.epsc, scale=1.0 / D)
            P.act(rs, rs, AF.Exp, scale=-0.5)
            for c in range(KC):
                xv = V(self.xT[:, c, t0:t1], ('xT', ti))
                P.stt(xv, xv, self.pcol('final_norm_g', c), rs, ALU.mult, ALU.mult)
        identf = V(self.identf[:], 'identf')
        for i in range(8):
            stg = self.R[i % 2]
            sk = 'R%d' % (i % 2)
            for half in range(2):
                pn, ps = P.ps()
                for c in range(4):
                    cc = half * 4 + c
                    P.tr(V(ps[:, c * 128:(c + 1) * 128], pn), V(self.xT[:, cc, i * 128:(i + 1) * 128], 'xT'), identf)
                P.copy(V(stg[:, half * 512:(half + 1) * 512], sk), V(ps[:, :], pn), eng='act' if half else 'dve')
            P.dma(self.O['y_p'][blk * TP + i * 128:blk * TP + (i + 1) * 128, :], V(stg[:, 0:1024], sk), sk)
        stg = self.R[0]
        for half in range(2):
            pn, ps = P.ps()
            for c in range(4):
                cc = half * 4 + c
                P.tr(V(ps[0:TS, c * 128:(c + 1) * 128], pn), V(self.xT[:, cc, TP:T], 'xT'), identf)
            P.copy(V(stg[0:TS, half * 512:(half + 1) * 512], 'R0'), V(ps[0:TS, :], pn))
        P.dma(self.O['y_s'][blk * TS:(blk + 1) * TS, :], V(stg[0:TS, 0:1024], 'R0'), 'R0')

    def build(self):
        with ExitStack() as es:
            self.P = Prog(self.nc, es)
            P = self.P
            P.init_psum(reserve=1 if K_WARM else 0)
            self.alloc()
            consts = P.sb('consts', [128, 4], F32)
            P.memset(V(consts[:, 0:1], 'consts'), EPS)
            P.memset(V(consts[:, 1:2], 'consts'), 1.0)
            P.memset(V(consts[:, 2:3], 'consts'), B_LN_EPS)
            self.epsc = V(consts[:, 0:1], 'consts')
            self.onec = V(consts[:, 1:2], 'consts')
            self.lnepsc = V(consts[:, 2:3], 'consts')
            self.setup_consts()
            self.sched = self.weight_schedule()
            self.w_i = 0
            self.w_issued = 0
            def warm_fn():
                pnw, psw = P.psum_extra[0]
                for _ in range(K_WARM):
                    P.mm(V(psw[:, 0:512], pnw), V(self.identb[:], 'identb'), V(self.rmask[:, 0:512], 'rmask'))
            self.marks = []
            mk = lambda lab: self.marks.append((lab, P.cnt['pe'], P.cnt['act'], P.cnt['dve']))
            for blk in range(NBLK):
                mk('load%d' % blk)
                self.load_x_block(blk)
                for l in range(K_LAYERS):
                    mk('norm1 b%d l%d' % (blk, l))
                    self.rmsnorm_to_h('norm1_g', l)
                    if not (EN_A and EN_B and EN_C):
                        P.memset(V(self.yb[:], 'yb'), 0.0, eng='dve')
                    mk('A b%d l%d' % (blk, l))
                    if K_WARM:
                        P.warm_fn = warm_fn
                    if EN_A:
                        try:
                            self.branch_A(blk, l)
                        except _Stop:
                            pass
                    mk('B b%d l%d' % (blk, l))
                    if EN_B:
                        self.branch_B(blk, l)
                    mk('C b%d l%d' % (blk, l))
                    if EN_C:
                        self.branch_C(blk, l)
                    P.warm_fn = None
                    mk('merge b%d l%d' % (blk, l))
                    if K_MERGE:
                        self.merge_and_ffn(blk, l)
                mk('final%d' % blk)
                self.final_norm_store(blk)
            mk('end')
            assert K_ASTOP or self.w_i == len(self.sched)
            P.final_wait_all()
            P.build()
        return self.nc


_CACHE = {}


def kernel(**inputs):
    inp = {k: np.ascontiguousarray(np.asarray(v, dtype=np.float32)) for k, v in inputs.items()}
    if 'nc' not in _CACHE:
        _CACHE['nc'] = Builder().build()
    nc = _CACHE['nc']
    wnames = [k for k in INPUT_SHAPES if k not in ('xp', 'xs', 'sa_S', 'sa_conv', 'sb_S', 'sb_shift', 'sc_h', 'sc_conv')]
    in_maps = []
    for c in range(NCORES):
        s = slice(c * NSEQ_CORE, (c + 1) * NSEQ_CORE)
        m = {
            'xp': inp['x_prompt'][c],
            'xs': inp['x_sample'][s].reshape(NSEQ_CORE * 4, D),
            'sa_S': inp['state_a_S'][:, s], 'sa_conv': inp['state_a_conv'][:, s],
            'sb_S': inp['state_b_S'][:, s], 'sb_shift': inp['state_b_shift'][:, s],
            'sc_h': inp['state_c_h'][:, s], 'sc_conv': inp['state_c_conv'][:, s],
        }
        for k in wnames:
            m[k] = inp[k]
        in_maps.append({k: np.ascontiguousarray(v) for k, v in m.items()})
    res = run_bass_kernel_spmd(nc, in_maps, core_ids=list(range(NCORES)))
    rs = res.results
    y_prompt = np.stack([rs[c]['y_p'] for c in range(NCORES)], axis=0)
    y_sample = np.concatenate([rs[c]['y_s'].reshape(NSEQ_CORE, 4, D) for c in range(NCORES)], axis=0)
    outs = [y_prompt, y_sample]
    for nm in ['p_a_S', 'p_a_conv', 'p_b_S', 'p_b_shift', 'p_c_h', 'p_c_conv']:
        outs.append(np.stack([rs[c][nm] for c in range(NCORES)], axis=1))
    for nm in ['s_a_S', 's_a_conv', 's_b_S', 's_b_shift', 's_c_h', 's_c_conv']:
        outs.append(np.concatenate([rs[c][nm] for c in range(NCORES)], axis=1))
    return tuple(np.ascontiguousarray(o.astype(np.float32)) for o in outs)
```
